# Optimizing a Trainium2 kernel written in Bass

```python
import math
import jax, jax.numpy as jnp
from jax import lax
import numpy as np

D_MODEL = 2048
BATCH = 16
SEQ = 2048
DEPTH = 4

CTX_LEN = 256
GRID_W = 64
QBLK = 128
ROPE_BASE = 10000.0
EPS = 1e-6

DIFF_HEADS = 8
DIFF_HD = 64
DIFF_VD = 2 * DIFF_HD
DIFF_W = DIFF_HEADS * DIFF_VD
DIFF_SCALE = DIFF_HD ** -0.5

MLA_HEADS = 8
MLA_Q_RANK = 512
MLA_KV_RANK = 256
MLA_NOPE = 128
MLA_ROPE = 64
MLA_VD = 128
MLA_W = MLA_HEADS * MLA_VD
MLA_SCALE = (MLA_NOPE + MLA_ROPE) ** -0.5

ROPE_DIM = 64
AXIS_DIM = ROPE_DIM // 2

D_FF = 4 * D_MODEL
N_BRANCH = 2
N_MOD = 6

IN_SPLITS = (
    DIFF_HEADS * 2 * DIFF_HD,
    DIFF_HEADS * 2 * DIFF_HD,
    DIFF_HEADS * DIFF_VD,
    MLA_Q_RANK,
    MLA_KV_RANK,
    MLA_ROPE,
    N_BRANCH * D_MODEL,
)
IN_W = sum(IN_SPLITS)

kernel_name = "hybrid_diffattn_mla_dit_trunk"


def rmsnorm(x, g):
    xf = x.astype(jnp.float32)
    y = xf * lax.rsqrt(jnp.mean(xf * xf, axis=-1, keepdims=True) + EPS)
    return (y * g.astype(jnp.float32)).astype(x.dtype)


def modulate(h, shift, scale):
    return h * (1 + scale) + shift


def axial_tables(n_tokens):
    rows = n_tokens // GRID_W
    row, col = jnp.meshgrid(jnp.arange(rows), jnp.arange(GRID_W), indexing="ij")
    row = row.reshape(-1).astype(jnp.float32)
    col = col.reshape(-1).astype(jnp.float32)
    freqs = ROPE_BASE ** (-jnp.arange(0, AXIS_DIM, 2, dtype=jnp.float32) / AXIS_DIM)
    ang_r = row[:, None] * freqs
    ang_c = col[:, None] * freqs
    ang = jnp.concatenate([ang_r, ang_r, ang_c, ang_c], axis=-1)
    return jnp.cos(ang), jnp.sin(ang)


def rotate_half_axial(x):
    h = AXIS_DIM // 2
    return jnp.concatenate([-x[..., h:AXIS_DIM], x[..., :h],
                            -x[..., AXIS_DIM + h:], x[..., AXIS_DIM:AXIS_DIM + h]], axis=-1)


def apply_rope(x, rope):
    cos, sin = rope
    xf = x.astype(jnp.float32)
    out = xf * cos[None, :, None, :] + rotate_half_axial(xf) * sin[None, :, None, :]
    return out.astype(x.dtype)


def sweep_query_blocks(fn, qs):
    b, t = qs[0].shape[:2]
    nb = t // QBLK
    blocks = tuple(jnp.moveaxis(q.reshape(b, nb, QBLK, *q.shape[2:]), 1, 0) for q in qs)
    out = lax.map(lambda blk: fn(*blk), blocks)
    out = jnp.moveaxis(out, 0, 1)
    return out.reshape(b, t, *out.shape[3:])


def diff_block(q1, q2, k1, k2, v, lam):
    s1 = jnp.einsum("bqhd,bkhd->bhqk", q1, k1).astype(jnp.float32) * DIFF_SCALE
    s2 = jnp.einsum("bqhd,bkhd->bhqk", q2, k2).astype(jnp.float32) * DIFF_SCALE
    a = jax.nn.softmax(s1, axis=-1) - lam * jax.nn.softmax(s2, axis=-1)
    return jnp.einsum("bhqk,bkhd->bqhd", a.astype(v.dtype), v)


def mla_block(q, k, v):
    s = jnp.einsum("bqhd,bkhd->bhqk", q, k).astype(jnp.float32) * MLA_SCALE
    p = jax.nn.softmax(s, axis=-1)
    return jnp.einsum("bhqk,bkhd->bqhd", p.astype(v.dtype), v)


def mixer_projections(h, w_in, b_gate, q_a_norm, w_uq, kv_a_norm, w_ukv, rope):
    b, t, _ = h.shape
    offsets = [int(o) for o in np.cumsum(IN_SPLITS)[:-1]]
    qd, kd, vd, cq, ckv, kr, gpre = jnp.split(h @ w_in, offsets, axis=-1)
    qd = qd.reshape(b, t, DIFF_HEADS, 2, DIFF_HD)
    kd = kd.reshape(b, t, DIFF_HEADS, 2, DIFF_HD)
    q1, q2 = qd[..., 0, :], qd[..., 1, :]
    k1, k2 = kd[..., 0, :], kd[..., 1, :]
    v_d = vd.reshape(b, t, DIFF_HEADS, DIFF_VD)
    qm = (rmsnorm(cq, q_a_norm) @ w_uq).reshape(b, t, MLA_HEADS, MLA_NOPE + MLA_ROPE)
    q_nope, q_pe = qm[..., :MLA_NOPE], qm[..., MLA_NOPE:]
    kv = (rmsnorm(ckv, kv_a_norm) @ w_ukv).reshape(b, t, MLA_HEADS, MLA_NOPE + MLA_VD)
    k_nope, v_m = kv[..., :MLA_NOPE], kv[..., MLA_NOPE:]
    k_pe = kr.reshape(b, t, 1, MLA_ROPE)
    if rope is not None:
        q1, q2, k1, k2 = (apply_rope(a, rope) for a in (q1, q2, k1, k2))
        q_pe = apply_rope(q_pe, rope)
        k_pe = apply_rope(k_pe, rope)
    q_m = jnp.concatenate([q_nope, q_pe], axis=-1)
    k_m = jnp.concatenate([k_nope, jnp.broadcast_to(k_pe, (b, t, MLA_HEADS, MLA_ROPE))], axis=-1)
    gates = jax.nn.sigmoid((gpre + b_gate).astype(jnp.float32)).astype(h.dtype)
    return (q1, q2, q_m), (k1, k2, v_d, k_m, v_m), gates


def mix(queries, keys, gates, lam, lam_init, diff_subln, w_o_diff, w_o_mla, w_out):
    q1, q2, q_m = queries
    k1, k2, v_d, k_m, v_m = keys
    b, t = q1.shape[:2]
    y_d = sweep_query_blocks(lambda a, c: diff_block(a, c, k1, k2, v_d, lam), (q1, q2))
    y_d = rmsnorm(y_d, diff_subln) * (1.0 - lam_init)
    y_m = sweep_query_blocks(lambda a: mla_block(a, k_m, v_m), (q_m,))
    y_d = y_d.reshape(b, t, DIFF_W) @ w_o_diff
    y_m = y_m.reshape(b, t, MLA_W) @ w_o_mla
    merged = gates[..., :D_MODEL] * y_d + gates[..., D_MODEL:] * y_m
    return merged @ w_out


def sq_relu_mlp(h, w1, w2):
    u = jax.nn.relu(h @ w1)
    return (u * u) @ w2


def setup_inputs(seed: int = 0) -> dict:
    key = jax.random.key(seed)
    ks = jax.random.split(key, 24)

    def nrm(k, shape, scale):
        return jax.random.normal(k, shape, jnp.float32) * scale

    def gain(k, shape):
        return 1.0 + 0.02 * jax.random.normal(k, shape, jnp.float32)

    L, D = DEPTH, D_MODEL
    return {
        "x": nrm(ks[0], (BATCH, SEQ, D), 1.0),
        "c": nrm(ks[1], (BATCH, D), 1.0),
        "ctx": nrm(ks[2], (BATCH, CTX_LEN, D), 1.0),
        "c_ctx": nrm(ks[3], (D,), 0.5),
        "w_ada": nrm(ks[4], (L, D, N_MOD * D), 0.5 * D ** -0.5),
        "b_ada": nrm(ks[5], (L, N_MOD * D), 0.01),
        "norm1_g": gain(ks[6], (L, D)),
        "norm2_g": gain(ks[7], (L, D)),
        "w_in": nrm(ks[8], (L, D, IN_W), D ** -0.5),
        "b_gate": nrm(ks[9], (L, N_BRANCH * D), 0.01),
        "q_a_norm": gain(ks[10], (L, MLA_Q_RANK)),
        "w_uq": nrm(ks[11], (L, MLA_Q_RANK, MLA_HEADS * (MLA_NOPE + MLA_ROPE)), MLA_Q_RANK ** -0.5),
        "kv_a_norm": gain(ks[12], (L, MLA_KV_RANK)),
        "w_ukv": nrm(ks[13], (L, MLA_KV_RANK, MLA_HEADS * (MLA_NOPE + MLA_VD)), MLA_KV_RANK ** -0.5),
        "diff_lambda": nrm(ks[14], (L, 4, DIFF_HD), 0.1),
        "diff_subln": gain(ks[15], (L, DIFF_VD)),
        "w_o_diff": nrm(ks[16], (L, DIFF_W, D), DIFF_W ** -0.5),
        "w_o_mla": nrm(ks[17], (L, MLA_W, D), MLA_W ** -0.5),
        "w_out": nrm(ks[18], (L, D, D), D ** -0.5),
        "w_mlp1": nrm(ks[19], (L, D, D_FF), D ** -0.5),
        "w_mlp2": nrm(ks[20], (L, D_FF, D), D_FF ** -0.5),
        "final_norm_g": gain(ks[21], (D,)),
    }


def reference(x, c, ctx, c_ctx, w_ada, b_ada, norm1_g, norm2_g, w_in, b_gate, q_a_norm, w_uq,
              kv_a_norm, w_ukv, diff_lambda, diff_subln, w_o_diff, w_o_mla, w_out, w_mlp1,
              w_mlp2, final_norm_g):
    seq = x.shape[1]
    rope = axial_tables(seq)
    xc = ctx
    silu_c = jax.nn.silu(c)
    silu_cc = jax.nn.silu(c_ctx)
    for l in range(DEPTH):
        mod_x = (silu_c @ w_ada[l] + b_ada[l])[:, None, :]
        mod_c = silu_cc @ w_ada[l] + b_ada[l]
        sh1, sc1, g1, sh2, sc2, g2 = jnp.split(mod_x, N_MOD, axis=-1)
        csh1, csc1, cg1, csh2, csc2, cg2 = jnp.split(mod_c, N_MOD, axis=-1)
        lam_init = 0.8 - 0.6 * math.exp(-0.3 * l)
        dl = diff_lambda[l].astype(jnp.float32)
        lam = jnp.exp(jnp.sum(dl[0] * dl[1])) - jnp.exp(jnp.sum(dl[2] * dl[3])) + lam_init
        proj_w = (w_in[l], b_gate[l], q_a_norm[l], w_uq[l], kv_a_norm[l], w_ukv[l])
        mix_w = (lam, lam_init, diff_subln[l], w_o_diff[l], w_o_mla[l], w_out[l])

        hx = modulate(rmsnorm(x, norm1_g[l]), sh1, sc1)
        hc = modulate(rmsnorm(xc, norm1_g[l]), csh1, csc1)
        qx, kx, gx = mixer_projections(hx, *proj_w, rope)
        qc, kc, gc = mixer_projections(hc, *proj_w, None)
        keys_lat = tuple(jnp.concatenate([a, b], axis=1) for a, b in zip(kc, kx))
        x = x + g1 * mix(qx, keys_lat, gx, *mix_w)
        x = x + g2 * sq_relu_mlp(modulate(rmsnorm(x, norm2_g[l]), sh2, sc2), w_mlp1[l], w_mlp2[l])
        if l < DEPTH - 1:
            xc = xc + cg1 * mix(qc, kc, gc, *mix_w)
            xc = xc + cg2 * sq_relu_mlp(modulate(rmsnorm(xc, norm2_g[l]), csh2, csc2),
                                        w_mlp1[l], w_mlp2[l])
    return rmsnorm(x, final_norm_g)
```

```python
import math
from contextlib import ExitStack

import numpy as np
import concourse.bass as bass
import concourse.mybir as mybir
from concourse.bass_utils import run_bass_kernel_spmd

F32 = mybir.dt.float32
BF16 = mybir.dt.bfloat16
AF = mybir.ActivationFunctionType
ALU = mybir.AluOpType
AX = mybir.AxisListType

NCORES = 8
D = 2048
DEPTH = 4
SEQ = 2048
CTX = 256
NB = 2
NTOK = NB * (SEQ + CTX)
LAT0 = NB * CTX
EPS = 1e-6
DIFF_SCALE = 64 ** -0.5
MLA_SCALE = 192 ** -0.5
DFF = 8192
NH = 8
WB = 256

TILES = [(0, 512, 2, True)] + [(LAT0 + i * 1024, 1024, i // 2, False) for i in range(4)]


class Buf:
    __slots__ = ("name", "w", "r")

    def __init__(self, sched, name):
        self.name = name
        self.w = {}
        self.r = {}
        sched.bufs.append(self)


class Sched:
    def __init__(self, nc, es, n_sp=44, n_pool=44):
        self.nc = nc
        self.engs = {"pe": nc.tensor, "act": nc.scalar, "dve": nc.vector, "pool": nc.gpsimd,
                     "sp": nc.sync}
        self.sem = {}
        self.cnt = {}
        for n in ("pe", "act", "dve", "pool"):
            self.sem[n] = es.enter_context(nc.semaphore("s_" + n))
            self.cnt[n] = 0
        self.dsem = []
        self.dcnt = []
        self.qsems = {"sp": [], "pool": []}
        self.qnext = {"sp": 0, "pool": 0}
        for q, n in (("sp", n_sp), ("pool", n_pool)):
            for i in range(n):
                idx = len(self.dsem)
                self.dsem.append(es.enter_context(nc.semaphore(f"d_{q}{i}")))
                self.dcnt.append(0)
                self.qsems[q].append(idx)
        self.seen = {e: {} for e in self.engs}
        self.bufs = []
        self.ninstr = 0

    def buf(self, name):
        return Buf(self, name)

    def _wait(self, eng, clock, val):
        if val <= 0:
            return
        seen = self.seen[eng]
        if seen.get(clock, 0) >= val:
            return
        sem = self.sem[clock] if isinstance(clock, str) else self.dsem[clock]
        self.engs[eng].wait_ge(sem, val)
        seen[clock] = val
        self.ninstr += 1

    def _deps(self, eng, reads, writes, is_dma):
        need = {}
        for b in reads:
            for k, v in b.w.items():
                if need.get(k, 0) < v:
                    need[k] = v
        for b in writes:
            for k, v in b.w.items():
                if need.get(k, 0) < v:
                    need[k] = v
            for k, v in b.r.items():
                if need.get(k, 0) < v:
                    need[k] = v
        for k, v in need.items():
            if k == "pe" and eng == "pe":
                continue
            self._wait(eng, k, v)

    def op(self, eng, reads, writes, fn):
        self._deps(eng, reads, writes, False)
        ins = fn(self.engs[eng])
        self.cnt[eng] += 1
        v = self.cnt[eng]
        ins.then_inc(self.sem[eng], 1)
        for b in reads:
            b.r[eng] = v
        for b in writes:
            b.w = {eng: v}
            b.r = {}
        self.ninstr += 1
        return ins

    def dma(self, q, out_ap, in_ap, reads=(), writes=(), join=False):
        sl = self.qsems[q]
        i = sl[self.qnext[q]]
        self.qnext[q] = (self.qnext[q] + 1) % len(sl)
        self._wait(q, i, 16 * self.dcnt[i])
        if join:
            self._deps(q, reads, [], True)
        else:
            self._deps(q, reads, writes, True)
        self.dcnt[i] += 1
        v = 16 * self.dcnt[i]
        self.engs[q].dma_start(out=out_ap, in_=in_ap).then_inc(self.dsem[i], 16)
        for b in reads:
            b.r[i] = v
        for b in writes:
            if join:
                b.w[i] = v
            else:
                b.w = {i: v}
                b.r = {}
        self.ninstr += 1

    def barrier(self):
        for eng in self.engs:
            for clock in ("pe", "act", "dve", "pool"):
                if clock != eng:
                    self._wait(eng, clock, self.cnt[clock])
            for i in range(len(self.dsem)):
                self._wait(eng, i, 16 * self.dcnt[i])
        for b in self.bufs:
            b.w = {}
            b.r = {}


class Stream:
    def __init__(self, name, src, k0, KC, chunks):
        self.name = name
        self.src = src
        self.k0 = k0
        self.KC = KC
        chunks = list(chunks)
        if len(chunks) % 2:
            chunks.append(chunks[-1])
        self.chunks = chunks
        self.nblk = len(chunks) // 2
        self.dram = None


def contiguous_chunks(c0, n):
    return [[(c0 + i * 128, 128)] for i in range(n)]


def make_streams():
    S = {}
    ch = []
    ch += contiguous_chunks(3072, 4)
    ch += contiguous_chunks(3584, 2)
    ch += contiguous_chunks(0, 8)
    ch += contiguous_chunks(1024, 8)
    ch += [[(3840, 64), (3840, 64)]]
    ch += contiguous_chunks(3904, 32)
    S["win_f"] = Stream("win_f", "w_in", 0, 16, ch)
    S["win_t"] = Stream("win_t", "w_in", 0, 16, contiguous_chunks(2048, 8))
    ch = [[(h * 192, 128)] for h in range(8)]
    ch += [[((2 * i) * 192 + 128, 64), ((2 * i + 1) * 192 + 128, 64)] for i in range(4)]
    S["wuq_f"] = Stream("wuq_f", "w_uq", 0, 4, ch)
    S["wukv_f"] = Stream("wukv_f", "w_ukv", 0, 2, [[(h * 256, 128)] for h in range(8)])
    S["wukv_t"] = Stream("wukv_t", "w_ukv", 0, 2, [[(h * 256 + 128, 128)] for h in range(8)])
    S["wod_f"] = Stream("wod_f", "w_o_diff", 0, 8, contiguous_chunks(0, 16))
    S["wom_f"] = Stream("wom_f", "w_o_mla", 0, 8, contiguous_chunks(0, 16))
    S["wout_f"] = Stream("wout_f", "w_out", 0, 16, contiguous_chunks(0, 16))
    S["w1_f"] = Stream("w1_f", "w_mlp1", 0, 16, contiguous_chunks(0, 64))
    for q in range(4):
        S[f"w2_f{q}"] = Stream(f"w2_f{q}", "w_mlp2", q * 2048, 16, contiguous_chunks(0, 16))
    return S


WSHAPES = {
    "w_in": (2048, 8000), "w_uq": (512, 1536), "w_ukv": (256, 2048), "w_o_diff": (1024, 2048),
    "w_o_mla": (1024, 2048), "w_out": (2048, 2048), "w_mlp1": (2048, 8192), "w_mlp2": (8192, 2048),
}

V_N1, V_N2, V_BG, V_QA, V_KVA, V_SUB, V_BADA = 0, 16, 32, 64, 68, 70, 71
V_PER_LAYER = 71 + 96
V_FINAL = DEPTH * V_PER_LAYER
NVEC = V_FINAL + 16


class Builder:
    def __init__(self, n_layers=DEPTH, debug=None):
        self.n_layers = n_layers
        self.debug = debug or {}
        self.nc = nc = bass.Bass("TRN2", target_bir_lowering=False)
        self.streams = make_streams()
        dt = nc.dram_tensor
        self.xin = dt("xin", [D, NTOK], F32, kind="ExternalInput").ap()
        self.cvec = dt("cvec", [128, 16 * 3], F32, kind="ExternalInput").ap()
        self.vecs = dt("vecs", [128, NVEC], F32, kind="ExternalInput").ap()
        self.dlam = dt("dlam", [128, DEPTH * 256], F32, kind="ExternalInput").ap()
        self.cossin = dt("cossin", [128, 2 * SEQ], F32, kind="ExternalInput").ap()
        self.rmat = dt("rmat", [128, 128], F32, kind="ExternalInput").ap()
        self.w_ada = dt("w_ada", [DEPTH, D, 6 * D], F32, kind="ExternalInput").ap()
        self.wsrc = {}
        for n, (k, m) in WSHAPES.items():
            self.wsrc[n] = dt(n, [DEPTH, k, m], F32, kind="ExternalInput").ap()
        self.yout = dt("yout", [D, NB * SEQ], F32, kind="ExternalOutput").ap()
        def scr(name, shape, dtype=BF16):
            kind = "ExternalOutput" if name in self.debug.get("dump", ()) else "Internal"
            return dt(name, shape, dtype, kind=kind).ap()
        self.XS = scr("XS", [D, NTOK], F32)
        self.QD = scr("QD", [NH, 128, NTOK])
        self.KD = scr("KD", [NH, 128, NTOK])
        self.VD = scr("VD", [NTOK, 1024])
        self.QMN = scr("QMN", [NH, 128, NTOK])
        self.QMP = scr("QMP", [NH // 2, 128, NTOK])
        self.KMN = scr("KMN", [NH, 128, NTOK])
        self.KMP = scr("KMP", [128, NTOK])
        self.VM = scr("VM", [NTOK, 1024])
        self.G = scr("G", [32, 128, NTOK])
        self.YD = scr("YD", [NH, 128, NTOK])
        self.YM = scr("YM", [NH, 128, NTOK])
        self.MODD = scr("MODD", [128, 3 * 96], F32)
        for s in self.streams.values():
            s.dram = [scr(f"{s.name}_{l}", [s.nblk, 128, s.KC * WB]) for l in range(n_layers)]

    def build(self):
        nc = self.nc
        with ExitStack() as es:
            self.S = S = Sched(nc, es)
            self.ps = []
            self.psb = []
            for i in range(8):
                self.ps.append(es.enter_context(nc.psum_tensor(f"ps{i}", [128, 512], F32)))
                self.psb.append(S.buf(f"ps{i}"))
            self.consts(es)
            stop = self.debug.get("stop")
            for l in range(self.n_layers):
                self.l = l
                last = (l == DEPTH - 1)
                S.barrier()
                if l == 0:
                    with ExitStack() as sc:
                        self.phase_conv(sc, l, "A")
                        S.barrier()
                if stop == ("conv", l):
                    break
                with ExitStack() as sc:
                    self.phase_ada(sc, l)
                    S.barrier()
                if stop == ("ada", l):
                    break
                with ExitStack() as sc:
                    self.alloc_A(sc)
                    for t in self.debug.get("A_tile_list", range(self.debug.get("A_tiles", len(TILES)))):
                        self.phase_A(l, t)
                    S.barrier()
                if stop == ("A", l):
                    break
                with ExitStack() as sc:
                    self.alloc_B(sc)
                    staging = self.conv_staging(sc)
                    gens = [self.conv_gen(staging, l, ["dve", "pool"], "C")]
                    if l + 1 < self.n_layers:
                        gens.append(self.conv_gen(staging, l + 1, ["dve", "pool"], "A"))
                    self.bg = self.chain(gens)
                    for b in range(NB):
                        self.phase_B(l, b, last)
                    self.bg_step(100000)
                    S.barrier()
                if stop == ("B", l):
                    break
                with ExitStack() as sc:
                    self.alloc_C(sc)
                    for t in range(len(TILES)):
                        if last and TILES[t][3]:
                            continue
                        self.phase_C(l, t, last)
                    S.barrier()
            S.barrier()
        return nc

    def T(self, es, name, shape, dtype):
        self.uid = getattr(self, "uid", 0) + 1
        name = f"{name}_u{self.uid}"
        t = es.enter_context(self.nc.sbuf_tensor(name, shape, dtype))
        return t, self.S.buf(name)

    def consts(self, es):
        S = self.S
        self.ones, self.ones_b = self.T(es, "ones", [128, 128], BF16)
        self.rT, self.rT_b = self.T(es, "rT", [128, 128], BF16)
        self.vec, self.vec_b = self.T(es, "vec", [128, NVEC], F32)
        self.silc, self.silc_b = self.T(es, "silc", [128, 48], F32)
        self.mod, self.mod_b = self.T(es, "mod", [128, 3, 96], F32)
        self.A1, self.A1_b = self.T(es, "A1", [128, 3, 16], F32)
        self.A2, self.A2_b = self.T(es, "A2", [128, 3, 16], F32)
        self.lamt, self.lamt_b = self.T(es, "lamt", [128, 8], F32)
        self.epst, self.epst_b = self.T(es, "epst", [128, 1], F32)
        with ExitStack() as sc:
            tmp, tmp_b = self.T(sc, "ctmp", [128, 128], F32)
            S.op("dve", [], [self.ones_b], lambda e: e.memset(self.ones[:], 1.0))
            S.op("dve", [], [self.epst_b], lambda e: e.memset(self.epst[:], EPS))
            S.dma("sp", tmp[:], self.rmat[:, :], writes=[tmp_b])
            S.dma("sp", self.vec[:], self.vecs[:, :], writes=[self.vec_b])
            S.dma("sp", self.silc[:], self.cvec[:, :], writes=[self.silc_b])
            S.op("act", [tmp_b], [self.rT_b], lambda e: e.activation(out=self.rT[:], in_=tmp[:], func=AF.Copy))
            S.op("act", [self.silc_b], [self.silc_b],
                 lambda e: e.activation(out=self.silc[:], in_=self.silc[:], func=AF.Silu))
            S.barrier()

    A_SRCS = ("w_in", "w_uq", "w_ukv")

    def conv_staging(self, sc):
        st32 = [self.T(sc, f"cv32_{i}", [128, 2048], F32) for i in range(3)]
        st16 = [self.T(sc, f"cv16_{i}", [128, 2048], BF16) for i in range(3)]
        return st32, st16

    def phase_conv(self, sc, l, group):
        for _ in self.conv_gen(self.conv_staging(sc), l, ["act", "dve", "pool"], group):
            pass

    def conv_gen(self, staging, l, engs, group):
        S = self.S
        NST = 3
        PIECE = 2048
        st32, st16 = staging
        it = 0
        by_src = {}
        for s in self.streams.values():
            g = "A" if s.src in self.A_SRCS else "C"
            if g not in group:
                continue
            by_src.setdefault((s.src, s.k0, s.KC), []).append(s)
        for (src, k0, KC), slist in by_src.items():
            K, N = WSHAPES[src]
            segs = []
            for s in slist:
                for ci, chunk in enumerate(s.chunks):
                    off = (ci % 2) * 128
                    for (c0, w) in chunk:
                        segs.append((c0, w, s, ci // 2, off))
                        off += w
            segs.sort(key=lambda x: (x[0], x[2].name, x[3], x[4]))
            pieces = []
            pstart, pend = None, None
            for x in segs:
                if pstart is None:
                    pstart, pend = x[0], x[0] + x[1]
                elif x[0] + x[1] - pstart <= PIECE:
                    pend = max(pend, x[0] + x[1])
                else:
                    pieces.append((pstart, pend - pstart))
                    pstart, pend = x[0], x[0] + x[1]
            pieces.append((pstart, pend - pstart))
            W = self.wsrc[src]
            for kc in range(KC):
                r0 = k0 + kc * 128
                for (pc0, pw) in pieces:
                    inside = [x for x in segs if x[0] >= pc0 and x[0] + x[1] <= pc0 + pw]
                    straddle = [x for x in segs if not (x[0] + x[1] <= pc0 or x[0] >= pc0 + pw)
                                and x not in inside]
                    assert not straddle, (src, pc0, pw, straddle[:2])
                    if not inside:
                        continue
                    (t32, b32), (t16, b16) = st32[it % NST], st16[it % NST]
                    eng = engs[it % len(engs)]
                    it += 1
                    S.dma("sp", t32[:, 0:pw], W[l, r0:r0 + 128, pc0:pc0 + pw], writes=[b32])
                    if eng == "act":
                        S.op("act", [b32], [b16],
                             lambda e: e.activation(out=t16[:, 0:pw], in_=t32[:, 0:pw], func=AF.Copy))
                    else:
                        S.op(eng, [b32], [b16], lambda e: e.tensor_copy(t16[:, 0:pw], t32[:, 0:pw]))
                    i = 0
                    while i < len(inside):
                        c0, w, s, blk, off = inside[i]
                        run = None
                        if off == 0 and w == 128 and i + 1 < len(inside):
                            j = i
                            nb = 0
                            cc = c0
                            bb = blk
                            while (j + 1 < len(inside)
                                   and inside[j][2] is s and inside[j + 1][2] is s
                                   and inside[j][3] == bb and inside[j + 1][3] == bb
                                   and inside[j][4] == 0 and inside[j + 1][4] == 128
                                   and inside[j][1] == 128 and inside[j + 1][1] == 128
                                   and inside[j][0] == cc and inside[j + 1][0] == cc + 128):
                                nb += 1
                                j += 2
                                cc += 256
                                bb += 1
                            if nb >= 1:
                                run = (nb, j)
                        if run is not None:
                            nb, j = run
                            dst = s.dram[l][blk:blk + nb, :, kc * WB:(kc + 1) * WB].rearrange("b p c -> p b c")
                            srcap = t16[:, c0 - pc0:c0 - pc0 + nb * WB].rearrange("p (b c) -> p b c", b=nb)
                            S.dma("pool", dst, srcap, reads=[b16])
                            i = j
                        else:
                            dst = s.dram[l][blk, :, kc * WB + off:kc * WB + off + w]
                            S.dma("pool", dst, t16[:, c0 - pc0:c0 - pc0 + w], reads=[b16])
                            i += 1
                    yield

    def phase_ada(self, sc, l):
        S = self.S
        nc = self.nc
        NBLK = 24
        wb = [self.T(sc, f"adaw{i}", [128, 16, 512], F32) for i in range(2)]
        psum, psum_b = self.ps[0], self.psb[0]
        wsrc = self.w_ada[l].rearrange("(kc p) n -> p kc n", p=128)
        for nb in range(NBLK):
            t, b = wb[nb % 2]
            for k4 in range(4):
                S.dma("sp", t[:, k4 * 4:(k4 + 1) * 4, :], wsrc[:, k4 * 4:(k4 + 1) * 4, nb * 512:(nb + 1) * 512],
                      writes=[b], join=(k4 > 0))
            for m in range(4):
                ch = nb * 4 + m
                for kc in range(16):
                    S.op("pe", [b, self.silc_b], [psum_b],
                         lambda e: e.matmul(psum[:, ch * 3:ch * 3 + 3], t[:, kc, m * 128:(m + 1) * 128],
                                            self.silc[:, kc * 3:kc * 3 + 3], start=(kc == 0), stop=(kc == 15)))
        vb = l * V_PER_LAYER
        adat, adat_b = self.T(sc, "adat", [128, 288], F32)
        S.op("act", [psum_b], [adat_b], lambda e: e.activation(out=adat[:, :], in_=psum[:, 0:288], func=AF.Copy))
        psv = adat[:, :].rearrange("p (c j) -> p c j", j=3)
        psum_b = adat_b
        for j in range(3):
            S.op("dve", [psum_b, self.vec_b], [self.mod_b],
                 lambda e: e.tensor_tensor(out=self.mod[:, j, :], in0=psv[:, :, j],
                                           in1=self.vec[:, vb + V_BADA:vb + V_BADA + 96], op=ALU.add))
        for j in range(3):
            S.op("dve", [self.mod_b, self.vec_b], [self.A1_b],
                 lambda e: e.scalar_tensor_tensor(out=self.A1[:, j, :], in0=self.mod[:, j, 16:32], scalar=1.0,
                                                  in1=self.vec[:, vb + V_N1:vb + V_N1 + 16],
                                                  op0=ALU.add, op1=ALU.mult))
            S.op("dve", [self.mod_b, self.vec_b], [self.A2_b],
                 lambda e: e.scalar_tensor_tensor(out=self.A2[:, j, :], in0=self.mod[:, j, 64:80], scalar=1.0,
                                                  in1=self.vec[:, vb + V_N2:vb + V_N2 + 16],
                                                  op0=ALU.add, op1=ALU.mult))
        lam_init = 0.8 - 0.6 * math.exp(-0.3 * l)
        dl, dl_b = self.T(sc, "dl", [128, 256], F32)
        pr, pr_b = self.T(sc, "dlp", [128, 128], F32)
        S.dma("sp", dl[:], self.dlam[:, l * 256:(l + 1) * 256], writes=[dl_b])
        S.op("dve", [dl_b], [pr_b], lambda e: e.tensor_tensor(out=pr[:, 0:64], in0=dl[:, 0:64], in1=dl[:, 64:128], op=ALU.mult))
        S.op("dve", [dl_b], [pr_b], lambda e: e.tensor_tensor(out=pr[:, 64:128], in0=dl[:, 128:192], in1=dl[:, 192:256], op=ALU.mult))
        S.op("dve", [pr_b], [self.lamt_b], lambda e: e.reduce_sum(out=self.lamt[:, 2:3], in_=pr[:, 0:64], axis=AX.X))
        S.op("dve", [pr_b, self.lamt_b], [self.lamt_b], lambda e: e.reduce_sum(out=self.lamt[:, 3:4], in_=pr[:, 64:128], axis=AX.X))
        S.op("act", [self.lamt_b], [self.lamt_b], lambda e: e.activation(out=self.lamt[:, 4:6], in_=self.lamt[:, 2:4], func=AF.Exp))
        S.op("dve", [self.lamt_b], [self.lamt_b], lambda e: e.tensor_tensor(out=self.lamt[:, 6:7], in0=self.lamt[:, 5:6], in1=self.lamt[:, 4:5], op=ALU.subtract))
        S.op("dve", [self.lamt_b], [self.lamt_b], lambda e: e.tensor_scalar(out=self.lamt[:, 0:1], in0=self.lamt[:, 6:7], scalar1=-lam_init, scalar2=1.0, op0=ALU.add, op1=ALU.mult))
        S.op("dve", [self.vec_b, self.lamt_b], [self.lamt_b],
             lambda e: e.tensor_scalar(out=self.lamt[:, 1:2], in0=self.vec[:, vb + V_SUB:vb + V_SUB + 1],
                                       scalar1=(1.0 - lam_init), scalar2=1.0, op0=ALU.mult, op1=ALU.mult))
        if "MODD" in self.debug.get("dump", ()):
            S.dma("pool", self.MODD[:, :], self.mod[:].rearrange("p j c -> p (j c)"), reads=[self.mod_b])

    def alloc_wslots(self, sc, n):
        self.wslots = [self.T(sc, f"wsl{i}", [128, 16 * WB], BF16) for i in range(n)]
        self.wplan = []
        self.wissued = 0
        self.wbase = 0

    def wq_plan(self, items):
        self.wplan = list(items)
        self.wissued = 0
        self.wcons = 0

    def wq_issue_to(self, upto):
        n = len(self.wslots)
        while self.wissued < min(upto, len(self.wplan)):
            i = self.wissued
            s, blk = self.wplan[i]
            t, b = self.wslots[(self.wbase + i) % n]
            self.S.dma("sp", t[:, 0:s.KC * WB], s.dram[self.l][blk, :, :], writes=[b])
            self.wissued += 1

    def wq_get(self, i, oldest=None):
        n = len(self.wslots)
        if oldest is None:
            oldest = i
        self.wq_issue_to(oldest + n)
        return self.wslots[(self.wbase + i) % n]

    def wq_done(self):
        self.wbase = (self.wbase + len(self.wplan)) % len(self.wslots)
        self.wplan = []

    def gemm_F(self, plan_base, stream, blks, nchunks, rhs_fn, rhs_bufs, nsub, bank_groups, epilogue):
        S = self.S
        KC = stream.KC
        pending = None
        gi = 0
        for bi in range(len(blks)):
            wt, wb_ = self.wq_get(plan_base + bi)
            for j in range(2):
                ci = bi * 2 + j
                if ci >= nchunks:
                    break
                banks = bank_groups[gi % len(bank_groups)]
                gi += 1
                for s in range(nsub):
                    bk = banks[s]
                    for kc in range(KC):
                        S.op("pe", [wb_] + rhs_bufs, [self.psb[bk]],
                             lambda e: e.matmul(self.ps[bk][:, :], wt[:, kc * WB + j * 128:kc * WB + (j + 1) * 128],
                                                rhs_fn(kc, s), start=(kc == 0), stop=(kc == KC - 1)))
                if pending is not None:
                    pending()
                pending = epilogue(ci, banks)
        if pending is not None:
            pending()

    def rstd_from_stats(self, banks, nsub, dim, rstd, rstd_b):
        S = self.S
        for s in range(nsub):
            bk = banks[s]
            S.op("act", [self.psb[bk], self.epst_b], [rstd_b],
                 lambda e: e.activation(out=rstd[:, s * 512:(s + 1) * 512], in_=self.ps[bk][:, :], func=AF.Sqrt,
                                        bias=self.epst[:, 0:1], scale=1.0 / dim))
        w = nsub * 512
        S.op("dve", [rstd_b], [rstd_b], lambda e: e.reciprocal(out=rstd[:, 0:w], in_=rstd[:, 0:w]))

    def alloc_A(self, sc):
        T = self.T
        self.alloc_wslots(sc, 5)
        self.xc = [T(sc, f"xc{i}", [128, 1024], F32) for i in range(2)]
        self.hT, self.hT_b = T(sc, "hT", [128, 16, 1024], BF16)
        self.cq32, self.cq32_b = T(sc, "cq32", [128, 4, 1024], F32)
        self.ckv32, self.ckv32_b = T(sc, "ckv32", [128, 2, 1024], F32)
        self.cqn, self.cqn_b = T(sc, "cqn", [128, 4, 1024], BF16)
        self.ckvn, self.ckvn_b = T(sc, "ckvn", [128, 2, 1024], BF16)
        self.rstd, self.rstd_b = T(sc, "rstd", [128, 1024], F32)
        self.tmpf = [T(sc, f"tmpf{i}", [128, 1024], F32) for i in range(3)]
        self.stage = [T(sc, f"stage{i}", [128, 1024], BF16) for i in range(4)]
        self.sq = [T(sc, f"sq{i}", [128, 1024], BF16) for i in range(2)]
        self.xb = [T(sc, f"xb{i}", [128, 1024], BF16) for i in range(2)]
        self.cs, self.cs_b = T(sc, "cs", [128, 2 * SEQ], F32)
        self.S.dma("sp", self.cs[:], self.cossin[:, :], writes=[self.cs_b])
        self.rr = {"xc": 0, "tmpf": 0, "stage": 0, "sq": 0, "xb": 0}

    def rot(self, name):
        lst = getattr(self, name)
        i = self.rr[name]
        self.rr[name] = (i + 1) % len(lst)
        return lst[i]

    def norm_modulate(self, load_chunk, nsub, A, Bsh, j, out_t, out_b, stat_banks):
        S = self.S
        Tw = nsub * 512
        for c in range(16):
            xt, xb_ = load_chunk(c)
            sq, sq_b = self.rot("sq")
            S.op("act", [xb_], [sq_b], lambda e: e.activation(out=sq[:, 0:Tw], in_=xt[:, 0:Tw], func=AF.Square))
            for s in range(nsub):
                bk = stat_banks[s]
                S.op("pe", [sq_b, self.ones_b], [self.psb[bk]],
                     lambda e: e.matmul(self.ps[bk][:, :], self.ones[:, :], sq[:, s * 512:(s + 1) * 512],
                                        start=(c == 0), stop=(c == 15)))
        self.rstd_from_stats(stat_banks, nsub, D, self.rstd, self.rstd_b)
        for c in range(16):
            xt, xb_ = load_chunk(c)
            tf, tf_b = self.rot("tmpf")
            S.op("dve", [xb_, self.rstd_b, A[1]], [tf_b],
                 lambda e: e.scalar_tensor_tensor(out=tf[:, 0:Tw], in0=xt[:, 0:Tw], scalar=A[0][:, j, c:c + 1],
                                                  in1=self.rstd[:, 0:Tw], op0=ALU.mult, op1=ALU.mult))
            S.op("act", [tf_b, self.mod_b], [out_b],
                 lambda e: e.activation(out=out_t[:, c, 0:Tw], in_=tf[:, 0:Tw], func=AF.Identity,
                                        bias=self.mod[:, j, Bsh + c:Bsh + c + 1], scale=1.0))

    def store_chunk(self, dst_ap, src_t, src_b, Tw, rows=None):
        if rows is None:
            self.S.dma("pool", dst_ap, src_t[:, 0:Tw], reads=[src_b])
        else:
            r0, r1 = rows
            self.S.dma("pool", dst_ap, src_t[r0:r1, 0:Tw], reads=[src_b])

    def epi_copy_store(self, banks, nsub, dsts):
        S = self.S
        st, st_b = self.rot("stage")
        for s in range(nsub):
            bk = banks[s]
            S.op("act", [self.psb[bk]], [st_b],
                 lambda e: e.activation(out=st[:, s * 512:(s + 1) * 512], in_=self.ps[bk][:, :], func=AF.Copy))
        for dst, rows in dsts:
            self.store_chunk(dst, st, st_b, nsub * 512, rows)

    def epi_rope_store(self, banks, nsub, pos0, dsts):
        S = self.S
        xb, xb_b = self.rot("xb")
        for s in range(nsub):
            bk = banks[s]
            S.op("act", [self.psb[bk]], [xb_b],
                 lambda e: e.activation(out=xb[:, s * 512:(s + 1) * 512], in_=self.ps[bk][:, :], func=AF.Copy))

        def post():
            st, st_b = self.rot("stage")
            for s in range(nsub):
                bk = banks[s]
                rb = 6 + s
                S.op("pe", [xb_b, self.rT_b], [self.psb[rb]],
                     lambda e: e.matmul(self.ps[rb][:, :], self.rT[:, :], xb[:, s * 512:(s + 1) * 512],
                                        start=True, stop=True))
                if self.debug.get("rope_pe_only"):
                    continue
                t1, t1_b = self.rot("tmpf")
                t2, t2_b = self.rot("tmpf")
                p = pos0 + s * 512
                if not self.debug.get("rope_psum"):
                    S.op("act", [self.psb[bk]], [t1_b],
                         lambda e: e.activation(out=t1[:, 0:512], in_=self.ps[bk][:, :], func=AF.Copy))
                    S.op("act", [self.psb[rb]], [t2_b],
                         lambda e: e.activation(out=t2[:, 0:512], in_=self.ps[rb][:, :], func=AF.Copy))
                    S.op("dve", [t1_b, self.cs_b], [t1_b],
                         lambda e: e.tensor_tensor(out=t1[:, 0:512], in0=t1[:, 0:512], in1=self.cs[:, p:p + 512], op=ALU.mult))
                    S.op("dve", [t2_b, self.cs_b], [t2_b],
                         lambda e: e.tensor_tensor(out=t2[:, 0:512], in0=t2[:, 0:512],
                                                   in1=self.cs[:, SEQ + p:SEQ + p + 512], op=ALU.mult))
                else:
                    S.op("dve", [self.psb[bk], self.cs_b], [t1_b],
                         lambda e: e.tensor_tensor(out=t1[:, 0:512], in0=self.ps[bk][:, :], in1=self.cs[:, p:p + 512], op=ALU.mult))
                    S.op("dve", [self.psb[rb], self.cs_b], [t2_b],
                         lambda e: e.tensor_tensor(out=t2[:, 0:512], in0=self.ps[rb][:, :],
                                                   in1=self.cs[:, SEQ + p:SEQ + p + 512], op=ALU.mult))
                S.op("pool" if self.debug.get("pool_add") else "dve", [t1_b, t2_b], [st_b],
                     lambda e: e.tensor_tensor(out=st[:, s * 512:(s + 1) * 512], in0=t1[:, 0:512], in1=t2[:, 0:512], op=ALU.add))
            if self.debug.get("rope_pe_only") or self.debug.get("rope_no_store"):
                return
            for dst, rows in dsts:
                self.store_chunk(dst, st, st_b, nsub * 512, rows)
        if self.debug.get("no_defer"):
            post()
            return None
        return post

    def phase_A(self, l, t):
        S = self.S
        tok0, Tw, j, is_ctx = TILES[t]
        nsub = Tw // 512
        src = self.xin if l == 0 else self.XS
        vb = l * V_PER_LAYER
        pos0 = 0 if is_ctx else (tok0 - LAT0) % SEQ
        st = self.streams
        cols = slice(tok0, tok0 + Tw)

        def load_chunk(c):
            xt, xb_ = self.rot("xc")
            S.dma("sp", xt[:, 0:Tw], src[c * 128:(c + 1) * 128, tok0:tok0 + Tw], writes=[xb_])
            return xt, xb_

        plan = [(st["win_f"], b) for b in range(st["win_f"].nblk)]
        p_wint = len(plan)
        plan += [(st["win_t"], b) for b in range(4)]
        p_wuq = len(plan)
        plan += [(st["wuq_f"], b) for b in range(6)]
        p_wukv = len(plan)
        plan += [(st["wukv_f"], b) for b in range(4)]
        p_wukvt = len(plan)
        plan += [(st["wukv_t"], b) for b in range(4)]
        self.wq_plan(plan)
        self.wq_issue_to(len(self.wslots))

        self.norm_modulate(load_chunk, nsub, (self.A1, self.A1_b), 0, j, self.hT, self.hT_b, [6, 7])

        groups = [[0, 1], [2, 3], [4, 5]]
        steps = self.debug.get("A_steps", 99)
        if steps < 2:
            self.wq_done()
            return

        def rhs_h(kc, s):
            return self.hT[:, kc, s * 512:(s + 1) * 512]

        def epi_win(ci, banks):
            if ci < 6:
                if ci < 4:
                    dst_t, dst_b, cc, first, lastc = self.cq32, self.cq32_b, ci, ci == 0, ci == 3
                else:
                    dst_t, dst_b, cc, first, lastc = self.ckv32, self.ckv32_b, ci - 4, ci == 4, ci == 5
                sq, sq_b = self.rot("sq")
                for s in range(nsub):
                    bk = banks[s]
                    S.op("act", [self.psb[bk]], [dst_b],
                         lambda e: e.activation(out=dst_t[:, cc, s * 512:(s + 1) * 512], in_=self.ps[bk][:, :], func=AF.Copy))
                    S.op("act", [self.psb[bk]], [sq_b],
                         lambda e: e.activation(out=sq[:, s * 512:(s + 1) * 512], in_=self.ps[bk][:, :], func=AF.Square))

                def post():
                    for s in range(nsub):
                        rb = 6 + s
                        S.op("pe", [sq_b, self.ones_b], [self.psb[rb]],
                             lambda e: e.matmul(self.ps[rb][:, :], self.ones[:, :], sq[:, s * 512:(s + 1) * 512],
                                                start=first, stop=lastc))
                    if lastc:
                        if ci == 3:
                            n_t, n_b, nn, dim, gv = self.cqn, self.cqn_b, 4, 512, V_QA
                        else:
                            n_t, n_b, nn, dim, gv = self.ckvn, self.ckvn_b, 2, 256, V_KVA
                        self.rstd_from_stats([6, 7], nsub, dim, self.rstd, self.rstd_b)
                        for c2 in range(nn):
                            S.op("dve", [dst_b, self.rstd_b, self.vec_b], [n_b],
                                 lambda e: e.scalar_tensor_tensor(out=n_t[:, c2, 0:Tw], in0=dst_t[:, c2, 0:Tw],
                                                                  scalar=self.vec[:, vb + gv + c2:vb + gv + c2 + 1],
                                                                  in1=self.rstd[:, 0:Tw], op0=ALU.mult, op1=ALU.mult))
                return post
            if ci < 22:
                h = (ci - 6) % 8
                dstT = self.QD if ci < 14 else self.KD
                dsts = [(dstT[h, :, cols], None)]
                if is_ctx:
                    self.epi_copy_store(banks, nsub, dsts)
                    return None
                return self.epi_rope_store(banks, nsub, pos0, dsts)
            if ci == 22:
                dsts = [(self.KMP[:, cols], None)]
                if is_ctx:
                    self.epi_copy_store(banks, nsub, dsts)
                    return None
                return self.epi_rope_store(banks, nsub, pos0, dsts)
            g = ci - 23
            sg, sg_b = self.rot("stage")
            for s in range(nsub):
                bk = banks[s]
                S.op("act", [self.psb[bk], self.vec_b], [sg_b],
                     lambda e: e.activation(out=sg[:, s * 512:(s + 1) * 512], in_=self.ps[bk][:, :], func=AF.Sigmoid,
                                            bias=self.vec[:, vb + V_BG + g:vb + V_BG + g + 1], scale=1.0))
            self.store_chunk(self.G[g, :, cols], sg, sg_b, Tw)
            return None

        skip = self.debug.get("A_skip", ())
        self.gemm_F(0, st["win_f"], list(range(st["win_f"].nblk)), 55 if "win" not in skip else 6, rhs_h, [self.hT_b], nsub, groups, epi_win)

        if steps < 3:
            self.wq_done()
            return
        if "vd" not in skip:
            self.gemm_T(p_wint, st["win_t"], self.hT, self.hT_b, 16, Tw, self.VD, tok0)
        if steps < 4:
            self.wq_done()
            return

        def rhs_cq(kc, s):
            return self.cqn[:, kc, s * 512:(s + 1) * 512]

        def epi_wuq(ci, banks):
            if ci < 8:
                self.epi_copy_store(banks, nsub, [(self.QMN[ci, :, cols], None)])
                return None
            dsts = [((self.QD if self.debug.get("qmp_to_qd") else self.QMP)[ci - 8, :, cols], None)]
            if is_ctx:
                self.epi_copy_store(banks, nsub, dsts)
                return None
            return self.epi_rope_store(banks, nsub, pos0, dsts)

        if "wuq" not in skip:
            self.gemm_F(p_wuq, st["wuq_f"], list(range(6)), self.debug.get("wuq_n", 12), rhs_cq, [self.cqn_b], nsub, groups, epi_wuq)

        if steps < 5:
            self.wq_done()
            return
        def rhs_ckv(kc, s):
            return self.ckvn[:, kc, s * 512:(s + 1) * 512]

        def epi_wukv(ci, banks):
            self.epi_copy_store(banks, nsub, [(self.KMN[ci, :, cols], None)])
            return None

        if "wukv" not in skip:
            self.gemm_F(p_wukv, st["wukv_f"], list(range(4)), 8, rhs_ckv, [self.ckvn_b], nsub, groups, epi_wukv)
        if steps < 6:
            self.wq_done()
            return
        self.gemm_T(p_wukvt, st["wukv_t"], self.ckvn, self.ckvn_b, 2, Tw, self.VD if self.debug.get("vm_to_vd") else self.VM, tok0)
        self.wq_done()

    def gemm_T(self, plan_base, stream, act_t, act_b, KC, Tw, dstV, tok0):
        S = self.S
        slots = [self.wq_get(plan_base + b, oldest=plan_base) for b in range(4)]
        pairs = [[0, 1], [2, 3], [4, 5]]
        for tb in range(Tw // 128):
            banks = pairs[tb % 3]
            for cb in range(4):
                wt, wb_ = slots[cb]
                bk = banks[cb // 2]
                half = (cb % 2) * 256
                for kc in range(KC):
                    S.op("pe", [wb_, act_b], [self.psb[bk]],
                         lambda e: e.matmul(self.ps[bk][:, half:half + 256], act_t[:, kc, tb * 128:(tb + 1) * 128],
                                            wt[:, kc * WB:(kc + 1) * WB], start=(kc == 0), stop=(kc == KC - 1)))
            sv, sv_b = self.rot("stage")
            for hb in range(2):
                bk = banks[hb]
                S.op("act", [self.psb[bk]], [sv_b],
                     lambda e: e.activation(out=sv[:, hb * 512:(hb + 1) * 512], in_=self.ps[bk][:, :], func=AF.Copy))
            r0 = tok0 + tb * 128
            S.dma("pool", dstV[r0:r0 + 128, :], sv[:, 0:1024], reads=[sv_b])

    def alloc_B(self, sc):
        T = self.T
        S = self.S
        self.B_d = []
        self.B_m = []
        for i in range(2):
            d = {}
            d["K"] = T(sc, f"bK{i}", [128, 2304], BF16)
            d["V"] = T(sc, f"bV{i}", [128, 18, 128], BF16)
            d["Q1"] = T(sc, f"bQ1{i}", [128, 2304], BF16)
            d["Q2"] = T(sc, f"bQ2{i}", [128, 2304], BF16)
            self.B_d.append(d)
            m = {}
            m["K"] = T(sc, f"mK{i}", [128, 2304], BF16)
            m["V"] = T(sc, f"mV{i}", [128, 18, 128], BF16)
            m["Q"] = T(sc, f"mQ{i}", [128, 2304], BF16)
            m["QP"] = T(sc, f"mQP{i}", [128, 2304], BF16)
            self.B_m.append(m)
        self.KP = [T(sc, f"mKP{i}", [128, 2304], BF16) for i in range(2)]
        self.E = [T(sc, f"E{i}", [128, 512], BF16) for i in range(6)]
        self.fin = [T(sc, f"fin{i}", [128, 512], F32) for i in range(8)]
        self.ys = [T(sc, f"ys{i}", [128, 512], BF16) for i in range(3)]
        self.sqb = [T(sc, f"sqb{i}", [128, 512], BF16) for i in range(2)]
        self.rr = {"E": 0, "fin": 0, "ys": 0, "sqb": 0}
        for i in range(2):
            t, b = self.B_d[i]["Q1"]
            S.op("dve", [], [b], lambda e: e.memset(t[64:128, :], 0.0))
            t, b = self.B_d[i]["Q2"]
            S.op("dve", [], [b], lambda e: e.memset(t[0:64, :], 0.0))
            t, b = self.B_m[i]["QP"]
            S.op("dve", [], [b], lambda e: e.memset(t[64:128, :], 0.0))
            t, b = self.KP[i]
            S.op("dve", [], [b], lambda e: e.memset(t[64:128, :], 0.0))
        self.bset = 0

    @staticmethod
    def chain(gens):
        for g in gens:
            for _ in g:
                yield

    def bg_step(self, n):
        if self.bg is None:
            return
        for _ in range(n):
            try:
                next(self.bg)
            except StopIteration:
                self.bg = None
                return

    def tok_ranges(self, b):
        return [(0, b * CTX, CTX), (CTX, LAT0 + b * SEQ, SEQ)]

    def phase_B(self, l, b, last):
        S = self.S
        rng = self.tok_ranges(b)
        kp, kp_b = self.KP[b % 2]
        for ri, (c0, t0, n) in enumerate(rng):
            S.dma("sp", kp[0:64, c0:c0 + n], self.KMP[0:64, t0:t0 + n], writes=[kp_b], join=(ri > 0))
        qgroups = [(CTX + g * 512, 512, list(range(18))) for g in range(4)]
        if not last:
            qgroups.append((0, CTX, [0, 1]))
        for h in range(NH):
            d = self.B_d[self.bset % 2]
            m = self.B_m[self.bset % 2]
            self.bset += 1
            (K, K_b), (V, V_b), (Q1, Q1_b), (Q2, Q2_b) = d["K"], d["V"], d["Q1"], d["Q2"]
            for ri, (c0, t0, n) in enumerate(rng):
                jn = ri > 0
                S.dma("sp", K[:, c0:c0 + n], self.KD[h, :, t0:t0 + n], writes=[K_b], join=jn)
                S.dma("sp", Q1[0:64, c0:c0 + n], self.QD[h, 0:64, t0:t0 + n], writes=[Q1_b], join=jn)
                S.dma("sp", Q2[64:128, c0:c0 + n], self.QD[h, 64:128, t0:t0 + n], writes=[Q2_b], join=jn)
                for j0 in range(0, n // 128, 8):
                    nb = min(8, n // 128 - j0)
                    S.dma("sp", V[:, c0 // 128 + j0:c0 // 128 + j0 + nb, :],
                          self.VD[t0 + j0 * 128:t0 + (j0 + nb) * 128, h * 128:(h + 1) * 128].rearrange("(j p) c -> p j c", p=128),
                          writes=[V_b], join=(jn or j0 > 0))
            (MK, MK_b), (MV, MV_b), (MQ, MQ_b), (MQP, MQP_b) = m["K"], m["V"], m["Q"], m["QP"]
            for ri, (c0, t0, n) in enumerate(rng):
                jn = ri > 0
                S.dma("sp", MK[:, c0:c0 + n], self.KMN[h, :, t0:t0 + n], writes=[MK_b], join=jn)
                S.dma("sp", MQ[:, c0:c0 + n], self.QMN[h, :, t0:t0 + n], writes=[MQ_b], join=jn)
                S.dma("sp", MQP[0:64, c0:c0 + n], self.QMP[h // 2, (h % 2) * 64:(h % 2) * 64 + 64, t0:t0 + n], writes=[MQP_b], join=jn)
                for j0 in range(0, n // 128, 8):
                    nb = min(8, n // 128 - j0)
                    S.dma("sp", MV[:, c0 // 128 + j0:c0 // 128 + j0 + nb, :],
                          self.VM[t0 + j0 * 128:t0 + (j0 + nb) * 128, h * 128:(h + 1) * 128].rearrange("(j p) c -> p j c", p=128),
                          writes=[MV_b], join=(jn or j0 > 0))
            for (q0, qw, kbs) in qgroups:
                self.attn_diff(b, h, q0, qw, kbs, K, K_b, V, V_b, Q1, Q1_b, Q2, Q2_b)
                self.bg_step(2)
            for gi, (q0, qw, kbs) in enumerate(qgroups):
                self.attn_mla(b, h, q0, qw, kbs, MK, MK_b, kp, kp_b, MV, MV_b, MQ, MQ_b, MQP, MQP_b, gi % 2)
                self.bg_step(2)

    def q_dst(self, dstT, h, b, q0, qw):
        if q0 < CTX:
            t0 = b * CTX + q0
        else:
            t0 = LAT0 + b * SEQ + (q0 - CTX)
        return dstT[h, :, t0:t0 + qw]

    def attn_diff(self, b, h, q0, qw, kbs, K, K_b, V, V_b, Q1, Q1_b, Q2, Q2_b):
        S = self.S
        ps, psb = self.ps, self.psb
        SA, SB = [0, 1], [2, 3]
        A1, D1, A2, D2 = 4, 5, 6, 7
        n = len(kbs)
        Es = {}

        def Smm(i):
            kb = kbs[i]
            S.op("pe", [K_b, Q1_b], [psb[SA[i % 2]]],
                 lambda e: e.matmul(ps[SA[i % 2]][:, 0:qw], K[:, kb * 128:(kb + 1) * 128], Q1[:, q0:q0 + qw], start=True, stop=True))
            S.op("pe", [K_b, Q2_b], [psb[SB[i % 2]]],
                 lambda e: e.matmul(ps[SB[i % 2]][:, 0:qw], K[:, kb * 128:(kb + 1) * 128], Q2[:, q0:q0 + qw], start=True, stop=True))

        def Xp(i):
            e1, e1_b = self.rot("E")
            e2, e2_b = self.rot("E")
            S.op("act", [psb[SA[i % 2]]], [e1_b],
                 lambda e: e.activation(out=e1[:, 0:qw], in_=ps[SA[i % 2]][:, 0:qw], func=AF.Exp, scale=DIFF_SCALE))
            S.op("act", [psb[SB[i % 2]]], [e2_b],
                 lambda e: e.activation(out=e2[:, 0:qw], in_=ps[SB[i % 2]][:, 0:qw], func=AF.Exp, scale=DIFF_SCALE))
            Es[i] = (e1, e1_b, e2, e2_b)

        def AV(i):
            kb = kbs[i]
            e1, e1_b, e2, e2_b = Es.pop(i)
            st, sp = (i == 0), (i == n - 1)
            S.op("pe", [V_b, e1_b], [psb[A1]], lambda e: e.matmul(ps[A1][:, 0:qw], V[:, kb, :], e1[:, 0:qw], start=st, stop=sp))
            S.op("pe", [self.ones_b, e1_b], [psb[D1]], lambda e: e.matmul(ps[D1][:, 0:qw], self.ones[:, :], e1[:, 0:qw], start=st, stop=sp))
            S.op("pe", [V_b, e2_b], [psb[A2]], lambda e: e.matmul(ps[A2][:, 0:qw], V[:, kb, :], e2[:, 0:qw], start=st, stop=sp))
            S.op("pe", [self.ones_b, e2_b], [psb[D2]], lambda e: e.matmul(ps[D2][:, 0:qw], self.ones[:, :], e2[:, 0:qw], start=st, stop=sp))

        Smm(0)
        for i in range(n):
            if i + 1 < n:
                Smm(i + 1)
            Xp(i)
            AV(i)
        f1, f1_b = self.rot("fin")
        f2, f2_b = self.rot("fin")
        f3, f3_b = self.rot("fin")
        f4, f4_b = self.rot("fin")
        S.op("act", [psb[D1]], [f1_b], lambda e: e.activation(out=f1[:, 0:qw], in_=ps[D1][:, 0:qw], func=AF.Copy))
        S.op("act", [psb[A1]], [f3_b], lambda e: e.activation(out=f3[:, 0:qw], in_=ps[A1][:, 0:qw], func=AF.Copy))
        S.op("act", [psb[D2]], [f2_b], lambda e: e.activation(out=f2[:, 0:qw], in_=ps[D2][:, 0:qw], func=AF.Copy))
        S.op("act", [psb[A2]], [f4_b], lambda e: e.activation(out=f4[:, 0:qw], in_=ps[A2][:, 0:qw], func=AF.Copy))
        S.op("dve", [f1_b], [f1_b], lambda e: e.reciprocal(out=f1[:, 0:qw], in_=f1[:, 0:qw]))
        S.op("dve", [f3_b, f1_b], [f1_b], lambda e: e.tensor_tensor(out=f1[:, 0:qw], in0=f3[:, 0:qw], in1=f1[:, 0:qw], op=ALU.mult))
        S.op("dve", [f2_b], [f2_b], lambda e: e.reciprocal(out=f2[:, 0:qw], in_=f2[:, 0:qw]))
        S.op("dve", [f4_b, f2_b], [f2_b], lambda e: e.tensor_tensor(out=f2[:, 0:qw], in0=f4[:, 0:qw], in1=f2[:, 0:qw], op=ALU.mult))
        S.op("dve", [f1_b, f2_b, self.lamt_b], [f3_b],
             lambda e: e.scalar_tensor_tensor(out=f3[:, 0:qw], in0=f2[:, 0:qw], scalar=self.lamt[:, 0:1], in1=f1[:, 0:qw],
                                              op0=ALU.mult, op1=ALU.add))
        sq, sq_b = self.rot("sqb")
        S.op("dve", [f3_b], [sq_b], lambda e: e.tensor_tensor(out=sq[:, 0:qw], in0=f3[:, 0:qw], in1=f3[:, 0:qw], op=ALU.mult))
        S.op("pe", [sq_b, self.ones_b], [psb[D1]], lambda e: e.matmul(ps[D1][:, 0:qw], self.ones[:, :], sq[:, 0:qw], start=True, stop=True))
        S.op("act", [psb[D1], self.epst_b], [f1_b],
             lambda e: e.activation(out=f1[:, 0:qw], in_=ps[D1][:, 0:qw], func=AF.Ln, bias=self.epst[:, 0:1], scale=1.0 / 128))
        S.op("act", [f1_b], [f1_b], lambda e: e.activation(out=f1[:, 0:qw], in_=f1[:, 0:qw], func=AF.Exp, scale=-0.5))
        ys, ys_b = self.rot("ys")
        S.op("dve", [f3_b, f1_b, self.lamt_b], [ys_b],
             lambda e: e.scalar_tensor_tensor(out=ys[:, 0:qw], in0=f3[:, 0:qw], scalar=self.lamt[:, 1:2], in1=f1[:, 0:qw],
                                              op0=ALU.mult, op1=ALU.mult))
        S.dma("pool", self.q_dst(self.YD, h, b, q0, qw), ys[:, 0:qw], reads=[ys_b])

    def attn_mla(self, b, h, q0, qw, kbs, K, K_b, KP, KP_b, V, V_b, Q, Q_b, QP, QP_b, par):
        S = self.S
        ps, psb = self.ps, self.psb
        SA = [0, 1]
        A1, D1 = (4, 5) if par == 0 else (6, 7)
        n = len(kbs)
        Es = {}

        def Smm(i):
            kb = kbs[i]
            S.op("pe", [K_b, Q_b], [psb[SA[i % 2]]],
                 lambda e: e.matmul(ps[SA[i % 2]][:, 0:qw], K[:, kb * 128:(kb + 1) * 128], Q[:, q0:q0 + qw], start=True, stop=False))
            S.op("pe", [KP_b, QP_b], [psb[SA[i % 2]]],
                 lambda e: e.matmul(ps[SA[i % 2]][:, 0:qw], KP[:, kb * 128:(kb + 1) * 128], QP[:, q0:q0 + qw], start=False, stop=True))

        def Xp(i):
            e1, e1_b = self.rot("E")
            S.op("act", [psb[SA[i % 2]]], [e1_b],
                 lambda e: e.activation(out=e1[:, 0:qw], in_=ps[SA[i % 2]][:, 0:qw], func=AF.Exp, scale=MLA_SCALE))
            Es[i] = (e1, e1_b)

        def AV(i):
            kb = kbs[i]
            e1, e1_b = Es.pop(i)
            st, sp = (i == 0), (i == n - 1)
            S.op("pe", [V_b, e1_b], [psb[A1]], lambda e: e.matmul(ps[A1][:, 0:qw], V[:, kb, :], e1[:, 0:qw], start=st, stop=sp))
            S.op("pe", [self.ones_b, e1_b], [psb[D1]], lambda e: e.matmul(ps[D1][:, 0:qw], self.ones[:, :], e1[:, 0:qw], start=st, stop=sp))

        Smm(0)
        for i in range(n):
            if i + 1 < n:
                Smm(i + 1)
            Xp(i)
            AV(i)
        f1, f1_b = self.rot("fin")
        f2, f2_b = self.rot("fin")
        S.op("act", [psb[D1]], [f1_b], lambda e: e.activation(out=f1[:, 0:qw], in_=ps[D1][:, 0:qw], func=AF.Copy))
        S.op("act", [psb[A1]], [f2_b], lambda e: e.activation(out=f2[:, 0:qw], in_=ps[A1][:, 0:qw], func=AF.Copy))
        S.op("dve", [f1_b], [f1_b], lambda e: e.reciprocal(out=f1[:, 0:qw], in_=f1[:, 0:qw]))
        ys, ys_b = self.rot("ys")
        S.op("dve", [f2_b, f1_b], [ys_b], lambda e: e.tensor_tensor(out=ys[:, 0:qw], in0=f2[:, 0:qw], in1=f1[:, 0:qw], op=ALU.mult))
        S.dma("pool", self.q_dst(self.YM, h, b, q0, qw), ys[:, 0:qw], reads=[ys_b])

    def alloc_C(self, sc):
        T = self.T
        self.alloc_wslots(sc, 4)
        self.xt, _ = T(sc, "xt", [128, 16, 1024], F32)
        self.xt_b = [self.S.buf(f"xt{c}") for c in range(16)]
        self.mh, self.mh_b = T(sc, "mh", [128, 16, 1024], BF16)
        self.r1, self.r1_b = T(sc, "r1", [128, 16, 1024], BF16)
        self.rstd, self.rstd_b = T(sc, "rstdc", [128, 1024], F32)
        self.tmpf = [T(sc, f"tmpfc{i}", [128, 1024], F32) for i in range(4)]
        self.sq = [T(sc, f"sqc{i}", [128, 1024], BF16) for i in range(2)]
        self.gb = [T(sc, f"gb{i}", [128, 1024], BF16) for i in range(2)]
        self.rr = {"tmpf": 0, "sq": 0, "gb": 0}

    def phase_C(self, l, t, last):
        S = self.S
        tok0, Tw, j, is_ctx = TILES[t]
        nsub = Tw // 512
        src = self.xin if l == 0 else self.XS
        st = self.streams
        cols = slice(tok0, tok0 + Tw)
        ps, psb = self.ps, self.psb
        plan = []
        for bi in range(8):
            plan.append((st["wod_f"], bi))
            plan.append((st["wom_f"], bi))
        p_wout = len(plan)
        plan += [(st["wout_f"], bi) for bi in range(8)]
        p_mlp = len(plan)
        for q in range(4):
            plan += [(st["w1_f"], 8 * q + bi) for bi in range(8)]
            plan += [(st[f"w2_f{q}"], bi) for bi in range(8)]
        self.wq_plan(plan)
        self.wq_issue_to(len(self.wslots))
        for h in range(NH):
            S.dma("sp", self.r1[:, h, 0:Tw], self.YD[h, :, cols], writes=[self.r1_b], join=(h > 0))
        for h in range(NH):
            S.dma("sp", self.r1[:, 8 + h, 0:Tw], self.YM[h, :, cols], writes=[self.r1_b], join=True)
        for c in range(16):
            S.dma("sp", self.xt[:, c, 0:Tw], src[c * 128:(c + 1) * 128, cols], writes=[self.xt_b[c]])

        groups = [[0, 1, 2, 3], [4, 5, 6, 7]]
        gi = 0
        for bi in range(8):
            wd, wd_b = self.wq_get(2 * bi, oldest=2 * bi)
            wm, wm_b = self.wq_get(2 * bi + 1, oldest=2 * bi)
            for jj in range(2):
                c = bi * 2 + jj
                banks = groups[gi % 2]
                gi += 1
                gd, gd_b = self.rot("gb")
                gm, gm_b = self.rot("gb")
                S.dma("sp", gd[:, 0:Tw], self.G[c, :, cols], writes=[gd_b])
                S.dma("sp", gm[:, 0:Tw], self.G[16 + c, :, cols], writes=[gm_b])
                for s in range(nsub):
                    for (wt, wb_, off, bk) in ((wd, wd_b, 0, banks[s]), (wm, wm_b, 8, banks[2 + s])):
                        for kc in range(8):
                            S.op("pe", [wb_, self.r1_b], [psb[bk]],
                                 lambda e: e.matmul(ps[bk][:, :], wt[:, kc * WB + jj * 128:kc * WB + (jj + 1) * 128],
                                                    self.r1[:, off + kc, s * 512:(s + 1) * 512], start=(kc == 0), stop=(kc == 7)))
                for s in range(nsub):
                    t1, t1_b = self.rot("tmpf")
                    t2, t2_b = self.rot("tmpf")
                    sl = slice(s * 512, (s + 1) * 512)
                    S.op("act", [psb[banks[s]]], [t1_b],
                         lambda e: e.activation(out=t1[:, 0:512], in_=ps[banks[s]][:, :], func=AF.Copy))
                    S.op("act", [psb[banks[2 + s]]], [t2_b],
                         lambda e: e.activation(out=t2[:, 0:512], in_=ps[banks[2 + s]][:, :], func=AF.Copy))
                    S.op("dve", [t1_b, gd_b], [t1_b],
                         lambda e: e.tensor_tensor(out=t1[:, 0:512], in0=t1[:, 0:512], in1=gd[:, sl], op=ALU.mult))
                    S.op("dve", [t2_b, gm_b], [t2_b],
                         lambda e: e.tensor_tensor(out=t2[:, 0:512], in0=t2[:, 0:512], in1=gm[:, sl], op=ALU.mult))
                    S.op("dve", [t1_b, t2_b], [self.mh_b],
                         lambda e: e.tensor_tensor(out=self.mh[:, c, sl], in0=t1[:, 0:512], in1=t2[:, 0:512], op=ALU.add))

        groups3 = [[0, 1], [2, 3], [4, 5]]

        def rhs_m(kc, s):
            return self.mh[:, kc, s * 512:(s + 1) * 512]

        def epi_wout(ci, banks):
            for s in range(nsub):
                bk = banks[s]
                sl = slice(s * 512, (s + 1) * 512)
                tf, tf_b = self.rot("tmpf")
                S.op("act", [psb[bk], self.mod_b], [tf_b],
                     lambda e: e.activation(out=tf[:, 0:512], in_=ps[bk][:, :], func=AF.Copy,
                                            scale=self.mod[:, j, 32 + ci:33 + ci]))
                S.op("dve", [tf_b, self.xt_b[ci]], [self.xt_b[ci]],
                     lambda e: e.tensor_tensor(out=self.xt[:, ci, sl], in0=self.xt[:, ci, sl], in1=tf[:, 0:512], op=ALU.add))
            return None

        self.gemm_F(p_wout, st["wout_f"], list(range(8)), 16, rhs_m, [self.mh_b], nsub, groups3, epi_wout)

        def load_chunk(c):
            return self.xt[:, c, :], self.xt_b[c]

        self.norm_modulate(load_chunk, nsub, (self.A2, self.A2_b), 48, j, self.mh, self.mh_b, [6, 7])

        def rhs_u(kc, s):
            return self.r1[:, kc, s * 512:(s + 1) * 512]

        for q in range(4):
            def epi_w1(ci, banks):
                for s in range(nsub):
                    bk = banks[s]
                    sl = slice(s * 512, (s + 1) * 512)
                    tf, tf_b = self.rot("tmpf")
                    S.op("act", [psb[bk]], [tf_b], lambda e: e.activation(out=tf[:, 0:512], in_=ps[bk][:, :], func=AF.Relu))
                    S.op("dve", [tf_b], [self.r1_b],
                         lambda e: e.tensor_tensor(out=self.r1[:, ci, sl], in0=tf[:, 0:512], in1=tf[:, 0:512], op=ALU.mult))
                return None

            def epi_w2(ci, banks):
                for s in range(nsub):
                    bk = banks[s]
                    sl = slice(s * 512, (s + 1) * 512)
                    tf, tf_b = self.rot("tmpf")
                    S.op("act", [psb[bk], self.mod_b], [tf_b],
                         lambda e: e.activation(out=tf[:, 0:512], in_=ps[bk][:, :], func=AF.Copy,
                                                scale=self.mod[:, j, 80 + ci:81 + ci]))
                    S.op("dve", [tf_b, self.xt_b[ci]], [self.xt_b[ci]],
                         lambda e: e.tensor_tensor(out=self.xt[:, ci, sl], in0=self.xt[:, ci, sl], in1=tf[:, 0:512], op=ALU.add))
                return None

            base = p_mlp + q * 16
            self.gemm_F(base, st["w1_f"], list(range(8)), 16, rhs_m, [self.mh_b], nsub, groups3, epi_w1)
            self.gemm_F(base + 8, st[f"w2_f{q}"], list(range(8)), 16, rhs_u, [self.r1_b], nsub, groups3, epi_w2)
        self.wq_done()

        if not last:
            for c in range(16):
                S.dma("pool", self.XS[c * 128:(c + 1) * 128, cols], self.xt[:, c, 0:Tw], reads=[self.xt_b[c]])
        else:
            for c in range(16):
                sq, sq_b = self.rot("sq")
                S.op("act", [self.xt_b[c]], [sq_b], lambda e: e.activation(out=sq[:, 0:Tw], in_=self.xt[:, c, 0:Tw], func=AF.Square))
                for s in range(nsub):
                    bk = 6 + s
                    S.op("pe", [sq_b, self.ones_b], [psb[bk]],
                         lambda e: e.matmul(ps[bk][:, :], self.ones[:, :], sq[:, s * 512:(s + 1) * 512], start=(c == 0), stop=(c == 15)))
            self.rstd_from_stats([6, 7], nsub, D, self.rstd, self.rstd_b)
            o0 = tok0 - LAT0
            for c in range(16):
                tf, tf_b = self.rot("tmpf")
                S.op("dve", [self.xt_b[c], self.rstd_b, self.vec_b], [tf_b],
                     lambda e: e.scalar_tensor_tensor(out=tf[:, 0:Tw], in0=self.xt[:, c, 0:Tw],
                                                      scalar=self.vec[:, V_FINAL + c:V_FINAL + c + 1],
                                                      in1=self.rstd[:, 0:Tw], op0=ALU.mult, op1=ALU.mult))
                S.dma("pool", self.yout[c * 128:(c + 1) * 128, o0:o0 + Tw], tf[:, 0:Tw], reads=[tf_b])


def fm(v, nch):
    return np.ascontiguousarray(np.asarray(v, np.float32).reshape(nch, 128).T)


def rope_tables():
    rows = SEQ // 64
    row, col = np.meshgrid(np.arange(rows), np.arange(64), indexing="ij")
    row = row.reshape(-1).astype(np.float32)
    col = col.reshape(-1).astype(np.float32)
    freqs = (np.float32(10000.0) ** (-np.arange(0, 32, 2, dtype=np.float32) / np.float32(32))).astype(np.float32)
    ang_r = row[:, None] * freqs
    ang_c = col[:, None] * freqs
    ang = np.concatenate([ang_r, ang_r, ang_c, ang_c], axis=-1).astype(np.float32)
    cos = np.cos(ang).astype(np.float32).T
    sin = np.sin(ang).astype(np.float32).T
    cs = np.zeros((128, 2 * SEQ), np.float32)
    cs[0:64, 0:SEQ] = cos
    cs[64:128, 0:SEQ] = cos
    cs[0:64, SEQ:] = sin
    cs[64:128, SEQ:] = sin
    return cs


def rot_matrix_T():
    R = np.zeros((64, 64), np.float32)
    for i in range(64):
        blk, o = divmod(i, 32)
        if o < 16:
            R[i, blk * 32 + o + 16] = -1.0
        else:
            R[i, blk * 32 + o - 16] = 1.0
    R128 = np.zeros((128, 128), np.float32)
    R128[0:64, 0:64] = R
    R128[64:128, 64:128] = R
    return np.ascontiguousarray(R128.T)


def make_in_maps(inp, n_layers=DEPTH):
    x = np.asarray(inp["x"], np.float32)
    ctx = np.asarray(inp["ctx"], np.float32)
    c = np.asarray(inp["c"], np.float32)
    c_ctx = np.asarray(inp["c_ctx"], np.float32)
    vecs = np.zeros((128, NVEC), np.float32)
    for l in range(DEPTH):
        vb = l * V_PER_LAYER
        vecs[:, vb + V_N1:vb + V_N1 + 16] = fm(inp["norm1_g"][l], 16)
        vecs[:, vb + V_N2:vb + V_N2 + 16] = fm(inp["norm2_g"][l], 16)
        vecs[:, vb + V_BG:vb + V_BG + 32] = fm(inp["b_gate"][l], 32)
        vecs[:, vb + V_QA:vb + V_QA + 4] = fm(inp["q_a_norm"][l], 4)
        vecs[:, vb + V_KVA:vb + V_KVA + 2] = fm(inp["kv_a_norm"][l], 2)
        vecs[:, vb + V_SUB:vb + V_SUB + 1] = fm(inp["diff_subln"][l], 1)
        vecs[:, vb + V_BADA:vb + V_BADA + 96] = fm(inp["b_ada"][l], 96)
    vecs[:, V_FINAL:V_FINAL + 16] = fm(inp["final_norm_g"], 16)
    dlam = np.ascontiguousarray(np.broadcast_to(np.asarray(inp["diff_lambda"], np.float32).reshape(1, DEPTH * 256),
                                                (128, DEPTH * 256)))
    cs = rope_tables()
    rmat = rot_matrix_T()
    shared = {"vecs": vecs, "dlam": dlam, "cossin": cs, "rmat": rmat,
              "w_ada": np.ascontiguousarray(np.asarray(inp["w_ada"], np.float32))}
    for n in WSHAPES:
        shared[n] = np.ascontiguousarray(np.asarray(inp[n], np.float32))
    maps = []
    for core in range(NCORES):
        b0 = core * NB
        xin = np.empty((D, NTOK), np.float32)
        for i in range(NB):
            xin[:, i * CTX:(i + 1) * CTX] = ctx[b0 + i].T
            xin[:, LAT0 + i * SEQ:LAT0 + (i + 1) * SEQ] = x[b0 + i].T
        cv = np.stack([c[b0], c[b0 + 1], c_ctx], axis=-1)
        cvec = np.ascontiguousarray(cv.reshape(16, 128, 3).transpose(1, 0, 2).reshape(128, 48))
        m = dict(shared)
        m["xin"] = xin
        m["cvec"] = cvec
        maps.append(m)
    return maps


_CACHE = {}


def get_nc(n_layers=DEPTH, debug=None):
    key = (n_layers, repr(debug))
    if key not in _CACHE:
        b = Builder(n_layers, debug)
        b.build()
        _CACHE[key] = b
    return _CACHE[key]


def kernel(**inputs):
    b = get_nc()
    maps = make_in_maps(inputs)
    res = run_bass_kernel_spmd(b.nc, maps, core_ids=list(range(NCORES)))
    out = np.empty((NCORES * NB, SEQ, D), np.float32)
    for core in range(NCORES):
        y = res.results[core]["yout"]
        for i in range(NB):
            out[core * NB + i] = y[:, i * SEQ:(i + 1) * SEQ].T
    return out
```

```python
import math
from contextlib import ExitStack

import numpy as np
import concourse.bass as bass
import concourse.mybir as mybir
from concourse.bass_utils import run_bass_kernel_spmd

F32 = mybir.dt.float32
BF16 = mybir.dt.bfloat16
AF = mybir.ActivationFunctionType
ALU = mybir.AluOpType
AX = mybir.AxisListType

NCORES = 8
D = 2048
DEPTH = 4
SEQ = 2048
CTX = 256
NB = 2
NTOK = NB * (SEQ + CTX)
LAT0 = NB * CTX
EPS = 1e-6
DIFF_SCALE = 64 ** -0.5
MLA_SCALE = 192 ** -0.5
DFF = 8192
NH = 8
WB = 256
ADA_OVERLAP = True

TILES = [(0, 512, 2, True)] + [(LAT0 + i * 1024, 1024, i // 2, False) for i in range(4)]


class Buf:
    __slots__ = ("name", "w", "r")

    def __init__(self, sched, name):
        self.name = name
        self.w = {}
        self.r = {}
        sched.bufs.append(self)


class Sched:
    def __init__(self, nc, es, n_sp=44, n_pool=44):
        self.nc = nc
        self.engs = {"pe": nc.tensor, "act": nc.scalar, "dve": nc.vector, "pool": nc.gpsimd,
                     "sp": nc.sync}
        self.sem = {}
        self.cnt = {}
        for n in ("pe", "act", "dve", "pool"):
            self.sem[n] = es.enter_context(nc.semaphore("s_" + n))
            self.cnt[n] = 0
        self.dsem = []
        self.dcnt = []
        self.qsems = {"sp": [], "pool": []}
        self.qnext = {"sp": 0, "pool": 0}
        for q, n in (("sp", n_sp), ("pool", n_pool)):
            for i in range(n):
                idx = len(self.dsem)
                self.dsem.append(es.enter_context(nc.semaphore(f"d_{q}{i}")))
                self.dcnt.append(0)
                self.qsems[q].append(idx)
        self.seen = {e: {} for e in self.engs}
        self.bufs = []
        self.ninstr = 0

    def buf(self, name):
        return Buf(self, name)

    def _wait(self, eng, clock, val):
        if val <= 0:
            return
        seen = self.seen[eng]
        if seen.get(clock, 0) >= val:
            return
        sem = self.sem[clock] if isinstance(clock, str) else self.dsem[clock]
        self.engs[eng].wait_ge(sem, val)
        seen[clock] = val
        self.ninstr += 1

    def _deps(self, eng, reads, writes, is_dma):
        need = {}
        for b in reads:
            for k, v in b.w.items():
                if need.get(k, 0) < v:
                    need[k] = v
        for b in writes:
            for k, v in b.w.items():
                if need.get(k, 0) < v:
                    need[k] = v
            for k, v in b.r.items():
                if need.get(k, 0) < v:
                    need[k] = v
        for k, v in need.items():
            if k == "pe" and eng == "pe":
                continue
            self._wait(eng, k, v)

    def op(self, eng, reads, writes, fn):
        self._deps(eng, reads, writes, False)
        ins = fn(self.engs[eng])
        self.cnt[eng] += 1
        v = self.cnt[eng]
        ins.then_inc(self.sem[eng], 1)
        for b in reads:
            b.r[eng] = v
        for b in writes:
            b.w = {eng: v}
            b.r = {}
        self.ninstr += 1
        return ins

    def dma(self, q, out_ap, in_ap, reads=(), writes=(), join=False):
        sl = self.qsems[q]
        i = sl[self.qnext[q]]
        self.qnext[q] = (self.qnext[q] + 1) % len(sl)
        self._wait(q, i, 16 * self.dcnt[i])
        if join:
            self._deps(q, reads, [], True)
        else:
            self._deps(q, reads, writes, True)
        self.dcnt[i] += 1
        v = 16 * self.dcnt[i]
        self.engs[q].dma_start(out=out_ap, in_=in_ap).then_inc(self.dsem[i], 16)
        for b in reads:
            b.r[i] = v
        for b in writes:
            if join:
                b.w[i] = v
            else:
                b.w = {i: v}
                b.r = {}
        self.ninstr += 1

    def barrier(self):
        for eng in self.engs:
            for clock in ("pe", "act", "dve", "pool"):
                if clock != eng:
                    self._wait(eng, clock, self.cnt[clock])
            for i in range(len(self.dsem)):
                self._wait(eng, i, 16 * self.dcnt[i])
        for b in self.bufs:
            b.w = {}
            b.r = {}


class Stream:
    def __init__(self, name, src, k0, KC, chunks):
        self.name = name
        self.src = src
        self.k0 = k0
        self.KC = KC
        chunks = list(chunks)
        if len(chunks) % 2:
            chunks.append(chunks[-1])
        self.chunks = chunks
        self.nblk = len(chunks) // 2
        self.dram = None


def contiguous_chunks(c0, n):
    return [[(c0 + i * 128, 128)] for i in range(n)]


def make_streams():
    S = {}
    ch = []
    ch += contiguous_chunks(3072, 4)
    ch += contiguous_chunks(3584, 2)
    ch += contiguous_chunks(0, 8)
    ch += contiguous_chunks(1024, 8)
    ch += [[(3840, 64), (3840, 64)]]
    ch += contiguous_chunks(3904, 32)
    S["win_f"] = Stream("win_f", "w_in", 0, 16, ch)
    S["win_t"] = Stream("win_t", "w_in", 0, 16, contiguous_chunks(2048, 8))
    ch = [[(h * 192, 128)] for h in range(8)]
    ch += [[((2 * i) * 192 + 128, 64), ((2 * i + 1) * 192 + 128, 64)] for i in range(4)]
    S["wuq_f"] = Stream("wuq_f", "w_uq", 0, 4, ch)
    S["wukv_f"] = Stream("wukv_f", "w_ukv", 0, 2, [[(h * 256, 128)] for h in range(8)])
    S["wukv_t"] = Stream("wukv_t", "w_ukv", 0, 2, [[(h * 256 + 128, 128)] for h in range(8)])
    S["wod_f"] = Stream("wod_f", "w_o_diff", 0, 8, contiguous_chunks(0, 16))
    S["wom_f"] = Stream("wom_f", "w_o_mla", 0, 8, contiguous_chunks(0, 16))
    S["wout_f"] = Stream("wout_f", "w_out", 0, 16, contiguous_chunks(0, 16))
    S["w1_f"] = Stream("w1_f", "w_mlp1", 0, 16, contiguous_chunks(0, 64))
    for q in range(4):
        S[f"w2_f{q}"] = Stream(f"w2_f{q}", "w_mlp2", q * 2048, 16, contiguous_chunks(0, 16))
    return S


WSHAPES = {
    "w_in": (2048, 8000), "w_uq": (512, 1536), "w_ukv": (256, 2048), "w_o_diff": (1024, 2048),
    "w_o_mla": (1024, 2048), "w_out": (2048, 2048), "w_mlp1": (2048, 8192), "w_mlp2": (8192, 2048),
}

V_N1, V_N2, V_BG, V_QA, V_KVA, V_SUB, V_BADA = 0, 16, 32, 64, 68, 70, 71
V_PER_LAYER = 71 + 96
V_FINAL = DEPTH * V_PER_LAYER
NVEC = V_FINAL + 16


class Builder:
    def __init__(self, n_layers=DEPTH, debug=None):
        self.n_layers = n_layers
        self.debug = debug or {}
        self.nc = nc = bass.Bass("TRN2", target_bir_lowering=False)
        self.streams = make_streams()
        dt = nc.dram_tensor
        self.xin = dt("xin", [D, NTOK], F32, kind="ExternalInput").ap()
        self.cvec = dt("cvec", [128, 16 * 3], F32, kind="ExternalInput").ap()
        self.vecs = dt("vecs", [128, NVEC], F32, kind="ExternalInput").ap()
        self.dlam = dt("dlam", [128, DEPTH * 256], F32, kind="ExternalInput").ap()
        self.cossin = dt("cossin", [128, 2 * SEQ], F32, kind="ExternalInput").ap()
        self.rmat = dt("rmat", [128, 128], F32, kind="ExternalInput").ap()
        self.w_ada = dt("w_ada", [DEPTH, D, 6 * D], F32, kind="ExternalInput").ap()
        self.wsrc = {}
        for n, (k, m) in WSHAPES.items():
            self.wsrc[n] = dt(n, [DEPTH, k, m], F32, kind="ExternalInput").ap()
        self.yout = dt("yout", [D, NB * SEQ], F32, kind="ExternalOutput").ap()
        def scr(name, shape, dtype=BF16):
            kind = "ExternalOutput" if name in self.debug.get("dump", ()) else "Internal"
            return dt(name, shape, dtype, kind=kind).ap()
        self.XS = scr("XS", [D, NTOK], F32)
        self.QD = scr("QD", [NH, 128, NTOK])
        self.KD = scr("KD", [NH, 128, NTOK])
        self.VD = scr("VD", [NTOK, 1024])
        self.QMN = scr("QMN", [NH, 128, NTOK])
        self.QMP = scr("QMP", [NH // 2, 128, NTOK])
        self.KMN = scr("KMN", [NH, 128, NTOK])
        self.KMP = scr("KMP", [128, NTOK])
        self.VM = scr("VM", [NTOK, 1024])
        self.G = scr("G", [32, 128, NTOK])
        self.YD = scr("YD", [NH, 128, NTOK])
        self.YM = scr("YM", [NH, 128, NTOK])
        self.MODD = scr("MODD", [128, 3 * 96], F32)
        for s in self.streams.values():
            s.dram = [scr(f"{s.name}_{l}", [s.nblk, 128, s.KC * WB]) for l in range(n_layers)]

    def build(self):
        nc = self.nc
        with ExitStack() as es:
            self.S = S = Sched(nc, es)
            self.ps = []
            self.psb = []
            for i in range(8):
                self.ps.append(es.enter_context(nc.psum_tensor(f"ps{i}", [128, 512], F32)))
                self.psb.append(S.buf(f"ps{i}"))
            self.consts(es)
            stop = self.debug.get("stop")
            for l in range(self.n_layers):
                self.l = l
                last = (l == DEPTH - 1)
                S.barrier()
                if l == 0:
                    with ExitStack() as sc:
                        self.phase_conv(sc, l, "A")
                        S.barrier()
                if stop == ("conv", l):
                    break
                if l == 0 or not ADA_OVERLAP:
                    with ExitStack() as sc:
                        self.phase_ada(sc, l)
                        S.barrier()
                self.use_modset(l)
                if stop == ("ada", l):
                    break
                with ExitStack() as sc:
                    self.alloc_A(sc)
                    for t in self.debug.get("A_tile_list", range(self.debug.get("A_tiles", len(TILES)))):
                        self.phase_A(l, t)
                    S.barrier()
                if stop == ("A", l):
                    break
                with ExitStack() as sc:
                    self.alloc_B(sc)
                    staging = self.conv_staging(sc)
                    gens = [self.conv_gen(staging, l, ["dve", "pool"], "C")]
                    if l + 1 < self.n_layers:
                        gens.append(self.conv_gen(staging, l + 1, ["dve", "pool"], "A"))
                    self.bg = self.chain(gens)
                    self.bg_ada = None
                    if ADA_OVERLAP and l + 1 < self.n_layers:
                        self.bg_ada = self.ada_gen(sc, l + 1, 2)
                    for b in range(NB):
                        self.phase_B(l, b, last)
                    self.bg_step(100000)
                    self.ada_step(100000)
                    S.barrier()
                if stop == ("B", l):
                    break
                with ExitStack() as sc:
                    self.alloc_C(sc)
                    for t in range(len(TILES)):
                        if last and TILES[t][3]:
                            continue
                        self.phase_C(l, t, last)
                    S.barrier()
            S.barrier()
        return nc

    def T(self, es, name, shape, dtype):
        self.uid = getattr(self, "uid", 0) + 1
        name = f"{name}_u{self.uid}"
        t = es.enter_context(self.nc.sbuf_tensor(name, shape, dtype))
        return t, self.S.buf(name)

    def use_modset(self, i):
        (self.mod, self.mod_b), (self.A1, self.A1_b), (self.A2, self.A2_b), (self.lamt, self.lamt_b) = self.modsets[i % 2]

    def consts(self, es):
        S = self.S
        self.ones, self.ones_b = self.T(es, "ones", [128, 128], BF16)
        self.rT, self.rT_b = self.T(es, "rT", [128, 128], BF16)
        self.vec, self.vec_b = self.T(es, "vec", [128, NVEC], F32)
        self.silc, self.silc_b = self.T(es, "silc", [128, 48], F32)
        self.modsets = []
        for i in range(2):
            self.modsets.append((self.T(es, f"mod{i}", [128, 3, 96], F32), self.T(es, f"A1{i}", [128, 3, 16], F32),
                                 self.T(es, f"A2{i}", [128, 3, 16], F32), self.T(es, f"lamt{i}", [128, 8], F32)))
        self.use_modset(0)
        self.epst, self.epst_b = self.T(es, "epst", [128, 1], F32)
        with ExitStack() as sc:
            tmp, tmp_b = self.T(sc, "ctmp", [128, 128], F32)
            S.op("dve", [], [self.ones_b], lambda e: e.memset(self.ones[:], 1.0))
            S.op("dve", [], [self.epst_b], lambda e: e.memset(self.epst[:], EPS))
            S.dma("sp", tmp[:], self.rmat[:, :], writes=[tmp_b])
            S.dma("sp", self.vec[:], self.vecs[:, :], writes=[self.vec_b])
            S.dma("sp", self.silc[:], self.cvec[:, :], writes=[self.silc_b])
            S.op("act", [tmp_b], [self.rT_b], lambda e: e.activation(out=self.rT[:], in_=tmp[:], func=AF.Copy))
            S.op("act", [self.silc_b], [self.silc_b],
                 lambda e: e.activation(out=self.silc[:], in_=self.silc[:], func=AF.Silu))
            S.barrier()

    A_SRCS = ("w_in", "w_uq", "w_ukv")

    def conv_staging(self, sc):
        st32 = [self.T(sc, f"cv32_{i}", [128, 2048], F32) for i in range(3)]
        st16 = [self.T(sc, f"cv16_{i}", [128, 2048], BF16) for i in range(3)]
        return st32, st16

    def phase_conv(self, sc, l, group):
        for _ in self.conv_gen(self.conv_staging(sc), l, ["act", "dve", "pool"], group):
            pass

    def conv_gen(self, staging, l, engs, group):
        S = self.S
        NST = 3
        PIECE = 2048
        st32, st16 = staging
        it = 0
        by_src = {}
        for s in self.streams.values():
            g = "A" if s.src in self.A_SRCS else "C"
            if g not in group:
                continue
            by_src.setdefault((s.src, s.k0, s.KC), []).append(s)
        for (src, k0, KC), slist in by_src.items():
            K, N = WSHAPES[src]
            segs = []
            for s in slist:
                for ci, chunk in enumerate(s.chunks):
                    off = (ci % 2) * 128
                    for (c0, w) in chunk:
                        segs.append((c0, w, s, ci // 2, off))
                        off += w
            segs.sort(key=lambda x: (x[0], x[2].name, x[3], x[4]))
            pieces = []
            pstart, pend = None, None
            for x in segs:
                if pstart is None:
                    pstart, pend = x[0], x[0] + x[1]
                elif x[0] + x[1] - pstart <= PIECE:
                    pend = max(pend, x[0] + x[1])
                else:
                    pieces.append((pstart, pend - pstart))
                    pstart, pend = x[0], x[0] + x[1]
            pieces.append((pstart, pend - pstart))
            W = self.wsrc[src]
            for kc in range(KC):
                r0 = k0 + kc * 128
                for (pc0, pw) in pieces:
                    inside = [x for x in segs if x[0] >= pc0 and x[0] + x[1] <= pc0 + pw]
                    straddle = [x for x in segs if not (x[0] + x[1] <= pc0 or x[0] >= pc0 + pw)
                                and x not in inside]
                    assert not straddle, (src, pc0, pw, straddle[:2])
                    if not inside:
                        continue
                    (t32, b32), (t16, b16) = st32[it % NST], st16[it % NST]
                    eng = engs[it % len(engs)]
                    it += 1
                    S.dma("sp", t32[:, 0:pw], W[l, r0:r0 + 128, pc0:pc0 + pw], writes=[b32])
                    if eng == "act":
                        S.op("act", [b32], [b16],
                             lambda e: e.activation(out=t16[:, 0:pw], in_=t32[:, 0:pw], func=AF.Copy))
                    else:
                        S.op(eng, [b32], [b16], lambda e: e.tensor_copy(t16[:, 0:pw], t32[:, 0:pw]))
                    i = 0
                    while i < len(inside):
                        c0, w, s, blk, off = inside[i]
                        run = None
                        if off == 0 and w == 128 and i + 1 < len(inside):
                            j = i
                            nb = 0
                            cc = c0
                            bb = blk
                            while (j + 1 < len(inside)
                                   and inside[j][2] is s and inside[j + 1][2] is s
                                   and inside[j][3] == bb and inside[j + 1][3] == bb
                                   and inside[j][4] == 0 and inside[j + 1][4] == 128
                                   and inside[j][1] == 128 and inside[j + 1][1] == 128
                                   and inside[j][0] == cc and inside[j + 1][0] == cc + 128):
                                nb += 1
                                j += 2
                                cc += 256
                                bb += 1
                            if nb >= 1:
                                run = (nb, j)
                        if run is not None:
                            nb, j = run
                            dst = s.dram[l][blk:blk + nb, :, kc * WB:(kc + 1) * WB].rearrange("b p c -> p b c")
                            srcap = t16[:, c0 - pc0:c0 - pc0 + nb * WB].rearrange("p (b c) -> p b c", b=nb)
                            S.dma("pool", dst, srcap, reads=[b16])
                            i = j
                        else:
                            dst = s.dram[l][blk, :, kc * WB + off:kc * WB + off + w]
                            S.dma("pool", dst, t16[:, c0 - pc0:c0 - pc0 + w], reads=[b16])
                            i += 1
                    yield

    def phase_ada(self, sc, l):
        self.use_modset(l)
        for _ in self.ada_gen(sc, l, 0):
            pass

    def ada_gen(self, sc, l, bank):
        S = self.S
        NBLK = 48
        wb = [self.T(sc, f"adaw{i}", [128, 16, 256], F32) for i in range(2)]
        adat, adat_b = self.T(sc, "adat", [128, 288], F32)
        psum, psum_b = self.ps[bank], self.psb[bank]
        (mod, mod_b), (A1, A1_b), (A2, A2_b), (lamt, lamt_b) = self.modsets[l % 2]
        wsrc = self.w_ada[l].rearrange("(kc p) n -> p kc n", p=128)
        for nb in range(NBLK):
            t, b = wb[nb % 2]
            for k4 in range(2):
                S.dma("sp", t[:, k4 * 8:(k4 + 1) * 8, :], wsrc[:, k4 * 8:(k4 + 1) * 8, nb * 256:(nb + 1) * 256],
                      writes=[b], join=(k4 > 0))
            for m in range(2):
                ch = nb * 2 + m
                for kc in range(16):
                    S.op("pe", [b, self.silc_b], [psum_b],
                         lambda e: e.matmul(psum[:, m * 3:m * 3 + 3], t[:, kc, m * 128:(m + 1) * 128],
                                            self.silc[:, kc * 3:kc * 3 + 3], start=(kc == 0), stop=(kc == 15)))
            S.op("act", [psum_b], [adat_b],
                 lambda e: e.activation(out=adat[:, nb * 6:nb * 6 + 6], in_=psum[:, 0:6], func=AF.Copy))
            yield
        vb = l * V_PER_LAYER
        psv = adat[:, :].rearrange("p (c j) -> p c j", j=3)
        psum_b = adat_b
        self_mod, self_mod_b, self_A1, self_A1_b, self_A2, self_A2_b, self_lamt, self_lamt_b = mod, mod_b, A1, A1_b, A2, A2_b, lamt, lamt_b
        for j in range(3):
            S.op("dve", [psum_b, self.vec_b], [self_mod_b],
                 lambda e: e.tensor_tensor(out=self_mod[:, j, :], in0=psv[:, :, j],
                                           in1=self.vec[:, vb + V_BADA:vb + V_BADA + 96], op=ALU.add))
        for j in range(3):
            S.op("dve", [self_mod_b, self.vec_b], [self_A1_b],
                 lambda e: e.scalar_tensor_tensor(out=self_A1[:, j, :], in0=self_mod[:, j, 16:32], scalar=1.0,
                                                  in1=self.vec[:, vb + V_N1:vb + V_N1 + 16],
                                                  op0=ALU.add, op1=ALU.mult))
            S.op("dve", [self_mod_b, self.vec_b], [self_A2_b],
                 lambda e: e.scalar_tensor_tensor(out=self_A2[:, j, :], in0=self_mod[:, j, 64:80], scalar=1.0,
                                                  in1=self.vec[:, vb + V_N2:vb + V_N2 + 16],
                                                  op0=ALU.add, op1=ALU.mult))
        lam_init = 0.8 - 0.6 * math.exp(-0.3 * l)
        dl, dl_b = self.T(sc, "dl", [128, 256], F32)
        pr, pr_b = self.T(sc, "dlp", [128, 128], F32)
        S.dma("sp", dl[:], self.dlam[:, l * 256:(l + 1) * 256], writes=[dl_b])
        S.op("dve", [dl_b], [pr_b], lambda e: e.tensor_tensor(out=pr[:, 0:64], in0=dl[:, 0:64], in1=dl[:, 64:128], op=ALU.mult))
        S.op("dve", [dl_b], [pr_b], lambda e: e.tensor_tensor(out=pr[:, 64:128], in0=dl[:, 128:192], in1=dl[:, 192:256], op=ALU.mult))
        S.op("dve", [pr_b], [self_lamt_b], lambda e: e.reduce_sum(out=self_lamt[:, 2:3], in_=pr[:, 0:64], axis=AX.X))
        S.op("dve", [pr_b, self_lamt_b], [self_lamt_b], lambda e: e.reduce_sum(out=self_lamt[:, 3:4], in_=pr[:, 64:128], axis=AX.X))
        S.op("act", [self_lamt_b], [self_lamt_b], lambda e: e.activation(out=self_lamt[:, 4:6], in_=self_lamt[:, 2:4], func=AF.Exp))
        S.op("dve", [self_lamt_b], [self_lamt_b], lambda e: e.tensor_tensor(out=self_lamt[:, 6:7], in0=self_lamt[:, 5:6], in1=self_lamt[:, 4:5], op=ALU.subtract))
        S.op("dve", [self_lamt_b], [self_lamt_b], lambda e: e.tensor_scalar(out=self_lamt[:, 0:1], in0=self_lamt[:, 6:7], scalar1=-lam_init, scalar2=1.0, op0=ALU.add, op1=ALU.mult))
        S.op("dve", [self.vec_b, self_lamt_b], [self_lamt_b],
             lambda e: e.tensor_scalar(out=self_lamt[:, 1:2], in0=self.vec[:, vb + V_SUB:vb + V_SUB + 1],
                                       scalar1=(1.0 - lam_init), scalar2=1.0, op0=ALU.mult, op1=ALU.mult))
        if "MODD" in self.debug.get("dump", ()):
            S.dma("pool", self.MODD[:, :], self_mod[:].rearrange("p j c -> p (j c)"), reads=[self_mod_b])
        yield

    def alloc_wslots(self, sc, n):
        self.wslots = [self.T(sc, f"wsl{i}", [128, 16 * WB], BF16) for i in range(n)]
        self.wplan = []
        self.wissued = 0
        self.wbase = 0

    def wq_plan(self, items):
        self.wplan = list(items)
        self.wissued = 0
        self.wcons = 0

    def wq_issue_to(self, upto):
        n = len(self.wslots)
        while self.wissued < min(upto, len(self.wplan)):
            i = self.wissued
            s, blk = self.wplan[i]
            t, b = self.wslots[(self.wbase + i) % n]
            self.S.dma("sp", t[:, 0:s.KC * WB], s.dram[self.l][blk, :, :], writes=[b])
            self.wissued += 1

    def wq_get(self, i, oldest=None):
        n = len(self.wslots)
        if oldest is None:
            oldest = i
        self.wq_issue_to(oldest + n)
        return self.wslots[(self.wbase + i) % n]

    def wq_done(self):
        self.wbase = (self.wbase + len(self.wplan)) % len(self.wslots)
        self.wplan = []

    def gemm_F(self, plan_base, stream, blks, nchunks, rhs_fn, rhs_bufs, nsub, bank_groups, epilogue):
        S = self.S
        KC = stream.KC
        pending = None
        gi = 0
        for bi in range(len(blks)):
            wt, wb_ = self.wq_get(plan_base + bi)
            for j in range(2):
                ci = bi * 2 + j
                if ci >= nchunks:
                    break
                banks = bank_groups[gi % len(bank_groups)]
                gi += 1
                for s in range(nsub):
                    bk = banks[s]
                    for kc in range(KC):
                        S.op("pe", [wb_] + rhs_bufs, [self.psb[bk]],
                             lambda e: e.matmul(self.ps[bk][:, :], wt[:, kc * WB + j * 128:kc * WB + (j + 1) * 128],
                                                rhs_fn(kc, s), start=(kc == 0), stop=(kc == KC - 1)))
                if pending is not None:
                    pending()
                pending = epilogue(ci, banks)
        if pending is not None:
            pending()

    def rstd_from_stats(self, banks, nsub, dim, rstd, rstd_b):
        S = self.S
        for s in range(nsub):
            bk = banks[s]
            S.op("act", [self.psb[bk], self.epst_b], [rstd_b],
                 lambda e: e.activation(out=rstd[:, s * 512:(s + 1) * 512], in_=self.ps[bk][:, :], func=AF.Sqrt,
                                        bias=self.epst[:, 0:1], scale=1.0 / dim))
        w = nsub * 512
        S.op("dve", [rstd_b], [rstd_b], lambda e: e.reciprocal(out=rstd[:, 0:w], in_=rstd[:, 0:w]))

    def alloc_A(self, sc):
        T = self.T
        self.alloc_wslots(sc, 6)
        self.xc = [T(sc, f"xc{i}", [128, 1024], F32) for i in range(3)]
        self.hT, self.hT_b = T(sc, "hT", [128, 16, 1024], BF16)
        self.cq32, self.cq32_b = T(sc, "cq32", [128, 4, 1024], F32)
        self.ckv32, self.ckv32_b = T(sc, "ckv32", [128, 2, 1024], F32)
        self.cqn, self.cqn_b = T(sc, "cqn", [128, 4, 1024], BF16)
        self.ckvn, self.ckvn_b = T(sc, "ckvn", [128, 2, 1024], BF16)
        self.rstd, self.rstd_b = T(sc, "rstd", [128, 1024], F32)
        self.tmpf = [T(sc, f"tmpf{i}", [128, 1024], F32) for i in range(3)]
        self.stage = [T(sc, f"stage{i}", [128, 1024], BF16) for i in range(4)]
        self.sq = [T(sc, f"sq{i}", [128, 1024], BF16) for i in range(2)]
        self.xb = [T(sc, f"xb{i}", [128, 1024], BF16) for i in range(2)]
        self.cs, self.cs_b = T(sc, "cs", [128, 2 * SEQ], F32)
        self.S.dma("sp", self.cs[:], self.cossin[:, :], writes=[self.cs_b])
        self.rr = {"xc": 0, "tmpf": 0, "stage": 0, "sq": 0, "xb": 0}

    def rot(self, name):
        lst = getattr(self, name)
        i = self.rr[name]
        self.rr[name] = (i + 1) % len(lst)
        return lst[i]

    def norm_modulate(self, load_chunk, nsub, A, Bsh, j, out_t, out_b, stat_banks):
        S = self.S
        Tw = nsub * 512
        for c in range(16):
            xt, xb_ = load_chunk(c)
            sq, sq_b = self.rot("sq")
            S.op("act", [xb_], [sq_b], lambda e: e.activation(out=sq[:, 0:Tw], in_=xt[:, 0:Tw], func=AF.Square))
            for s in range(nsub):
                bk = stat_banks[s]
                S.op("pe", [sq_b, self.ones_b], [self.psb[bk]],
                     lambda e: e.matmul(self.ps[bk][:, :], self.ones[:, :], sq[:, s * 512:(s + 1) * 512],
                                        start=(c == 0), stop=(c == 15)))
        self.rstd_from_stats(stat_banks, nsub, D, self.rstd, self.rstd_b)
        for c in range(16):
            xt, xb_ = load_chunk(c)
            tf, tf_b = self.rot("tmpf")
            S.op("dve", [xb_, self.rstd_b, A[1]], [tf_b],
                 lambda e: e.scalar_tensor_tensor(out=tf[:, 0:Tw], in0=xt[:, 0:Tw], scalar=A[0][:, j, c:c + 1],
                                                  in1=self.rstd[:, 0:Tw], op0=ALU.mult, op1=ALU.mult))
            S.op("act", [tf_b, self.mod_b], [out_b],
                 lambda e: e.activation(out=out_t[:, c, 0:Tw], in_=tf[:, 0:Tw], func=AF.Identity,
                                        bias=self.mod[:, j, Bsh + c:Bsh + c + 1], scale=1.0))

    def store_chunk(self, dst_ap, src_t, src_b, Tw, rows=None):
        if rows is None:
            self.S.dma("pool", dst_ap, src_t[:, 0:Tw], reads=[src_b])
        else:
            r0, r1 = rows
            self.S.dma("pool", dst_ap, src_t[r0:r1, 0:Tw], reads=[src_b])

    def epi_copy_store(self, banks, nsub, dsts):
        S = self.S
        st, st_b = self.rot("stage")
        for s in range(nsub):
            bk = banks[s]
            S.op("act", [self.psb[bk]], [st_b],
                 lambda e: e.activation(out=st[:, s * 512:(s + 1) * 512], in_=self.ps[bk][:, :], func=AF.Copy))
        for dst, rows in dsts:
            self.store_chunk(dst, st, st_b, nsub * 512, rows)

    def epi_rope_store(self, banks, nsub, pos0, dsts):
        S = self.S
        xb, xb_b = self.rot("xb")
        for s in range(nsub):
            bk = banks[s]
            S.op("act", [self.psb[bk]], [xb_b],
                 lambda e: e.activation(out=xb[:, s * 512:(s + 1) * 512], in_=self.ps[bk][:, :], func=AF.Copy))

        def post():
            st, st_b = self.rot("stage")
            for s in range(nsub):
                bk = banks[s]
                rb = 6 + s
                S.op("pe", [xb_b, self.rT_b], [self.psb[rb]],
                     lambda e: e.matmul(self.ps[rb][:, :], self.rT[:, :], xb[:, s * 512:(s + 1) * 512],
                                        start=True, stop=True))
                if self.debug.get("rope_pe_only"):
                    continue
                t1, t1_b = self.rot("tmpf")
                t2, t2_b = self.rot("tmpf")
                p = pos0 + s * 512
                if not self.debug.get("rope_psum"):
                    S.op("act", [self.psb[bk]], [t1_b],
                         lambda e: e.activation(out=t1[:, 0:512], in_=self.ps[bk][:, :], func=AF.Copy))
                    S.op("act", [self.psb[rb]], [t2_b],
                         lambda e: e.activation(out=t2[:, 0:512], in_=self.ps[rb][:, :], func=AF.Copy))
                    S.op("dve", [t1_b, self.cs_b], [t1_b],
                         lambda e: e.tensor_tensor(out=t1[:, 0:512], in0=t1[:, 0:512], in1=self.cs[:, p:p + 512], op=ALU.mult))
                    S.op("dve", [t2_b, self.cs_b], [t2_b],
                         lambda e: e.tensor_tensor(out=t2[:, 0:512], in0=t2[:, 0:512],
                                                   in1=self.cs[:, SEQ + p:SEQ + p + 512], op=ALU.mult))
                else:
                    S.op("dve", [self.psb[bk], self.cs_b], [t1_b],
                         lambda e: e.tensor_tensor(out=t1[:, 0:512], in0=self.ps[bk][:, :], in1=self.cs[:, p:p + 512], op=ALU.mult))
                    S.op("dve", [self.psb[rb], self.cs_b], [t2_b],
                         lambda e: e.tensor_tensor(out=t2[:, 0:512], in0=self.ps[rb][:, :],
                                                   in1=self.cs[:, SEQ + p:SEQ + p + 512], op=ALU.mult))
                S.op("pool" if self.debug.get("pool_add") else "dve", [t1_b, t2_b], [st_b],
                     lambda e: e.tensor_tensor(out=st[:, s * 512:(s + 1) * 512], in0=t1[:, 0:512], in1=t2[:, 0:512], op=ALU.add))
            if self.debug.get("rope_pe_only") or self.debug.get("rope_no_store"):
                return
            for dst, rows in dsts:
                self.store_chunk(dst, st, st_b, nsub * 512, rows)
        if self.debug.get("no_defer"):
            post()
            return None
        return post

    def phase_A(self, l, t):
        S = self.S
        tok0, Tw, j, is_ctx = TILES[t]
        nsub = Tw // 512
        src = self.xin if l == 0 else self.XS
        vb = l * V_PER_LAYER
        pos0 = 0 if is_ctx else (tok0 - LAT0) % SEQ
        st = self.streams
        cols = slice(tok0, tok0 + Tw)

        def load_chunk(c):
            xt, xb_ = self.rot("xc")
            S.dma("sp", xt[:, 0:Tw], src[c * 128:(c + 1) * 128, tok0:tok0 + Tw], writes=[xb_])
            return xt, xb_

        plan = [(st["win_f"], b) for b in range(st["win_f"].nblk)]
        p_wint = len(plan)
        plan += [(st["win_t"], b) for b in range(4)]
        p_wuq = len(plan)
        plan += [(st["wuq_f"], b) for b in range(6)]
        p_wukv = len(plan)
        plan += [(st["wukv_f"], b) for b in range(4)]
        p_wukvt = len(plan)
        plan += [(st["wukv_t"], b) for b in range(4)]
        self.wq_plan(plan)
        self.wq_issue_to(len(self.wslots))

        self.norm_modulate(load_chunk, nsub, (self.A1, self.A1_b), 0, j, self.hT, self.hT_b, [6, 7])

        groups = [[0, 1], [2, 3], [4, 5]]
        steps = self.debug.get("A_steps", 99)
        if steps < 2:
            self.wq_done()
            return

        def rhs_h(kc, s):
            return self.hT[:, kc, s * 512:(s + 1) * 512]

        def epi_win(ci, banks):
            if ci < 6:
                if ci < 4:
                    dst_t, dst_b, cc, first, lastc = self.cq32, self.cq32_b, ci, ci == 0, ci == 3
                else:
                    dst_t, dst_b, cc, first, lastc = self.ckv32, self.ckv32_b, ci - 4, ci == 4, ci == 5
                sq, sq_b = self.rot("sq")
                for s in range(nsub):
                    bk = banks[s]
                    S.op("act", [self.psb[bk]], [dst_b],
                         lambda e: e.activation(out=dst_t[:, cc, s * 512:(s + 1) * 512], in_=self.ps[bk][:, :], func=AF.Copy))
                    S.op("act", [self.psb[bk]], [sq_b],
                         lambda e: e.activation(out=sq[:, s * 512:(s + 1) * 512], in_=self.ps[bk][:, :], func=AF.Square))

                def post():
                    for s in range(nsub):
                        rb = 6 + s
                        S.op("pe", [sq_b, self.ones_b], [self.psb[rb]],
                             lambda e: e.matmul(self.ps[rb][:, :], self.ones[:, :], sq[:, s * 512:(s + 1) * 512],
                                                start=first, stop=lastc))
                    if lastc:
                        if ci == 3:
                            n_t, n_b, nn, dim, gv = self.cqn, self.cqn_b, 4, 512, V_QA
                        else:
                            n_t, n_b, nn, dim, gv = self.ckvn, self.ckvn_b, 2, 256, V_KVA
                        self.rstd_from_stats([6, 7], nsub, dim, self.rstd, self.rstd_b)
                        for c2 in range(nn):
                            S.op("dve", [dst_b, self.rstd_b, self.vec_b], [n_b],
                                 lambda e: e.scalar_tensor_tensor(out=n_t[:, c2, 0:Tw], in0=dst_t[:, c2, 0:Tw],
                                                                  scalar=self.vec[:, vb + gv + c2:vb + gv + c2 + 1],
                                                                  in1=self.rstd[:, 0:Tw], op0=ALU.mult, op1=ALU.mult))
                return post
            if ci < 22:
                h = (ci - 6) % 8
                dstT = self.QD if ci < 14 else self.KD
                dsts = [(dstT[h, :, cols], None)]
                if is_ctx:
                    self.epi_copy_store(banks, nsub, dsts)
                    return None
                return self.epi_rope_store(banks, nsub, pos0, dsts)
            if ci == 22:
                dsts = [(self.KMP[:, cols], None)]
                if is_ctx:
                    self.epi_copy_store(banks, nsub, dsts)
                    return None
                return self.epi_rope_store(banks, nsub, pos0, dsts)
            g = ci - 23
            sg, sg_b = self.rot("stage")
            for s in range(nsub):
                bk = banks[s]
                S.op("act", [self.psb[bk], self.vec_b], [sg_b],
                     lambda e: e.activation(out=sg[:, s * 512:(s + 1) * 512], in_=self.ps[bk][:, :], func=AF.Sigmoid,
                                            bias=self.vec[:, vb + V_BG + g:vb + V_BG + g + 1], scale=1.0))
            self.store_chunk(self.G[g, :, cols], sg, sg_b, Tw)
            return None

        skip = self.debug.get("A_skip", ())
        self.gemm_F(0, st["win_f"], list(range(st["win_f"].nblk)), 55 if "win" not in skip else 6, rhs_h, [self.hT_b], nsub, groups, epi_win)

        if steps < 3:
            self.wq_done()
            return
        if "vd" not in skip:
            self.gemm_T(p_wint, st["win_t"], self.hT, self.hT_b, 16, Tw, self.VD, tok0)
        if steps < 4:
            self.wq_done()
            return

        def rhs_cq(kc, s):
            return self.cqn[:, kc, s * 512:(s + 1) * 512]

        def epi_wuq(ci, banks):
            if ci < 8:
                self.epi_copy_store(banks, nsub, [(self.QMN[ci, :, cols], None)])
                return None
            dsts = [((self.QD if self.debug.get("qmp_to_qd") else self.QMP)[ci - 8, :, cols], None)]
            if is_ctx:
                self.epi_copy_store(banks, nsub, dsts)
                return None
            return self.epi_rope_store(banks, nsub, pos0, dsts)

        if "wuq" not in skip:
            self.gemm_F(p_wuq, st["wuq_f"], list(range(6)), self.debug.get("wuq_n", 12), rhs_cq, [self.cqn_b], nsub, groups, epi_wuq)

        if steps < 5:
            self.wq_done()
            return
        def rhs_ckv(kc, s):
            return self.ckvn[:, kc, s * 512:(s + 1) * 512]

        def epi_wukv(ci, banks):
            self.epi_copy_store(banks, nsub, [(self.KMN[ci, :, cols], None)])
            return None

        if "wukv" not in skip:
            self.gemm_F(p_wukv, st["wukv_f"], list(range(4)), 8, rhs_ckv, [self.ckvn_b], nsub, groups, epi_wukv)
        if steps < 6:
            self.wq_done()
            return
        self.gemm_T(p_wukvt, st["wukv_t"], self.ckvn, self.ckvn_b, 2, Tw, self.VD if self.debug.get("vm_to_vd") else self.VM, tok0)
        self.wq_done()

    def gemm_T(self, plan_base, stream, act_t, act_b, KC, Tw, dstV, tok0):
        S = self.S
        slots = [self.wq_get(plan_base + b, oldest=plan_base) for b in range(4)]
        pairs = [[0, 1], [2, 3], [4, 5]]
        for tb in range(Tw // 128):
            banks = pairs[tb % 3]
            for cb in range(4):
                wt, wb_ = slots[cb]
                bk = banks[cb // 2]
                half = (cb % 2) * 256
                for kc in range(KC):
                    S.op("pe", [wb_, act_b], [self.psb[bk]],
                         lambda e: e.matmul(self.ps[bk][:, half:half + 256], act_t[:, kc, tb * 128:(tb + 1) * 128],
                                            wt[:, kc * WB:(kc + 1) * WB], start=(kc == 0), stop=(kc == KC - 1)))
            sv, sv_b = self.rot("stage")
            for hb in range(2):
                bk = banks[hb]
                S.op("act", [self.psb[bk]], [sv_b],
                     lambda e: e.activation(out=sv[:, hb * 512:(hb + 1) * 512], in_=self.ps[bk][:, :], func=AF.Copy))
            r0 = tok0 + tb * 128
            S.dma("pool", dstV[r0:r0 + 128, :], sv[:, 0:1024], reads=[sv_b])

    def alloc_B(self, sc):
        T = self.T
        S = self.S
        self.B_d = []
        self.B_m = []
        for i in range(2):
            d = {}
            d["K"] = T(sc, f"bK{i}", [128, 2304], BF16)
            d["V"] = T(sc, f"bV{i}", [128, 18, 128], BF16)
            d["Q1"] = T(sc, f"bQ1{i}", [128, 2304], BF16)
            d["Q2"] = T(sc, f"bQ2{i}", [128, 2304], BF16)
            self.B_d.append(d)
            m = {}
            m["K"] = T(sc, f"mK{i}", [128, 2304], BF16)
            m["V"] = T(sc, f"mV{i}", [128, 18, 128], BF16)
            m["Q"] = T(sc, f"mQ{i}", [128, 2304], BF16)
            m["QP"] = T(sc, f"mQP{i}", [128, 2304], BF16)
            self.B_m.append(m)
        self.KP = [T(sc, f"mKP{i}", [128, 2304], BF16) for i in range(2)]
        self.E = [T(sc, f"E{i}", [128, 512], BF16) for i in range(6)]
        self.fin = [T(sc, f"fin{i}", [128, 512], F32) for i in range(8)]
        self.ys = [T(sc, f"ys{i}", [128, 512], BF16) for i in range(3)]
        self.sqb = [T(sc, f"sqb{i}", [128, 512], BF16) for i in range(2)]
        self.Esum = [T(sc, f"Es{i}", [128, 512], BF16) for i in range(4)]
        self.rr = {"E": 0, "fin": 0, "ys": 0, "sqb": 0, "Esum": 0}
        for i in range(2):
            t, b = self.B_d[i]["Q1"]
            S.op("dve", [], [b], lambda e: e.memset(t[64:128, :], 0.0))
            t, b = self.B_d[i]["Q2"]
            S.op("dve", [], [b], lambda e: e.memset(t[0:64, :], 0.0))
            t, b = self.B_m[i]["QP"]
            S.op("dve", [], [b], lambda e: e.memset(t[64:128, :], 0.0))
            t, b = self.KP[i]
            S.op("dve", [], [b], lambda e: e.memset(t[64:128, :], 0.0))
        self.bset = 0

    @staticmethod
    def chain(gens):
        for g in gens:
            for _ in g:
                yield

    def ada_step(self, n):
        if getattr(self, "bg_ada", None) is None:
            return
        for _ in range(n):
            try:
                next(self.bg_ada)
            except StopIteration:
                self.bg_ada = None
                return

    def bg_step(self, n):
        if self.bg is None:
            return
        for _ in range(n):
            try:
                next(self.bg)
            except StopIteration:
                self.bg = None
                return

    def tok_ranges(self, b):
        return [(0, b * CTX, CTX), (CTX, LAT0 + b * SEQ, SEQ)]

    def phase_B(self, l, b, last):
        S = self.S
        rng = self.tok_ranges(b)
        kp, kp_b = self.KP[b % 2]
        for ri, (c0, t0, n) in enumerate(rng):
            S.dma("sp", kp[0:64, c0:c0 + n], self.KMP[0:64, t0:t0 + n], writes=[kp_b], join=(ri > 0))
        qgroups = [(CTX + g * 512, 512, list(range(18))) for g in range(4)]
        if not last:
            qgroups.append((0, CTX, [0, 1]))
        for h in range(NH):
            d = self.B_d[self.bset % 2]
            m = self.B_m[self.bset % 2]
            self.bset += 1
            (K, K_b), (V, V_b), (Q1, Q1_b), (Q2, Q2_b) = d["K"], d["V"], d["Q1"], d["Q2"]
            for ri, (c0, t0, n) in enumerate(rng):
                jn = ri > 0
                S.dma("sp", K[:, c0:c0 + n], self.KD[h, :, t0:t0 + n], writes=[K_b], join=jn)
                S.dma("sp", Q1[0:64, c0:c0 + n], self.QD[h, 0:64, t0:t0 + n], writes=[Q1_b], join=jn)
                S.dma("sp", Q2[64:128, c0:c0 + n], self.QD[h, 64:128, t0:t0 + n], writes=[Q2_b], join=jn)
                for j0 in range(0, n // 128, 8):
                    nb = min(8, n // 128 - j0)
                    S.dma("sp", V[:, c0 // 128 + j0:c0 // 128 + j0 + nb, :],
                          self.VD[t0 + j0 * 128:t0 + (j0 + nb) * 128, h * 128:(h + 1) * 128].rearrange("(j p) c -> p j c", p=128),
                          writes=[V_b], join=(jn or j0 > 0))
            (MK, MK_b), (MV, MV_b), (MQ, MQ_b), (MQP, MQP_b) = m["K"], m["V"], m["Q"], m["QP"]
            for ri, (c0, t0, n) in enumerate(rng):
                jn = ri > 0
                S.dma("sp", MK[:, c0:c0 + n], self.KMN[h, :, t0:t0 + n], writes=[MK_b], join=jn)
                S.dma("sp", MQ[:, c0:c0 + n], self.QMN[h, :, t0:t0 + n], writes=[MQ_b], join=jn)
                S.dma("sp", MQP[0:64, c0:c0 + n], self.QMP[h // 2, (h % 2) * 64:(h % 2) * 64 + 64, t0:t0 + n], writes=[MQP_b], join=jn)
                for j0 in range(0, n // 128, 8):
                    nb = min(8, n // 128 - j0)
                    S.dma("sp", MV[:, c0 // 128 + j0:c0 // 128 + j0 + nb, :],
                          self.VM[t0 + j0 * 128:t0 + (j0 + nb) * 128, h * 128:(h + 1) * 128].rearrange("(j p) c -> p j c", p=128),
                          writes=[MV_b], join=(jn or j0 > 0))
            for (q0, qw, kbs) in qgroups:
                self.attn_diff(b, h, q0, qw, kbs, K, K_b, V, V_b, Q1, Q1_b, Q2, Q2_b)
                self.bg_step(2)
            for gi, (q0, qw, kbs) in enumerate(qgroups):
                self.attn_mla(b, h, q0, qw, kbs, MK, MK_b, kp, kp_b, MV, MV_b, MQ, MQ_b, MQP, MQP_b, gi % 2)
                self.bg_step(2)
                self.ada_step(1)

    def q_dst(self, dstT, h, b, q0, qw):
        if q0 < CTX:
            t0 = b * CTX + q0
        else:
            t0 = LAT0 + b * SEQ + (q0 - CTX)
        return dstT[h, :, t0:t0 + qw]

    def attn_diff(self, b, h, q0, qw, kbs, K, K_b, V, V_b, Q1, Q1_b, Q2, Q2_b):
        S = self.S
        ps, psb = self.ps, self.psb
        SA, SB = [0, 1], [2, 3]
        A1, D1, A2, D2 = 4, 5, 6, 7
        n = len(kbs)
        Es = {}

        def Smm(i):
            kb = kbs[i]
            S.op("pe", [K_b, Q1_b], [psb[SA[i % 2]]],
                 lambda e: e.matmul(ps[SA[i % 2]][:, 0:qw], K[:, kb * 128:(kb + 1) * 128], Q1[:, q0:q0 + qw], start=True, stop=True))
            S.op("pe", [K_b, Q2_b], [psb[SB[i % 2]]],
                 lambda e: e.matmul(ps[SB[i % 2]][:, 0:qw], K[:, kb * 128:(kb + 1) * 128], Q2[:, q0:q0 + qw], start=True, stop=True))

        def Xp(i):
            e1, e1_b = self.rot("E")
            e2, e2_b = self.rot("E")
            S.op("act", [psb[SA[i % 2]]], [e1_b],
                 lambda e: e.activation(out=e1[:, 0:qw], in_=ps[SA[i % 2]][:, 0:qw], func=AF.Exp, scale=DIFF_SCALE))
            S.op("act", [psb[SB[i % 2]]], [e2_b],
                 lambda e: e.activation(out=e2[:, 0:qw], in_=ps[SB[i % 2]][:, 0:qw], func=AF.Exp, scale=DIFF_SCALE))
            Es[i] = (e1, e1_b, e2, e2_b)

        prev = {}

        def AV(i):
            kb = kbs[i]
            e1, e1_b, e2, e2_b = Es.pop(i)
            st, sp = (i == 0), (i == n - 1)
            S.op("pe", [V_b, e1_b], [psb[A1]], lambda e: e.matmul(ps[A1][:, 0:qw], V[:, kb, :], e1[:, 0:qw], start=st, stop=sp))
            S.op("pe", [V_b, e2_b], [psb[A2]], lambda e: e.matmul(ps[A2][:, 0:qw], V[:, kb, :], e2[:, 0:qw], start=st, stop=sp))
            if i % 2 == 0:
                prev[0] = (e1, e1_b, e2, e2_b)
                return
            p1, p1_b, p2, p2_b = prev.pop(0)
            s1, s1_b = self.rot("Esum")
            s2, s2_b = self.rot("Esum")
            S.op("dve", [p1_b, e1_b], [s1_b], lambda e: e.tensor_tensor(out=s1[:, 0:qw], in0=p1[:, 0:qw], in1=e1[:, 0:qw], op=ALU.add))
            S.op("dve", [p2_b, e2_b], [s2_b], lambda e: e.tensor_tensor(out=s2[:, 0:qw], in0=p2[:, 0:qw], in1=e2[:, 0:qw], op=ALU.add))
            S.op("pe", [self.ones_b, s1_b], [psb[D1]], lambda e: e.matmul(ps[D1][:, 0:qw], self.ones[:, :], s1[:, 0:qw], start=(i == 1), stop=sp))
            S.op("pe", [self.ones_b, s2_b], [psb[D2]], lambda e: e.matmul(ps[D2][:, 0:qw], self.ones[:, :], s2[:, 0:qw], start=(i == 1), stop=sp))

        Smm(0)
        for i in range(n):
            if i + 1 < n:
                Smm(i + 1)
            Xp(i)
            AV(i)
        f1, f1_b = self.rot("fin")
        f2, f2_b = self.rot("fin")
        f3, f3_b = self.rot("fin")
        f4, f4_b = self.rot("fin")
        S.op("act", [psb[D1]], [f1_b], lambda e: e.activation(out=f1[:, 0:qw], in_=ps[D1][:, 0:qw], func=AF.Copy))
        S.op("act", [psb[A1]], [f3_b], lambda e: e.activation(out=f3[:, 0:qw], in_=ps[A1][:, 0:qw], func=AF.Copy))
        S.op("act", [psb[D2]], [f2_b], lambda e: e.activation(out=f2[:, 0:qw], in_=ps[D2][:, 0:qw], func=AF.Copy))
        S.op("act", [psb[A2]], [f4_b], lambda e: e.activation(out=f4[:, 0:qw], in_=ps[A2][:, 0:qw], func=AF.Copy))
        S.op("dve", [f1_b], [f1_b], lambda e: e.reciprocal(out=f1[:, 0:qw], in_=f1[:, 0:qw]))
        S.op("dve", [f3_b, f1_b], [f1_b], lambda e: e.tensor_tensor(out=f1[:, 0:qw], in0=f3[:, 0:qw], in1=f1[:, 0:qw], op=ALU.mult))
        S.op("dve", [f2_b], [f2_b], lambda e: e.reciprocal(out=f2[:, 0:qw], in_=f2[:, 0:qw]))
        S.op("dve", [f4_b, f2_b], [f2_b], lambda e: e.tensor_tensor(out=f2[:, 0:qw], in0=f4[:, 0:qw], in1=f2[:, 0:qw], op=ALU.mult))
        S.op("dve", [f1_b, f2_b, self.lamt_b], [f3_b],
             lambda e: e.scalar_tensor_tensor(out=f3[:, 0:qw], in0=f2[:, 0:qw], scalar=self.lamt[:, 0:1], in1=f1[:, 0:qw],
                                              op0=ALU.mult, op1=ALU.add))
        sq, sq_b = self.rot("sqb")
        S.op("dve", [f3_b], [sq_b], lambda e: e.tensor_tensor(out=sq[:, 0:qw], in0=f3[:, 0:qw], in1=f3[:, 0:qw], op=ALU.mult))
        S.op("pe", [sq_b, self.ones_b], [psb[D1]], lambda e: e.matmul(ps[D1][:, 0:qw], self.ones[:, :], sq[:, 0:qw], start=True, stop=True))
        S.op("act", [psb[D1], self.epst_b], [f1_b],
             lambda e: e.activation(out=f1[:, 0:qw], in_=ps[D1][:, 0:qw], func=AF.Ln, bias=self.epst[:, 0:1], scale=1.0 / 128))
        S.op("act", [f1_b], [f1_b], lambda e: e.activation(out=f1[:, 0:qw], in_=f1[:, 0:qw], func=AF.Exp, scale=-0.5))
        ys, ys_b = self.rot("ys")
        S.op("dve", [f3_b, f1_b, self.lamt_b], [ys_b],
             lambda e: e.scalar_tensor_tensor(out=ys[:, 0:qw], in0=f3[:, 0:qw], scalar=self.lamt[:, 1:2], in1=f1[:, 0:qw],
                                              op0=ALU.mult, op1=ALU.mult))
        S.dma("pool", self.q_dst(self.YD, h, b, q0, qw), ys[:, 0:qw], reads=[ys_b])

    def attn_mla(self, b, h, q0, qw, kbs, K, K_b, KP, KP_b, V, V_b, Q, Q_b, QP, QP_b, par):
        S = self.S
        ps, psb = self.ps, self.psb
        SA = [0, 1]
        A1, D1 = (4, 5) if par == 0 else (6, 7)
        n = len(kbs)
        Es = {}

        def Smm(i):
            kb = kbs[i]
            S.op("pe", [K_b, Q_b], [psb[SA[i % 2]]],
                 lambda e: e.matmul(ps[SA[i % 2]][:, 0:qw], K[:, kb * 128:(kb + 1) * 128], Q[:, q0:q0 + qw], start=True, stop=False))
            S.op("pe", [KP_b, QP_b], [psb[SA[i % 2]]],
                 lambda e: e.matmul(ps[SA[i % 2]][:, 0:qw], KP[:, kb * 128:(kb + 1) * 128], QP[:, q0:q0 + qw], start=False, stop=True))

        def Xp(i):
            e1, e1_b = self.rot("E")
            S.op("act", [psb[SA[i % 2]]], [e1_b],
                 lambda e: e.activation(out=e1[:, 0:qw], in_=ps[SA[i % 2]][:, 0:qw], func=AF.Exp, scale=MLA_SCALE))
            Es[i] = (e1, e1_b)

        prev = {}

        def AV(i):
            kb = kbs[i]
            e1, e1_b = Es.pop(i)
            st, sp = (i == 0), (i == n - 1)
            S.op("pe", [V_b, e1_b], [psb[A1]], lambda e: e.matmul(ps[A1][:, 0:qw], V[:, kb, :], e1[:, 0:qw], start=st, stop=sp))
            if i % 2 == 0:
                prev[0] = (e1, e1_b)
                return
            p1, p1_b = prev.pop(0)
            s1, s1_b = self.rot("Esum")
            S.op("dve", [p1_b, e1_b], [s1_b], lambda e: e.tensor_tensor(out=s1[:, 0:qw], in0=p1[:, 0:qw], in1=e1[:, 0:qw], op=ALU.add))
            S.op("pe", [self.ones_b, s1_b], [psb[D1]], lambda e: e.matmul(ps[D1][:, 0:qw], self.ones[:, :], s1[:, 0:qw], start=(i == 1), stop=sp))

        Smm(0)
        for i in range(n):
            if i + 1 < n:
                Smm(i + 1)
            Xp(i)
            AV(i)
        f1, f1_b = self.rot("fin")
        f2, f2_b = self.rot("fin")
        S.op("act", [psb[D1]], [f1_b], lambda e: e.activation(out=f1[:, 0:qw], in_=ps[D1][:, 0:qw], func=AF.Copy))
        S.op("act", [psb[A1]], [f2_b], lambda e: e.activation(out=f2[:, 0:qw], in_=ps[A1][:, 0:qw], func=AF.Copy))
        S.op("dve", [f1_b], [f1_b], lambda e: e.reciprocal(out=f1[:, 0:qw], in_=f1[:, 0:qw]))
        ys, ys_b = self.rot("ys")
        S.op("dve", [f2_b, f1_b], [ys_b], lambda e: e.tensor_tensor(out=ys[:, 0:qw], in0=f2[:, 0:qw], in1=f1[:, 0:qw], op=ALU.mult))
        S.dma("pool", self.q_dst(self.YM, h, b, q0, qw), ys[:, 0:qw], reads=[ys_b])

    def alloc_C(self, sc):
        T = self.T
        self.alloc_wslots(sc, 4)
        self.xt, _ = T(sc, "xt", [128, 16, 1024], F32)
        self.xt_b = [self.S.buf(f"xt{c}") for c in range(16)]
        self.mh, self.mh_b = T(sc, "mh", [128, 16, 1024], BF16)
        self.r1, self.r1_b = T(sc, "r1", [128, 16, 1024], BF16)
        self.rstd, self.rstd_b = T(sc, "rstdc", [128, 1024], F32)
        self.tmpf = [T(sc, f"tmpfc{i}", [128, 1024], F32) for i in range(3)]
        self.sq = [T(sc, f"sqc{i}", [128, 1024], BF16) for i in range(2)]
        self.gb = [T(sc, f"gb{i}", [128, 1024], BF16) for i in range(4)]
        self.rr = {"tmpf": 0, "sq": 0, "gb": 0}

    def phase_C(self, l, t, last):
        S = self.S
        tok0, Tw, j, is_ctx = TILES[t]
        nsub = Tw // 512
        src = self.xin if l == 0 else self.XS
        st = self.streams
        cols = slice(tok0, tok0 + Tw)
        ps, psb = self.ps, self.psb
        plan = []
        for bi in range(8):
            plan.append((st["wod_f"], bi))
            plan.append((st["wom_f"], bi))
        p_wout = len(plan)
        plan += [(st["wout_f"], bi) for bi in range(8)]
        p_mlp = len(plan)
        for q in range(4):
            plan += [(st["w1_f"], 8 * q + bi) for bi in range(8)]
            plan += [(st[f"w2_f{q}"], bi) for bi in range(8)]
        self.wq_plan(plan)
        self.wq_issue_to(len(self.wslots))
        for h in range(NH):
            S.dma("sp", self.r1[:, h, 0:Tw], self.YD[h, :, cols], writes=[self.r1_b], join=(h > 0))
        for h in range(NH):
            S.dma("sp", self.r1[:, 8 + h, 0:Tw], self.YM[h, :, cols], writes=[self.r1_b], join=True)
        for c in range(16):
            S.dma("sp", self.xt[:, c, 0:Tw], src[c * 128:(c + 1) * 128, cols], writes=[self.xt_b[c]])

        groups = [[0, 1, 2, 3], [4, 5, 6, 7]]
        gi = 0
        for bi in range(8):
            wd, wd_b = self.wq_get(2 * bi, oldest=2 * bi)
            wm, wm_b = self.wq_get(2 * bi + 1, oldest=2 * bi)
            for jj in range(2):
                c = bi * 2 + jj
                banks = groups[gi % 2]
                gi += 1
                gd, gd_b = self.rot("gb")
                gm, gm_b = self.rot("gb")
                S.dma("sp", gd[:, 0:Tw], self.G[c, :, cols], writes=[gd_b])
                S.dma("sp", gm[:, 0:Tw], self.G[16 + c, :, cols], writes=[gm_b])
                for s in range(nsub):
                    for (wt, wb_, off, bk) in ((wd, wd_b, 0, banks[s]), (wm, wm_b, 8, banks[2 + s])):
                        for kc in range(8):
                            S.op("pe", [wb_, self.r1_b], [psb[bk]],
                                 lambda e: e.matmul(ps[bk][:, :], wt[:, kc * WB + jj * 128:kc * WB + (jj + 1) * 128],
                                                    self.r1[:, off + kc, s * 512:(s + 1) * 512], start=(kc == 0), stop=(kc == 7)))
                for s in range(nsub):
                    t1, t1_b = self.rot("tmpf")
                    t2, t2_b = self.rot("tmpf")
                    sl = slice(s * 512, (s + 1) * 512)
                    S.op("act", [psb[banks[s]]], [t1_b],
                         lambda e: e.activation(out=t1[:, 0:512], in_=ps[banks[s]][:, :], func=AF.Copy))
                    S.op("act", [psb[banks[2 + s]]], [t2_b],
                         lambda e: e.activation(out=t2[:, 0:512], in_=ps[banks[2 + s]][:, :], func=AF.Copy))
                    S.op("dve", [t1_b, gd_b], [t1_b],
                         lambda e: e.tensor_tensor(out=t1[:, 0:512], in0=t1[:, 0:512], in1=gd[:, sl], op=ALU.mult))
                    S.op("dve", [t2_b, gm_b], [t2_b],
                         lambda e: e.tensor_tensor(out=t2[:, 0:512], in0=t2[:, 0:512], in1=gm[:, sl], op=ALU.mult))
                    S.op("dve", [t1_b, t2_b], [self.mh_b],
                         lambda e: e.tensor_tensor(out=self.mh[:, c, sl], in0=t1[:, 0:512], in1=t2[:, 0:512], op=ALU.add))

        groups3 = [[0, 1], [2, 3], [4, 5]]

        def rhs_m(kc, s):
            return self.mh[:, kc, s * 512:(s + 1) * 512]

        def epi_wout(ci, banks):
            for s in range(nsub):
                bk = banks[s]
                sl = slice(s * 512, (s + 1) * 512)
                tf, tf_b = self.rot("tmpf")
                S.op("act", [psb[bk], self.mod_b], [tf_b],
                     lambda e: e.activation(out=tf[:, 0:512], in_=ps[bk][:, :], func=AF.Copy,
                                            scale=self.mod[:, j, 32 + ci:33 + ci]))
                S.op("dve", [tf_b, self.xt_b[ci]], [self.xt_b[ci]],
                     lambda e: e.tensor_tensor(out=self.xt[:, ci, sl], in0=self.xt[:, ci, sl], in1=tf[:, 0:512], op=ALU.add))
            return None

        self.gemm_F(p_wout, st["wout_f"], list(range(8)), 16, rhs_m, [self.mh_b], nsub, groups3, epi_wout)

        def load_chunk(c):
            return self.xt[:, c, :], self.xt_b[c]

        self.norm_modulate(load_chunk, nsub, (self.A2, self.A2_b), 48, j, self.mh, self.mh_b, [6, 7])

        def rhs_u(kc, s):
            return self.r1[:, kc, s * 512:(s + 1) * 512]

        for q in range(4):
            def epi_w1(ci, banks):
                for s in range(nsub):
                    bk = banks[s]
                    sl = slice(s * 512, (s + 1) * 512)
                    tf, tf_b = self.rot("tmpf")
                    S.op("act", [psb[bk]], [tf_b], lambda e: e.activation(out=tf[:, 0:512], in_=ps[bk][:, :], func=AF.Relu))
                    S.op("dve", [tf_b], [self.r1_b],
                         lambda e: e.tensor_tensor(out=self.r1[:, ci, sl], in0=tf[:, 0:512], in1=tf[:, 0:512], op=ALU.mult))
                return None

            def epi_w2(ci, banks):
                for s in range(nsub):
                    bk = banks[s]
                    sl = slice(s * 512, (s + 1) * 512)
                    tf, tf_b = self.rot("tmpf")
                    S.op("act", [psb[bk], self.mod_b], [tf_b],
                         lambda e: e.activation(out=tf[:, 0:512], in_=ps[bk][:, :], func=AF.Copy,
                                                scale=self.mod[:, j, 80 + ci:81 + ci]))
                    S.op("dve", [tf_b, self.xt_b[ci]], [self.xt_b[ci]],
                         lambda e: e.tensor_tensor(out=self.xt[:, ci, sl], in0=self.xt[:, ci, sl], in1=tf[:, 0:512], op=ALU.add))
                return None

            base = p_mlp + q * 16
            self.gemm_F(base, st["w1_f"], list(range(8)), 16, rhs_m, [self.mh_b], nsub, groups3, epi_w1)
            self.gemm_F(base + 8, st[f"w2_f{q}"], list(range(8)), 16, rhs_u, [self.r1_b], nsub, groups3, epi_w2)
        self.wq_done()

        if not last:
            for c in range(16):
                S.dma("pool", self.XS[c * 128:(c + 1) * 128, cols], self.xt[:, c, 0:Tw], reads=[self.xt_b[c]])
        else:
            for c in range(16):
                sq, sq_b = self.rot("sq")
                S.op("act", [self.xt_b[c]], [sq_b], lambda e: e.activation(out=sq[:, 0:Tw], in_=self.xt[:, c, 0:Tw], func=AF.Square))
                for s in range(nsub):
                    bk = 6 + s
                    S.op("pe", [sq_b, self.ones_b], [psb[bk]],
                         lambda e: e.matmul(ps[bk][:, :], self.ones[:, :], sq[:, s * 512:(s + 1) * 512], start=(c == 0), stop=(c == 15)))
            self.rstd_from_stats([6, 7], nsub, D, self.rstd, self.rstd_b)
            o0 = tok0 - LAT0
            for c in range(16):
                tf, tf_b = self.rot("tmpf")
                S.op("dve", [self.xt_b[c], self.rstd_b, self.vec_b], [tf_b],
                     lambda e: e.scalar_tensor_tensor(out=tf[:, 0:Tw], in0=self.xt[:, c, 0:Tw],
                                                      scalar=self.vec[:, V_FINAL + c:V_FINAL + c + 1],
                                                      in1=self.rstd[:, 0:Tw], op0=ALU.mult, op1=ALU.mult))
                S.dma("pool", self.yout[c * 128:(c + 1) * 128, o0:o0 + Tw], tf[:, 0:Tw], reads=[tf_b])


def fm(v, nch):
    return np.ascontiguousarray(np.asarray(v, np.float32).reshape(nch, 128).T)


def rope_tables():
    rows = SEQ // 64
    row, col = np.meshgrid(np.arange(rows), np.arange(64), indexing="ij")
    row = row.reshape(-1).astype(np.float32)
    col = col.reshape(-1).astype(np.float32)
    freqs = (np.float32(10000.0) ** (-np.arange(0, 32, 2, dtype=np.float32) / np.float32(32))).astype(np.float32)
    ang_r = row[:, None] * freqs
    ang_c = col[:, None] * freqs
    ang = np.concatenate([ang_r, ang_r, ang_c, ang_c], axis=-1).astype(np.float32)
    cos = np.cos(ang).astype(np.float32).T
    sin = np.sin(ang).astype(np.float32).T
    cs = np.zeros((128, 2 * SEQ), np.float32)
    cs[0:64, 0:SEQ] = cos
    cs[64:128, 0:SEQ] = cos
    cs[0:64, SEQ:] = sin
    cs[64:128, SEQ:] = sin
    return cs


def rot_matrix_T():
    R = np.zeros((64, 64), np.float32)
    for i in range(64):
        blk, o = divmod(i, 32)
        if o < 16:
            R[i, blk * 32 + o + 16] = -1.0
        else:
            R[i, blk * 32 + o - 16] = 1.0
    R128 = np.zeros((128, 128), np.float32)
    R128[0:64, 0:64] = R
    R128[64:128, 64:128] = R
    return np.ascontiguousarray(R128.T)


def make_in_maps(inp, n_layers=DEPTH):
    x = np.asarray(inp["x"], np.float32)
    ctx = np.asarray(inp["ctx"], np.float32)
    c = np.asarray(inp["c"], np.float32)
    c_ctx = np.asarray(inp["c_ctx"], np.float32)
    vecs = np.zeros((128, NVEC), np.float32)
    for l in range(DEPTH):
        vb = l * V_PER_LAYER
        vecs[:, vb + V_N1:vb + V_N1 + 16] = fm(inp["norm1_g"][l], 16)
        vecs[:, vb + V_N2:vb + V_N2 + 16] = fm(inp["norm2_g"][l], 16)
        vecs[:, vb + V_BG:vb + V_BG + 32] = fm(inp["b_gate"][l], 32)
        vecs[:, vb + V_QA:vb + V_QA + 4] = fm(inp["q_a_norm"][l], 4)
        vecs[:, vb + V_KVA:vb + V_KVA + 2] = fm(inp["kv_a_norm"][l], 2)
        vecs[:, vb + V_SUB:vb + V_SUB + 1] = fm(inp["diff_subln"][l], 1)
        vecs[:, vb + V_BADA:vb + V_BADA + 96] = fm(inp["b_ada"][l], 96)
    vecs[:, V_FINAL:V_FINAL + 16] = fm(inp["final_norm_g"], 16)
    dlam = np.ascontiguousarray(np.broadcast_to(np.asarray(inp["diff_lambda"], np.float32).reshape(1, DEPTH * 256),
                                                (128, DEPTH * 256)))
    cs = rope_tables()
    rmat = rot_matrix_T()
    shared = {"vecs": vecs, "dlam": dlam, "cossin": cs, "rmat": rmat,
              "w_ada": np.ascontiguousarray(np.asarray(inp["w_ada"], np.float32))}
    for n in WSHAPES:
        shared[n] = np.ascontiguousarray(np.asarray(inp[n], np.float32))
    maps = []
    for core in range(NCORES):
        b0 = core * NB
        xin = np.empty((D, NTOK), np.float32)
        for i in range(NB):
            xin[:, i * CTX:(i + 1) * CTX] = ctx[b0 + i].T
            xin[:, LAT0 + i * SEQ:LAT0 + (i + 1) * SEQ] = x[b0 + i].T
        cv = np.stack([c[b0], c[b0 + 1], c_ctx], axis=-1)
        cvec = np.ascontiguousarray(cv.reshape(16, 128, 3).transpose(1, 0, 2).reshape(128, 48))
        m = dict(shared)
        m["xin"] = xin
        m["cvec"] = cvec
        maps.append(m)
    return maps


_CACHE = {}


def get_nc(n_layers=DEPTH, debug=None):
    key = (n_layers, repr(debug))
    if key not in _CACHE:
        b = Builder(n_layers, debug)
        b.build()
        _CACHE[key] = b
    return _CACHE[key]


def kernel(**inputs):
    b = get_nc()
    maps = make_in_maps(inputs)
    res = run_bass_kernel_spmd(b.nc, maps, core_ids=list(range(NCORES)))
    out = np.empty((NCORES * NB, SEQ, D), np.float32)
    for core in range(NCORES):
        y = res.results[core]["yout"]
        for i in range(NB):
            out[core * NB + i] = y[:, i * SEQ:(i + 1) * SEQ].T
    return out
```

```python
import math
from contextlib import ExitStack

import numpy as np
import concourse.bass as bass
import concourse.mybir as mybir
from concourse.bass_utils import run_bass_kernel_spmd

F32 = mybir.dt.float32
BF16 = mybir.dt.bfloat16
AF = mybir.ActivationFunctionType
ALU = mybir.AluOpType
AX = mybir.AxisListType

NCORES = 8
D = 2048
DEPTH = 4
SEQ = 2048
CTX = 256
NB = 2
NTOK = NB * (SEQ + CTX)
LAT0 = NB * CTX
EPS = 1e-6
DIFF_SCALE = 64 ** -0.5
MLA_SCALE = 192 ** -0.5
DFF = 8192
NH = 8
WB = 256
ADA_OVERLAP = True

TILES = [(0, 512, 2, True)] + [(LAT0 + i * 1024, 1024, i // 2, False) for i in range(4)]


class Buf:
    __slots__ = ("name", "w", "r")

    def __init__(self, sched, name):
        self.name = name
        self.w = {}
        self.r = {}
        sched.bufs.append(self)


class Sched:
    def __init__(self, nc, es, n_sp=44, n_pool=44):
        self.nc = nc
        self.engs = {"pe": nc.tensor, "act": nc.scalar, "dve": nc.vector, "pool": nc.gpsimd,
                     "sp": nc.sync}
        self.sem = {}
        self.cnt = {}
        for n in ("pe", "act", "dve", "pool"):
            self.sem[n] = es.enter_context(nc.semaphore("s_" + n))
            self.cnt[n] = 0
        self.dsem = []
        self.dcnt = []
        self.qsems = {"sp": [], "pool": []}
        self.qnext = {"sp": 0, "pool": 0}
        for q, n in (("sp", n_sp), ("pool", n_pool)):
            for i in range(n):
                idx = len(self.dsem)
                self.dsem.append(es.enter_context(nc.semaphore(f"d_{q}{i}")))
                self.dcnt.append(0)
                self.qsems[q].append(idx)
        self.seen = {e: {} for e in self.engs}
        self.bufs = []
        self.ninstr = 0

    def buf(self, name):
        return Buf(self, name)

    def _wait(self, eng, clock, val):
        if val <= 0:
            return
        seen = self.seen[eng]
        if seen.get(clock, 0) >= val:
            return
        sem = self.sem[clock] if isinstance(clock, str) else self.dsem[clock]
        self.engs[eng].wait_ge(sem, val)
        seen[clock] = val
        self.ninstr += 1

    def _deps(self, eng, reads, writes, is_dma):
        need = {}
        for b in reads:
            for k, v in b.w.items():
                if need.get(k, 0) < v:
                    need[k] = v
        for b in writes:
            for k, v in b.w.items():
                if need.get(k, 0) < v:
                    need[k] = v
            for k, v in b.r.items():
                if need.get(k, 0) < v:
                    need[k] = v
        for k, v in need.items():
            if k == "pe" and eng == "pe":
                continue
            self._wait(eng, k, v)

    def op(self, eng, reads, writes, fn):
        self._deps(eng, reads, writes, False)
        ins = fn(self.engs[eng])
        self.cnt[eng] += 1
        v = self.cnt[eng]
        ins.then_inc(self.sem[eng], 1)
        for b in reads:
            b.r[eng] = v
        for b in writes:
            b.w = {eng: v}
            b.r = {}
        self.ninstr += 1
        return ins

    def dma(self, q, out_ap, in_ap, reads=(), writes=(), join=False):
        sl = self.qsems[q]
        i = sl[self.qnext[q]]
        self.qnext[q] = (self.qnext[q] + 1) % len(sl)
        self._wait(q, i, 16 * self.dcnt[i])
        if join:
            self._deps(q, reads, [], True)
        else:
            self._deps(q, reads, writes, True)
        self.dcnt[i] += 1
        v = 16 * self.dcnt[i]
        self.engs[q].dma_start(out=out_ap, in_=in_ap).then_inc(self.dsem[i], 16)
        for b in reads:
            b.r[i] = v
        for b in writes:
            if join:
                b.w[i] = v
            else:
                b.w = {i: v}
                b.r = {}
        self.ninstr += 1

    def barrier(self):
        for eng in self.engs:
            for clock in ("pe", "act", "dve", "pool"):
                if clock != eng:
                    self._wait(eng, clock, self.cnt[clock])
            for i in range(len(self.dsem)):
                self._wait(eng, i, 16 * self.dcnt[i])
        for b in self.bufs:
            b.w = {}
            b.r = {}


class Stream:
    def __init__(self, name, src, k0, KC, chunks):
        self.name = name
        self.src = src
        self.k0 = k0
        self.KC = KC
        chunks = list(chunks)
        if len(chunks) % 2:
            chunks.append(chunks[-1])
        self.chunks = chunks
        self.nblk = len(chunks) // 2
        self.dram = None


def contiguous_chunks(c0, n):
    return [[(c0 + i * 128, 128)] for i in range(n)]


def make_streams():
    S = {}
    ch = []
    ch += contiguous_chunks(3072, 4)
    ch += contiguous_chunks(3584, 2)
    ch += contiguous_chunks(0, 8)
    ch += contiguous_chunks(1024, 8)
    ch += [[(3840, 64), (3840, 64)]]
    ch += contiguous_chunks(3904, 32)
    S["win_f"] = Stream("win_f", "w_in", 0, 16, ch)
    S["win_t"] = Stream("win_t", "w_in", 0, 16, contiguous_chunks(2048, 8))
    ch = [[(h * 192, 128)] for h in range(8)]
    ch += [[((2 * i) * 192 + 128, 64), ((2 * i + 1) * 192 + 128, 64)] for i in range(4)]
    S["wuq_f"] = Stream("wuq_f", "w_uq", 0, 4, ch)
    S["wukv_f"] = Stream("wukv_f", "w_ukv", 0, 2, [[(h * 256, 128)] for h in range(8)])
    S["wukv_t"] = Stream("wukv_t", "w_ukv", 0, 2, [[(h * 256 + 128, 128)] for h in range(8)])
    S["wod_f"] = Stream("wod_f", "w_o_diff", 0, 8, contiguous_chunks(0, 16))
    S["wom_f"] = Stream("wom_f", "w_o_mla", 0, 8, contiguous_chunks(0, 16))
    S["wout_f"] = Stream("wout_f", "w_out", 0, 16, contiguous_chunks(0, 16))
    S["w1_f"] = Stream("w1_f", "w_mlp1", 0, 16, contiguous_chunks(0, 64))
    for q in range(4):
        S[f"w2_f{q}"] = Stream(f"w2_f{q}", "w_mlp2", q * 2048, 16, contiguous_chunks(0, 16))
    return S


WSHAPES = {
    "w_in": (2048, 8000), "w_uq": (512, 1536), "w_ukv": (256, 2048), "w_o_diff": (1024, 2048),
    "w_o_mla": (1024, 2048), "w_out": (2048, 2048), "w_mlp1": (2048, 8192), "w_mlp2": (8192, 2048),
}

V_N1, V_N2, V_BG, V_QA, V_KVA, V_SUB, V_BADA = 0, 16, 32, 64, 68, 70, 71
V_PER_LAYER = 71 + 96
V_FINAL = DEPTH * V_PER_LAYER
NVEC = V_FINAL + 16


class Builder:
    def __init__(self, n_layers=DEPTH, debug=None):
        self.n_layers = n_layers
        self.debug = debug or {}
        self.nc = nc = bass.Bass("TRN2", target_bir_lowering=False)
        self.streams = make_streams()
        dt = nc.dram_tensor
        self.xin = dt("xin", [D, NTOK], F32, kind="ExternalInput").ap()
        self.cvec = dt("cvec", [128, 16 * 3], F32, kind="ExternalInput").ap()
        self.vecs = dt("vecs", [128, NVEC], F32, kind="ExternalInput").ap()
        self.dlam = dt("dlam", [128, DEPTH * 256], F32, kind="ExternalInput").ap()
        self.cossin = dt("cossin", [128, 2 * SEQ], F32, kind="ExternalInput").ap()
        self.rmat = dt("rmat", [128, 128], F32, kind="ExternalInput").ap()
        self.w_ada = dt("w_ada", [DEPTH, D, 6 * D], F32, kind="ExternalInput").ap()
        self.wsrc = {}
        for n, (k, m) in WSHAPES.items():
            self.wsrc[n] = dt(n, [DEPTH, k, m], F32, kind="ExternalInput").ap()
        self.yout = dt("yout", [D, NB * SEQ], F32, kind="ExternalOutput").ap()
        def scr(name, shape, dtype=BF16):
            kind = "ExternalOutput" if name in self.debug.get("dump", ()) else "Internal"
            return dt(name, shape, dtype, kind=kind).ap()
        self.XS = scr("XS", [D, NTOK], F32)
        self.QD = scr("QD", [NH, 128, NTOK])
        self.KD = scr("KD", [NH, 128, NTOK])
        self.VD = scr("VD", [NTOK, 1024])
        self.QMN = scr("QMN", [NH, 128, NTOK])
        self.QMP = scr("QMP", [NH // 2, 128, NTOK])
        self.KMN = scr("KMN", [NH, 128, NTOK])
        self.KMP = scr("KMP", [128, NTOK])
        self.VM = scr("VM", [NTOK, 1024])
        self.G = scr("G", [32, 128, NTOK])
        self.YD = scr("YD", [NH, 128, NTOK])
        self.YM = scr("YM", [NH, 128, NTOK])
        self.MODD = scr("MODD", [128, 3 * 96], F32)
        for s in self.streams.values():
            s.dram = [scr(f"{s.name}_{l}", [s.nblk, 128, s.KC * WB]) for l in range(n_layers)]

    def build(self):
        nc = self.nc
        with ExitStack() as es:
            self.S = S = Sched(nc, es)
            self.ps = []
            self.psb = []
            for i in range(8):
                self.ps.append(es.enter_context(nc.psum_tensor(f"ps{i}", [128, 512], F32)))
                self.psb.append(S.buf(f"ps{i}"))
            self.consts(es)
            stop = self.debug.get("stop")
            for l in range(self.n_layers):
                self.l = l
                last = (l == DEPTH - 1)
                S.barrier()
                if l == 0 and ADA_OVERLAP and stop is None:
                    with ExitStack() as sc:
                        g1 = self.conv_gen(self.conv_staging(sc), 0, ["act", "dve", "pool"], "A")
                        g2 = self.ada_gen(sc, 0, 0)
                        alive = [g1, g1, g2]
                        while alive:
                            for g in list(alive):
                                try:
                                    next(g)
                                except StopIteration:
                                    alive = [a for a in alive if a is not g]
                        S.barrier()
                else:
                    if l == 0:
                        with ExitStack() as sc:
                            self.phase_conv(sc, l, "A")
                            S.barrier()
                    if stop == ("conv", l):
                        break
                    if l == 0 or not ADA_OVERLAP:
                        with ExitStack() as sc:
                            self.phase_ada(sc, l)
                            S.barrier()
                self.use_modset(l)
                if stop == ("ada", l):
                    break
                with ExitStack() as sc:
                    self.alloc_A(sc)
                    for t in self.debug.get("A_tile_list", range(self.debug.get("A_tiles", len(TILES)))):
                        self.phase_A(l, t)
                    S.barrier()
                if stop == ("A", l):
                    break
                with ExitStack() as sc:
                    self.alloc_B(sc)
                    staging = self.conv_staging(sc)
                    gens = [self.conv_gen(staging, l, ["dve", "pool"], "C")]
                    if l + 1 < self.n_layers:
                        gens.append(self.conv_gen(staging, l + 1, ["dve", "pool"], "A"))
                    self.bg = self.chain(gens)
                    self.bg_ada = None
                    if ADA_OVERLAP and l + 1 < self.n_layers:
                        self.bg_ada = self.ada_gen(sc, l + 1, 2)
                    for b in range(NB):
                        self.phase_B(l, b, last)
                    self.bg_step(100000)
                    self.ada_step(100000)
                    S.barrier()
                if stop == ("B", l):
                    break
                with ExitStack() as sc:
                    self.alloc_C(sc)
                    for t in range(len(TILES)):
                        if last and TILES[t][3]:
                            continue
                        self.phase_C(l, t, last)
                    S.barrier()
            S.barrier()
        return nc

    def T(self, es, name, shape, dtype):
        self.uid = getattr(self, "uid", 0) + 1
        name = f"{name}_u{self.uid}"
        t = es.enter_context(self.nc.sbuf_tensor(name, shape, dtype))
        return t, self.S.buf(name)

    def use_modset(self, i):
        (self.mod, self.mod_b), (self.A1, self.A1_b), (self.A2, self.A2_b), (self.lamt, self.lamt_b) = self.modsets[i % 2]

    def consts(self, es):
        S = self.S
        self.ones, self.ones_b = self.T(es, "ones", [128, 128], BF16)
        self.rT, self.rT_b = self.T(es, "rT", [128, 128], BF16)
        self.vec, self.vec_b = self.T(es, "vec", [128, NVEC], F32)
        self.silc, self.silc_b = self.T(es, "silc", [128, 48], F32)
        self.modsets = []
        for i in range(2):
            self.modsets.append((self.T(es, f"mod{i}", [128, 3, 96], F32), self.T(es, f"A1{i}", [128, 3, 16], F32),
                                 self.T(es, f"A2{i}", [128, 3, 16], F32), self.T(es, f"lamt{i}", [128, 8], F32)))
        self.use_modset(0)
        self.epst, self.epst_b = self.T(es, "epst", [128, 1], F32)
        with ExitStack() as sc:
            tmp, tmp_b = self.T(sc, "ctmp", [128, 128], F32)
            S.op("dve", [], [self.ones_b], lambda e: e.memset(self.ones[:], 1.0))
            S.op("dve", [], [self.epst_b], lambda e: e.memset(self.epst[:], EPS))
            S.dma("sp", tmp[:], self.rmat[:, :], writes=[tmp_b])
            S.dma("sp", self.vec[:], self.vecs[:, :], writes=[self.vec_b])
            S.dma("sp", self.silc[:], self.cvec[:, :], writes=[self.silc_b])
            S.op("act", [tmp_b], [self.rT_b], lambda e: e.activation(out=self.rT[:], in_=tmp[:], func=AF.Copy))
            S.op("act", [self.silc_b], [self.silc_b],
                 lambda e: e.activation(out=self.silc[:], in_=self.silc[:], func=AF.Silu))
            S.barrier()

    A_SRCS = ("w_in", "w_uq", "w_ukv")

    def conv_staging(self, sc):
        st32 = [self.T(sc, f"cv32_{i}", [128, 2048], F32) for i in range(3)]
        st16 = [self.T(sc, f"cv16_{i}", [128, 2048], BF16) for i in range(3)]
        return st32, st16

    def phase_conv(self, sc, l, group):
        for _ in self.conv_gen(self.conv_staging(sc), l, ["act", "dve", "pool"], group):
            pass

    def conv_gen(self, staging, l, engs, group):
        S = self.S
        NST = 3
        PIECE = 2048
        st32, st16 = staging
        it = 0
        by_src = {}
        for s in self.streams.values():
            g = "A" if s.src in self.A_SRCS else "C"
            if g not in group:
                continue
            by_src.setdefault((s.src, s.k0, s.KC), []).append(s)
        for (src, k0, KC), slist in by_src.items():
            K, N = WSHAPES[src]
            segs = []
            for s in slist:
                for ci, chunk in enumerate(s.chunks):
                    off = (ci % 2) * 128
                    for (c0, w) in chunk:
                        segs.append((c0, w, s, ci // 2, off))
                        off += w
            segs.sort(key=lambda x: (x[0], x[2].name, x[3], x[4]))
            pieces = []
            pstart, pend = None, None
            for x in segs:
                if pstart is None:
                    pstart, pend = x[0], x[0] + x[1]
                elif x[0] + x[1] - pstart <= PIECE:
                    pend = max(pend, x[0] + x[1])
                else:
                    pieces.append((pstart, pend - pstart))
                    pstart, pend = x[0], x[0] + x[1]
            pieces.append((pstart, pend - pstart))
            W = self.wsrc[src]
            for kc in range(KC):
                r0 = k0 + kc * 128
                for (pc0, pw) in pieces:
                    inside = [x for x in segs if x[0] >= pc0 and x[0] + x[1] <= pc0 + pw]
                    straddle = [x for x in segs if not (x[0] + x[1] <= pc0 or x[0] >= pc0 + pw)
                                and x not in inside]
                    assert not straddle, (src, pc0, pw, straddle[:2])
                    if not inside:
                        continue
                    (t32, b32), (t16, b16) = st32[it % NST], st16[it % NST]
                    eng = engs[it % len(engs)]
                    it += 1
                    S.dma("sp", t32[:, 0:pw], W[l, r0:r0 + 128, pc0:pc0 + pw], writes=[b32])
                    if eng == "act":
                        S.op("act", [b32], [b16],
                             lambda e: e.activation(out=t16[:, 0:pw], in_=t32[:, 0:pw], func=AF.Copy))
                    else:
                        S.op(eng, [b32], [b16], lambda e: e.tensor_copy(t16[:, 0:pw], t32[:, 0:pw]))
                    i = 0
                    while i < len(inside):
                        c0, w, s, blk, off = inside[i]
                        run = None
                        if off == 0 and w == 128 and i + 1 < len(inside):
                            j = i
                            nb = 0
                            cc = c0
                            bb = blk
                            while (j + 1 < len(inside)
                                   and inside[j][2] is s and inside[j + 1][2] is s
                                   and inside[j][3] == bb and inside[j + 1][3] == bb
                                   and inside[j][4] == 0 and inside[j + 1][4] == 128
                                   and inside[j][1] == 128 and inside[j + 1][1] == 128
                                   and inside[j][0] == cc and inside[j + 1][0] == cc + 128):
                                nb += 1
                                j += 2
                                cc += 256
                                bb += 1
                            if nb >= 1:
                                run = (nb, j)
                        if run is not None:
                            nb, j = run
                            dst = s.dram[l][blk:blk + nb, :, kc * WB:(kc + 1) * WB].rearrange("b p c -> p b c")
                            srcap = t16[:, c0 - pc0:c0 - pc0 + nb * WB].rearrange("p (b c) -> p b c", b=nb)
                            S.dma("pool", dst, srcap, reads=[b16])
                            i = j
                        else:
                            dst = s.dram[l][blk, :, kc * WB + off:kc * WB + off + w]
                            S.dma("pool", dst, t16[:, c0 - pc0:c0 - pc0 + w], reads=[b16])
                            i += 1
                    yield

    def phase_ada(self, sc, l):
        self.use_modset(l)
        for _ in self.ada_gen(sc, l, 0):
            pass

    def ada_gen(self, sc, l, bank):
        S = self.S
        NBLK = 48
        wb = [self.T(sc, f"adaw{i}", [128, 16, 256], F32) for i in range(2)]
        adat, adat_b = self.T(sc, "adat", [128, 288], F32)
        psum, psum_b = self.ps[bank], self.psb[bank]
        (mod, mod_b), (A1, A1_b), (A2, A2_b), (lamt, lamt_b) = self.modsets[l % 2]
        wsrc = self.w_ada[l].rearrange("(kc p) n -> p kc n", p=128)
        for nb in range(NBLK):
            t, b = wb[nb % 2]
            for k4 in range(2):
                S.dma("sp", t[:, k4 * 8:(k4 + 1) * 8, :], wsrc[:, k4 * 8:(k4 + 1) * 8, nb * 256:(nb + 1) * 256],
                      writes=[b], join=(k4 > 0))
            for m in range(2):
                ch = nb * 2 + m
                for kc in range(16):
                    S.op("pe", [b, self.silc_b], [psum_b],
                         lambda e: e.matmul(psum[:, m * 3:m * 3 + 3], t[:, kc, m * 128:(m + 1) * 128],
                                            self.silc[:, kc * 3:kc * 3 + 3], start=(kc == 0), stop=(kc == 15)))
            S.op("act", [psum_b], [adat_b],
                 lambda e: e.activation(out=adat[:, nb * 6:nb * 6 + 6], in_=psum[:, 0:6], func=AF.Copy))
            yield
        vb = l * V_PER_LAYER
        psv = adat[:, :].rearrange("p (c j) -> p c j", j=3)
        psum_b = adat_b
        self_mod, self_mod_b, self_A1, self_A1_b, self_A2, self_A2_b, self_lamt, self_lamt_b = mod, mod_b, A1, A1_b, A2, A2_b, lamt, lamt_b
        for j in range(3):
            S.op("dve", [psum_b, self.vec_b], [self_mod_b],
                 lambda e: e.tensor_tensor(out=self_mod[:, j, :], in0=psv[:, :, j],
                                           in1=self.vec[:, vb + V_BADA:vb + V_BADA + 96], op=ALU.add))
        for j in range(3):
            S.op("dve", [self_mod_b, self.vec_b], [self_A1_b],
                 lambda e: e.scalar_tensor_tensor(out=self_A1[:, j, :], in0=self_mod[:, j, 16:32], scalar=1.0,
                                                  in1=self.vec[:, vb + V_N1:vb + V_N1 + 16],
                                                  op0=ALU.add, op1=ALU.mult))
            S.op("dve", [self_mod_b, self.vec_b], [self_A2_b],
                 lambda e: e.scalar_tensor_tensor(out=self_A2[:, j, :], in0=self_mod[:, j, 64:80], scalar=1.0,
                                                  in1=self.vec[:, vb + V_N2:vb + V_N2 + 16],
                                                  op0=ALU.add, op1=ALU.mult))
        lam_init = 0.8 - 0.6 * math.exp(-0.3 * l)
        dl, dl_b = self.T(sc, "dl", [128, 256], F32)
        pr, pr_b = self.T(sc, "dlp", [128, 128], F32)
        S.dma("sp", dl[:], self.dlam[:, l * 256:(l + 1) * 256], writes=[dl_b])
        S.op("dve", [dl_b], [pr_b], lambda e: e.tensor_tensor(out=pr[:, 0:64], in0=dl[:, 0:64], in1=dl[:, 64:128], op=ALU.mult))
        S.op("dve", [dl_b], [pr_b], lambda e: e.tensor_tensor(out=pr[:, 64:128], in0=dl[:, 128:192], in1=dl[:, 192:256], op=ALU.mult))
        S.op("dve", [pr_b], [self_lamt_b], lambda e: e.reduce_sum(out=self_lamt[:, 2:3], in_=pr[:, 0:64], axis=AX.X))
        S.op("dve", [pr_b, self_lamt_b], [self_lamt_b], lambda e: e.reduce_sum(out=self_lamt[:, 3:4], in_=pr[:, 64:128], axis=AX.X))
        S.op("act", [self_lamt_b], [self_lamt_b], lambda e: e.activation(out=self_lamt[:, 4:6], in_=self_lamt[:, 2:4], func=AF.Exp))
        S.op("dve", [self_lamt_b], [self_lamt_b], lambda e: e.tensor_tensor(out=self_lamt[:, 6:7], in0=self_lamt[:, 5:6], in1=self_lamt[:, 4:5], op=ALU.subtract))
        S.op("dve", [self_lamt_b], [self_lamt_b], lambda e: e.tensor_scalar(out=self_lamt[:, 0:1], in0=self_lamt[:, 6:7], scalar1=-lam_init, scalar2=1.0, op0=ALU.add, op1=ALU.mult))
        S.op("dve", [self.vec_b, self_lamt_b], [self_lamt_b],
             lambda e: e.tensor_scalar(out=self_lamt[:, 1:2], in0=self.vec[:, vb + V_SUB:vb + V_SUB + 1],
                                       scalar1=(1.0 - lam_init), scalar2=1.0, op0=ALU.mult, op1=ALU.mult))
        if "MODD" in self.debug.get("dump", ()):
            S.dma("pool", self.MODD[:, :], self_mod[:].rearrange("p j c -> p (j c)"), reads=[self_mod_b])
        yield

    def alloc_wslots(self, sc, n):
        self.wslots = [self.T(sc, f"wsl{i}", [128, 16 * WB], BF16) for i in range(n)]
        self.wplan = []
        self.wissued = 0
        self.wbase = 0

    def wq_plan(self, items):
        self.wplan = list(items)
        self.wissued = 0
        self.wcons = 0

    def wq_issue_to(self, upto):
        n = len(self.wslots)
        while self.wissued < min(upto, len(self.wplan)):
            i = self.wissued
            s, blk = self.wplan[i]
            t, b = self.wslots[(self.wbase + i) % n]
            self.S.dma("sp", t[:, 0:s.KC * WB], s.dram[self.l][blk, :, :], writes=[b])
            self.wissued += 1

    def wq_get(self, i, oldest=None):
        n = len(self.wslots)
        if oldest is None:
            oldest = i
        self.wq_issue_to(oldest + n)
        return self.wslots[(self.wbase + i) % n]

    def wq_done(self):
        self.wbase = (self.wbase + len(self.wplan)) % len(self.wslots)
        self.wplan = []

    def gemm_F(self, plan_base, stream, blks, nchunks, rhs_fn, rhs_bufs, nsub, bank_groups, epilogue):
        S = self.S
        KC = stream.KC
        pending = None
        gi = 0
        for bi in range(len(blks)):
            wt, wb_ = self.wq_get(plan_base + bi)
            for j in range(2):
                ci = bi * 2 + j
                if ci >= nchunks:
                    break
                banks = bank_groups[gi % len(bank_groups)]
                gi += 1
                for s in range(nsub):
                    bk = banks[s]
                    for kc in range(KC):
                        S.op("pe", [wb_] + rhs_bufs, [self.psb[bk]],
                             lambda e: e.matmul(self.ps[bk][:, :], wt[:, kc * WB + j * 128:kc * WB + (j + 1) * 128],
                                                rhs_fn(kc, s), start=(kc == 0), stop=(kc == KC - 1)))
                if pending is not None:
                    pending()
                pending = epilogue(ci, banks)
        if pending is not None:
            pending()

    def rstd_from_stats(self, banks, nsub, dim, rstd, rstd_b):
        S = self.S
        for s in range(nsub):
            bk = banks[s]
            S.op("act", [self.psb[bk], self.epst_b], [rstd_b],
                 lambda e: e.activation(out=rstd[:, s * 512:(s + 1) * 512], in_=self.ps[bk][:, :], func=AF.Sqrt,
                                        bias=self.epst[:, 0:1], scale=1.0 / dim))
        w = nsub * 512
        S.op("dve", [rstd_b], [rstd_b], lambda e: e.reciprocal(out=rstd[:, 0:w], in_=rstd[:, 0:w]))

    def alloc_A(self, sc):
        T = self.T
        self.alloc_wslots(sc, 6)
        self.xc = [T(sc, f"xc{i}", [128, 1024], F32) for i in range(3)]
        self.hT, self.hT_b = T(sc, "hT", [128, 16, 1024], BF16)
        self.cq32, self.cq32_b = T(sc, "cq32", [128, 4, 1024], F32)
        self.ckv32, self.ckv32_b = T(sc, "ckv32", [128, 2, 1024], F32)
        self.cqn, self.cqn_b = T(sc, "cqn", [128, 4, 1024], BF16)
        self.ckvn, self.ckvn_b = T(sc, "ckvn", [128, 2, 1024], BF16)
        self.rstd, self.rstd_b = T(sc, "rstd", [128, 1024], F32)
        self.tmpf = [T(sc, f"tmpf{i}", [128, 1024], F32) for i in range(3)]
        self.stage = [T(sc, f"stage{i}", [128, 1024], BF16) for i in range(4)]
        self.sq = [T(sc, f"sq{i}", [128, 1024], BF16) for i in range(2)]
        self.xb = [T(sc, f"xb{i}", [128, 1024], BF16) for i in range(2)]
        self.cs, self.cs_b = T(sc, "cs", [128, 2 * SEQ], F32)
        self.S.dma("sp", self.cs[:], self.cossin[:, :], writes=[self.cs_b])
        self.rr = {"xc": 0, "tmpf": 0, "stage": 0, "sq": 0, "xb": 0}

    def rot(self, name):
        lst = getattr(self, name)
        i = self.rr[name]
        self.rr[name] = (i + 1) % len(lst)
        return lst[i]

    def norm_modulate(self, load_chunk, nsub, A, Bsh, j, out_t, out_b, stat_banks):
        S = self.S
        Tw = nsub * 512
        for c in range(16):
            xt, xb_ = load_chunk(c)
            sq, sq_b = self.rot("sq")
            S.op("act", [xb_], [sq_b], lambda e: e.activation(out=sq[:, 0:Tw], in_=xt[:, 0:Tw], func=AF.Square))
            for s in range(nsub):
                bk = stat_banks[s]
                S.op("pe", [sq_b, self.ones_b], [self.psb[bk]],
                     lambda e: e.matmul(self.ps[bk][:, :], self.ones[:, :], sq[:, s * 512:(s + 1) * 512],
                                        start=(c == 0), stop=(c == 15)))
        self.rstd_from_stats(stat_banks, nsub, D, self.rstd, self.rstd_b)
        for c in range(16):
            xt, xb_ = load_chunk(c)
            tf, tf_b = self.rot("tmpf")
            S.op("dve", [xb_, self.rstd_b, A[1]], [tf_b],
                 lambda e: e.scalar_tensor_tensor(out=tf[:, 0:Tw], in0=xt[:, 0:Tw], scalar=A[0][:, j, c:c + 1],
                                                  in1=self.rstd[:, 0:Tw], op0=ALU.mult, op1=ALU.mult))
            S.op("act", [tf_b, self.mod_b], [out_b],
                 lambda e: e.activation(out=out_t[:, c, 0:Tw], in_=tf[:, 0:Tw], func=AF.Identity,
                                        bias=self.mod[:, j, Bsh + c:Bsh + c + 1], scale=1.0))

    def store_chunk(self, dst_ap, src_t, src_b, Tw, rows=None):
        if rows is None:
            self.S.dma("pool", dst_ap, src_t[:, 0:Tw], reads=[src_b])
        else:
            r0, r1 = rows
            self.S.dma("pool", dst_ap, src_t[r0:r1, 0:Tw], reads=[src_b])

    def epi_copy_store(self, banks, nsub, dsts):
        S = self.S
        st, st_b = self.rot("stage")
        for s in range(nsub):
            bk = banks[s]
            S.op("act", [self.psb[bk]], [st_b],
                 lambda e: e.activation(out=st[:, s * 512:(s + 1) * 512], in_=self.ps[bk][:, :], func=AF.Copy))
        for dst, rows in dsts:
            self.store_chunk(dst, st, st_b, nsub * 512, rows)

    def epi_rope_store(self, banks, nsub, pos0, dsts):
        S = self.S
        xb, xb_b = self.rot("xb")
        for s in range(nsub):
            bk = banks[s]
            S.op("act", [self.psb[bk]], [xb_b],
                 lambda e: e.activation(out=xb[:, s * 512:(s + 1) * 512], in_=self.ps[bk][:, :], func=AF.Copy))

        def post():
            st, st_b = self.rot("stage")
            for s in range(nsub):
                bk = banks[s]
                rb = 6 + s
                S.op("pe", [xb_b, self.rT_b], [self.psb[rb]],
                     lambda e: e.matmul(self.ps[rb][:, :], self.rT[:, :], xb[:, s * 512:(s + 1) * 512],
                                        start=True, stop=True))
                if self.debug.get("rope_pe_only"):
                    continue
                t1, t1_b = self.rot("tmpf")
                t2, t2_b = self.rot("tmpf")
                p = pos0 + s * 512
                if not self.debug.get("rope_psum"):
                    S.op("act", [self.psb[bk]], [t1_b],
                         lambda e: e.activation(out=t1[:, 0:512], in_=self.ps[bk][:, :], func=AF.Copy))
                    S.op("act", [self.psb[rb]], [t2_b],
                         lambda e: e.activation(out=t2[:, 0:512], in_=self.ps[rb][:, :], func=AF.Copy))
                    S.op("dve", [t1_b, self.cs_b], [t1_b],
                         lambda e: e.tensor_tensor(out=t1[:, 0:512], in0=t1[:, 0:512], in1=self.cs[:, p:p + 512], op=ALU.mult))
                    S.op("dve", [t2_b, self.cs_b], [t2_b],
                         lambda e: e.tensor_tensor(out=t2[:, 0:512], in0=t2[:, 0:512],
                                                   in1=self.cs[:, SEQ + p:SEQ + p + 512], op=ALU.mult))
                else:
                    S.op("dve", [self.psb[bk], self.cs_b], [t1_b],
                         lambda e: e.tensor_tensor(out=t1[:, 0:512], in0=self.ps[bk][:, :], in1=self.cs[:, p:p + 512], op=ALU.mult))
                    S.op("dve", [self.psb[rb], self.cs_b], [t2_b],
                         lambda e: e.tensor_tensor(out=t2[:, 0:512], in0=self.ps[rb][:, :],
                                                   in1=self.cs[:, SEQ + p:SEQ + p + 512], op=ALU.mult))
                S.op("pool" if self.debug.get("pool_add") else "dve", [t1_b, t2_b], [st_b],
                     lambda e: e.tensor_tensor(out=st[:, s * 512:(s + 1) * 512], in0=t1[:, 0:512], in1=t2[:, 0:512], op=ALU.add))
            if self.debug.get("rope_pe_only") or self.debug.get("rope_no_store"):
                return
            for dst, rows in dsts:
                self.store_chunk(dst, st, st_b, nsub * 512, rows)
        if self.debug.get("no_defer"):
            post()
            return None
        return post

    def phase_A(self, l, t):
        S = self.S
        tok0, Tw, j, is_ctx = TILES[t]
        nsub = Tw // 512
        src = self.xin if l == 0 else self.XS
        vb = l * V_PER_LAYER
        pos0 = 0 if is_ctx else (tok0 - LAT0) % SEQ
        st = self.streams
        cols = slice(tok0, tok0 + Tw)

        def load_chunk(c):
            xt, xb_ = self.rot("xc")
            S.dma("sp", xt[:, 0:Tw], src[c * 128:(c + 1) * 128, tok0:tok0 + Tw], writes=[xb_])
            return xt, xb_

        plan = [(st["win_f"], b) for b in range(st["win_f"].nblk)]
        p_wint = len(plan)
        plan += [(st["win_t"], b) for b in range(4)]
        p_wuq = len(plan)
        plan += [(st["wuq_f"], b) for b in range(6)]
        p_wukv = len(plan)
        plan += [(st["wukv_f"], b) for b in range(4)]
        p_wukvt = len(plan)
        plan += [(st["wukv_t"], b) for b in range(4)]
        self.wq_plan(plan)
        self.wq_issue_to(len(self.wslots))

        self.norm_modulate(load_chunk, nsub, (self.A1, self.A1_b), 0, j, self.hT, self.hT_b, [6, 7])

        groups = [[0, 1], [2, 3], [4, 5]]
        steps = self.debug.get("A_steps", 99)
        if steps < 2:
            self.wq_done()
            return

        def rhs_h(kc, s):
            return self.hT[:, kc, s * 512:(s + 1) * 512]

        def epi_win(ci, banks):
            if ci < 6:
                if ci < 4:
                    dst_t, dst_b, cc, first, lastc = self.cq32, self.cq32_b, ci, ci == 0, ci == 3
                else:
                    dst_t, dst_b, cc, first, lastc = self.ckv32, self.ckv32_b, ci - 4, ci == 4, ci == 5
                sq, sq_b = self.rot("sq")
                for s in range(nsub):
                    bk = banks[s]
                    S.op("act", [self.psb[bk]], [dst_b],
                         lambda e: e.activation(out=dst_t[:, cc, s * 512:(s + 1) * 512], in_=self.ps[bk][:, :], func=AF.Copy))
                    S.op("act", [self.psb[bk]], [sq_b],
                         lambda e: e.activation(out=sq[:, s * 512:(s + 1) * 512], in_=self.ps[bk][:, :], func=AF.Square))

                def post():
                    for s in range(nsub):
                        rb = 6 + s
                        S.op("pe", [sq_b, self.ones_b], [self.psb[rb]],
                             lambda e: e.matmul(self.ps[rb][:, :], self.ones[:, :], sq[:, s * 512:(s + 1) * 512],
                                                start=first, stop=lastc))
                    if lastc:
                        if ci == 3:
                            n_t, n_b, nn, dim, gv = self.cqn, self.cqn_b, 4, 512, V_QA
                        else:
                            n_t, n_b, nn, dim, gv = self.ckvn, self.ckvn_b, 2, 256, V_KVA
                        self.rstd_from_stats([6, 7], nsub, dim, self.rstd, self.rstd_b)
                        for c2 in range(nn):
                            S.op("dve", [dst_b, self.rstd_b, self.vec_b], [n_b],
                                 lambda e: e.scalar_tensor_tensor(out=n_t[:, c2, 0:Tw], in0=dst_t[:, c2, 0:Tw],
                                                                  scalar=self.vec[:, vb + gv + c2:vb + gv + c2 + 1],
                                                                  in1=self.rstd[:, 0:Tw], op0=ALU.mult, op1=ALU.mult))
                return post
            if ci < 22:
                h = (ci - 6) % 8
                dstT = self.QD if ci < 14 else self.KD
                dsts = [(dstT[h, :, cols], None)]
                if is_ctx:
                    self.epi_copy_store(banks, nsub, dsts)
                    return None
                return self.epi_rope_store(banks, nsub, pos0, dsts)
            if ci == 22:
                dsts = [(self.KMP[:, cols], None)]
                if is_ctx:
                    self.epi_copy_store(banks, nsub, dsts)
                    return None
                return self.epi_rope_store(banks, nsub, pos0, dsts)
            g = ci - 23
            sg, sg_b = self.rot("stage")
            for s in range(nsub):
                bk = banks[s]
                S.op("act", [self.psb[bk], self.vec_b], [sg_b],
                     lambda e: e.activation(out=sg[:, s * 512:(s + 1) * 512], in_=self.ps[bk][:, :], func=AF.Sigmoid,
                                            bias=self.vec[:, vb + V_BG + g:vb + V_BG + g + 1], scale=1.0))
            self.store_chunk(self.G[g, :, cols], sg, sg_b, Tw)
            return None

        skip = self.debug.get("A_skip", ())
        self.gemm_F(0, st["win_f"], list(range(st["win_f"].nblk)), 55 if "win" not in skip else 6, rhs_h, [self.hT_b], nsub, groups, epi_win)

        if steps < 3:
            self.wq_done()
            return
        if "vd" not in skip:
            self.gemm_T(p_wint, st["win_t"], self.hT, self.hT_b, 16, Tw, self.VD, tok0)
        if steps < 4:
            self.wq_done()
            return

        def rhs_cq(kc, s):
            return self.cqn[:, kc, s * 512:(s + 1) * 512]

        def epi_wuq(ci, banks):
            if ci < 8:
                self.epi_copy_store(banks, nsub, [(self.QMN[ci, :, cols], None)])
                return None
            dsts = [((self.QD if self.debug.get("qmp_to_qd") else self.QMP)[ci - 8, :, cols], None)]
            if is_ctx:
                self.epi_copy_store(banks, nsub, dsts)
                return None
            return self.epi_rope_store(banks, nsub, pos0, dsts)

        if "wuq" not in skip:
            self.gemm_F(p_wuq, st["wuq_f"], list(range(6)), self.debug.get("wuq_n", 12), rhs_cq, [self.cqn_b], nsub, groups, epi_wuq)

        if steps < 5:
            self.wq_done()
            return
        def rhs_ckv(kc, s):
            return self.ckvn[:, kc, s * 512:(s + 1) * 512]

        def epi_wukv(ci, banks):
            self.epi_copy_store(banks, nsub, [(self.KMN[ci, :, cols], None)])
            return None

        if "wukv" not in skip:
            self.gemm_F(p_wukv, st["wukv_f"], list(range(4)), 8, rhs_ckv, [self.ckvn_b], nsub, groups, epi_wukv)
        if steps < 6:
            self.wq_done()
            return
        self.gemm_T(p_wukvt, st["wukv_t"], self.ckvn, self.ckvn_b, 2, Tw, self.VD if self.debug.get("vm_to_vd") else self.VM, tok0)
        self.wq_done()

    def gemm_T(self, plan_base, stream, act_t, act_b, KC, Tw, dstV, tok0):
        S = self.S
        slots = [self.wq_get(plan_base + b, oldest=plan_base) for b in range(4)]
        pairs = [[0, 1], [2, 3], [4, 5]]
        for tb in range(Tw // 128):
            banks = pairs[tb % 3]
            for cb in range(4):
                wt, wb_ = slots[cb]
                bk = banks[cb // 2]
                half = (cb % 2) * 256
                for kc in range(KC):
                    S.op("pe", [wb_, act_b], [self.psb[bk]],
                         lambda e: e.matmul(self.ps[bk][:, half:half + 256], act_t[:, kc, tb * 128:(tb + 1) * 128],
                                            wt[:, kc * WB:(kc + 1) * WB], start=(kc == 0), stop=(kc == KC - 1)))
            sv, sv_b = self.rot("stage")
            for hb in range(2):
                bk = banks[hb]
                S.op("act", [self.psb[bk]], [sv_b],
                     lambda e: e.activation(out=sv[:, hb * 512:(hb + 1) * 512], in_=self.ps[bk][:, :], func=AF.Copy))
            r0 = tok0 + tb * 128
            S.dma("pool", dstV[r0:r0 + 128, :], sv[:, 0:1024], reads=[sv_b])

    def alloc_B(self, sc):
        T = self.T
        S = self.S
        self.B_d = []
        self.B_m = []
        for i in range(2):
            d = {}
            d["K"] = T(sc, f"bK{i}", [128, 2304], BF16)
            d["V"] = T(sc, f"bV{i}", [128, 18, 128], BF16)
            d["Q1"] = T(sc, f"bQ1{i}", [128, 2304], BF16)
            d["Q2"] = T(sc, f"bQ2{i}", [128, 2304], BF16)
            self.B_d.append(d)
            m = {}
            m["K"] = T(sc, f"mK{i}", [128, 2304], BF16)
            m["V"] = T(sc, f"mV{i}", [128, 18, 128], BF16)
            m["Q"] = T(sc, f"mQ{i}", [128, 2304], BF16)
            m["QP"] = T(sc, f"mQP{i}", [128, 2304], BF16)
            self.B_m.append(m)
        self.KP = [T(sc, f"mKP{i}", [128, 2304], BF16) for i in range(2)]
        self.E = [T(sc, f"E{i}", [128, 512], BF16) for i in range(6)]
        self.fin = [T(sc, f"fin{i}", [128, 512], F32) for i in range(8)]
        self.ys = [T(sc, f"ys{i}", [128, 512], BF16) for i in range(3)]
        self.sqb = [T(sc, f"sqb{i}", [128, 512], BF16) for i in range(2)]
        self.rr = {"E": 0, "fin": 0, "ys": 0, "sqb": 0}
        for i in range(2):
            t, b = self.B_d[i]["Q1"]
            S.op("dve", [], [b], lambda e: e.memset(t[64:128, :], 0.0))
            t, b = self.B_d[i]["Q2"]
            S.op("dve", [], [b], lambda e: e.memset(t[0:64, :], 0.0))
            t, b = self.B_m[i]["QP"]
            S.op("dve", [], [b], lambda e: e.memset(t[64:128, :], 0.0))
            t, b = self.KP[i]
            S.op("dve", [], [b], lambda e: e.memset(t[64:128, :], 0.0))
        self.bset = 0

    @staticmethod
    def chain(gens):
        for g in gens:
            for _ in g:
                yield

    def ada_step(self, n):
        if getattr(self, "bg_ada", None) is None:
            return
        for _ in range(n):
            try:
                next(self.bg_ada)
            except StopIteration:
                self.bg_ada = None
                return

    def bg_step(self, n):
        if self.bg is None:
            return
        for _ in range(n):
            try:
                next(self.bg)
            except StopIteration:
                self.bg = None
                return

    def tok_ranges(self, b):
        return [(0, b * CTX, CTX), (CTX, LAT0 + b * SEQ, SEQ)]

    def phase_B(self, l, b, last):
        S = self.S
        rng = self.tok_ranges(b)
        kp, kp_b = self.KP[b % 2]
        for ri, (c0, t0, n) in enumerate(rng):
            S.dma("sp", kp[0:64, c0:c0 + n], self.KMP[0:64, t0:t0 + n], writes=[kp_b], join=(ri > 0))
        qgroups = [(CTX + g * 512, 512, list(range(18))) for g in range(4)]
        if not last:
            qgroups.append((0, CTX, [0, 1]))
        for h in range(NH):
            d = self.B_d[self.bset % 2]
            m = self.B_m[self.bset % 2]
            self.bset += 1
            (K, K_b), (V, V_b), (Q1, Q1_b), (Q2, Q2_b) = d["K"], d["V"], d["Q1"], d["Q2"]
            for ri, (c0, t0, n) in enumerate(rng):
                jn = ri > 0
                S.dma("sp", K[:, c0:c0 + n], self.KD[h, :, t0:t0 + n], writes=[K_b], join=jn)
                S.dma("sp", Q1[0:64, c0:c0 + n], self.QD[h, 0:64, t0:t0 + n], writes=[Q1_b], join=jn)
                S.dma("sp", Q2[64:128, c0:c0 + n], self.QD[h, 64:128, t0:t0 + n], writes=[Q2_b], join=jn)
                for j0 in range(0, n // 128, 8):
                    nb = min(8, n // 128 - j0)
                    S.dma("sp", V[:, c0 // 128 + j0:c0 // 128 + j0 + nb, :],
                          self.VD[t0 + j0 * 128:t0 + (j0 + nb) * 128, h * 128:(h + 1) * 128].rearrange("(j p) c -> p j c", p=128),
                          writes=[V_b], join=(jn or j0 > 0))
            (MK, MK_b), (MV, MV_b), (MQ, MQ_b), (MQP, MQP_b) = m["K"], m["V"], m["Q"], m["QP"]
            for ri, (c0, t0, n) in enumerate(rng):
                jn = ri > 0
                S.dma("sp", MK[:, c0:c0 + n], self.KMN[h, :, t0:t0 + n], writes=[MK_b], join=jn)
                S.dma("sp", MQ[:, c0:c0 + n], self.QMN[h, :, t0:t0 + n], writes=[MQ_b], join=jn)
                S.dma("sp", MQP[0:64, c0:c0 + n], self.QMP[h // 2, (h % 2) * 64:(h % 2) * 64 + 64, t0:t0 + n], writes=[MQP_b], join=jn)
                for j0 in range(0, n // 128, 8):
                    nb = min(8, n // 128 - j0)
                    S.dma("sp", MV[:, c0 // 128 + j0:c0 // 128 + j0 + nb, :],
                          self.VM[t0 + j0 * 128:t0 + (j0 + nb) * 128, h * 128:(h + 1) * 128].rearrange("(j p) c -> p j c", p=128),
                          writes=[MV_b], join=(jn or j0 > 0))
            for (q0, qw, kbs) in qgroups:
                self.attn_diff(b, h, q0, qw, kbs, K, K_b, V, V_b, Q1, Q1_b, Q2, Q2_b)
                self.bg_step(2)
            for gi, (q0, qw, kbs) in enumerate(qgroups):
                self.attn_mla(b, h, q0, qw, kbs, MK, MK_b, kp, kp_b, MV, MV_b, MQ, MQ_b, MQP, MQP_b, gi % 2)
                self.bg_step(2)
                self.ada_step(1)

    def q_dst(self, dstT, h, b, q0, qw):
        if q0 < CTX:
            t0 = b * CTX + q0
        else:
            t0 = LAT0 + b * SEQ + (q0 - CTX)
        return dstT[h, :, t0:t0 + qw]

    def attn_diff(self, b, h, q0, qw, kbs, K, K_b, V, V_b, Q1, Q1_b, Q2, Q2_b):
        S = self.S
        ps, psb = self.ps, self.psb
        SA, SB = [0, 1], [2, 3]
        A1, D1, A2, D2 = 4, 5, 6, 7
        n = len(kbs)
        Es = {}

        def Smm(i):
            kb = kbs[i]
            S.op("pe", [K_b, Q1_b], [psb[SA[i % 2]]],
                 lambda e: e.matmul(ps[SA[i % 2]][:, 0:qw], K[:, kb * 128:(kb + 1) * 128], Q1[:, q0:q0 + qw], start=True, stop=True))
            S.op("pe", [K_b, Q2_b], [psb[SB[i % 2]]],
                 lambda e: e.matmul(ps[SB[i % 2]][:, 0:qw], K[:, kb * 128:(kb + 1) * 128], Q2[:, q0:q0 + qw], start=True, stop=True))

        def Xp(i):
            e1, e1_b = self.rot("E")
            e2, e2_b = self.rot("E")
            S.op("act", [psb[SA[i % 2]]], [e1_b],
                 lambda e: e.activation(out=e1[:, 0:qw], in_=ps[SA[i % 2]][:, 0:qw], func=AF.Exp, scale=DIFF_SCALE))
            S.op("act", [psb[SB[i % 2]]], [e2_b],
                 lambda e: e.activation(out=e2[:, 0:qw], in_=ps[SB[i % 2]][:, 0:qw], func=AF.Exp, scale=DIFF_SCALE))
            Es[i] = (e1, e1_b, e2, e2_b)

        def AV(i):
            kb = kbs[i]
            e1, e1_b, e2, e2_b = Es.pop(i)
            st, sp = (i == 0), (i == n - 1)
            S.op("pe", [V_b, e1_b], [psb[A1]], lambda e: e.matmul(ps[A1][:, 0:qw], V[:, kb, :], e1[:, 0:qw], start=st, stop=sp))
            S.op("pe", [self.ones_b, e1_b], [psb[D1]], lambda e: e.matmul(ps[D1][:, 0:qw], self.ones[:, :], e1[:, 0:qw], start=st, stop=sp))
            S.op("pe", [V_b, e2_b], [psb[A2]], lambda e: e.matmul(ps[A2][:, 0:qw], V[:, kb, :], e2[:, 0:qw], start=st, stop=sp))
            S.op("pe", [self.ones_b, e2_b], [psb[D2]], lambda e: e.matmul(ps[D2][:, 0:qw], self.ones[:, :], e2[:, 0:qw], start=st, stop=sp))

        Smm(0)
        for i in range(n):
            if i + 1 < n:
                Smm(i + 1)
            Xp(i)
            AV(i)
        f1, f1_b = self.rot("fin")
        f2, f2_b = self.rot("fin")
        f3, f3_b = self.rot("fin")
        f4, f4_b = self.rot("fin")
        S.op("act", [psb[D1]], [f1_b], lambda e: e.activation(out=f1[:, 0:qw], in_=ps[D1][:, 0:qw], func=AF.Copy))
        S.op("act", [psb[A1]], [f3_b], lambda e: e.activation(out=f3[:, 0:qw], in_=ps[A1][:, 0:qw], func=AF.Copy))
        S.op("act", [psb[D2]], [f2_b], lambda e: e.activation(out=f2[:, 0:qw], in_=ps[D2][:, 0:qw], func=AF.Copy))
        S.op("act", [psb[A2]], [f4_b], lambda e: e.activation(out=f4[:, 0:qw], in_=ps[A2][:, 0:qw], func=AF.Copy))
        S.op("dve", [f1_b], [f1_b], lambda e: e.reciprocal(out=f1[:, 0:qw], in_=f1[:, 0:qw]))
        S.op("dve", [f3_b, f1_b], [f1_b], lambda e: e.tensor_tensor(out=f1[:, 0:qw], in0=f3[:, 0:qw], in1=f1[:, 0:qw], op=ALU.mult))
        S.op("dve", [f2_b], [f2_b], lambda e: e.reciprocal(out=f2[:, 0:qw], in_=f2[:, 0:qw]))
        S.op("dve", [f4_b, f2_b], [f2_b], lambda e: e.tensor_tensor(out=f2[:, 0:qw], in0=f4[:, 0:qw], in1=f2[:, 0:qw], op=ALU.mult))
        S.op("dve", [f1_b, f2_b, self.lamt_b], [f3_b],
             lambda e: e.scalar_tensor_tensor(out=f3[:, 0:qw], in0=f2[:, 0:qw], scalar=self.lamt[:, 0:1], in1=f1[:, 0:qw],
                                              op0=ALU.mult, op1=ALU.add))
        sq, sq_b = self.rot("sqb")
        S.op("dve", [f3_b], [sq_b], lambda e: e.tensor_tensor(out=sq[:, 0:qw], in0=f3[:, 0:qw], in1=f3[:, 0:qw], op=ALU.mult))
        S.op("pe", [sq_b, self.ones_b], [psb[D1]], lambda e: e.matmul(ps[D1][:, 0:qw], self.ones[:, :], sq[:, 0:qw], start=True, stop=True))
        S.op("act", [psb[D1], self.epst_b], [f1_b],
             lambda e: e.activation(out=f1[:, 0:qw], in_=ps[D1][:, 0:qw], func=AF.Ln, bias=self.epst[:, 0:1], scale=1.0 / 128))
        S.op("act", [f1_b], [f1_b], lambda e: e.activation(out=f1[:, 0:qw], in_=f1[:, 0:qw], func=AF.Exp, scale=-0.5))
        ys, ys_b = self.rot("ys")
        S.op("dve", [f3_b, f1_b, self.lamt_b], [ys_b],
             lambda e: e.scalar_tensor_tensor(out=ys[:, 0:qw], in0=f3[:, 0:qw], scalar=self.lamt[:, 1:2], in1=f1[:, 0:qw],
                                              op0=ALU.mult, op1=ALU.mult))
        S.dma("pool", self.q_dst(self.YD, h, b, q0, qw), ys[:, 0:qw], reads=[ys_b])

    def attn_mla(self, b, h, q0, qw, kbs, K, K_b, KP, KP_b, V, V_b, Q, Q_b, QP, QP_b, par):
        S = self.S
        ps, psb = self.ps, self.psb
        SA = [0, 1]
        A1, D1 = (4, 5) if par == 0 else (6, 7)
        n = len(kbs)
        Es = {}

        def Smm(i):
            kb = kbs[i]
            S.op("pe", [K_b, Q_b], [psb[SA[i % 2]]],
                 lambda e: e.matmul(ps[SA[i % 2]][:, 0:qw], K[:, kb * 128:(kb + 1) * 128], Q[:, q0:q0 + qw], start=True, stop=False))
            S.op("pe", [KP_b, QP_b], [psb[SA[i % 2]]],
                 lambda e: e.matmul(ps[SA[i % 2]][:, 0:qw], KP[:, kb * 128:(kb + 1) * 128], QP[:, q0:q0 + qw], start=False, stop=True))

        def Xp(i):
            e1, e1_b = self.rot("E")
            S.op("act", [psb[SA[i % 2]]], [e1_b],
                 lambda e: e.activation(out=e1[:, 0:qw], in_=ps[SA[i % 2]][:, 0:qw], func=AF.Exp, scale=MLA_SCALE))
            Es[i] = (e1, e1_b)

        def AV(i):
            kb = kbs[i]
            e1, e1_b = Es.pop(i)
            st, sp = (i == 0), (i == n - 1)
            S.op("pe", [V_b, e1_b], [psb[A1]], lambda e: e.matmul(ps[A1][:, 0:qw], V[:, kb, :], e1[:, 0:qw], start=st, stop=sp))
            S.op("pe", [self.ones_b, e1_b], [psb[D1]], lambda e: e.matmul(ps[D1][:, 0:qw], self.ones[:, :], e1[:, 0:qw], start=st, stop=sp))

        Smm(0)
        for i in range(n):
            if i + 1 < n:
                Smm(i + 1)
            Xp(i)
            AV(i)
        f1, f1_b = self.rot("fin")
        f2, f2_b = self.rot("fin")
        S.op("act", [psb[D1]], [f1_b], lambda e: e.activation(out=f1[:, 0:qw], in_=ps[D1][:, 0:qw], func=AF.Copy))
        S.op("act", [psb[A1]], [f2_b], lambda e: e.activation(out=f2[:, 0:qw], in_=ps[A1][:, 0:qw], func=AF.Copy))
        S.op("dve", [f1_b], [f1_b], lambda e: e.reciprocal(out=f1[:, 0:qw], in_=f1[:, 0:qw]))
        ys, ys_b = self.rot("ys")
        S.op("dve", [f2_b, f1_b], [ys_b], lambda e: e.tensor_tensor(out=ys[:, 0:qw], in0=f2[:, 0:qw], in1=f1[:, 0:qw], op=ALU.mult))
        S.dma("pool", self.q_dst(self.YM, h, b, q0, qw), ys[:, 0:qw], reads=[ys_b])

    def alloc_C(self, sc):
        T = self.T
        self.alloc_wslots(sc, 4)
        self.xt, _ = T(sc, "xt", [128, 16, 1024], F32)
        self.xt_b = [self.S.buf(f"xt{c}") for c in range(16)]
        self.mh, self.mh_b = T(sc, "mh", [128, 16, 1024], BF16)
        self.r1, self.r1_b = T(sc, "r1", [128, 16, 1024], BF16)
        self.rstd, self.rstd_b = T(sc, "rstdc", [128, 1024], F32)
        self.tmpf = [T(sc, f"tmpfc{i}", [128, 1024], F32) for i in range(3)]
        self.sq = [T(sc, f"sqc{i}", [128, 1024], BF16) for i in range(2)]
        self.gb = [T(sc, f"gb{i}", [128, 1024], BF16) for i in range(4)]
        self.rr = {"tmpf": 0, "sq": 0, "gb": 0}

    def phase_C(self, l, t, last):
        S = self.S
        tok0, Tw, j, is_ctx = TILES[t]
        nsub = Tw // 512
        src = self.xin if l == 0 else self.XS
        st = self.streams
        cols = slice(tok0, tok0 + Tw)
        ps, psb = self.ps, self.psb
        plan = []
        for bi in range(8):
            plan.append((st["wod_f"], bi))
            plan.append((st["wom_f"], bi))
        p_wout = len(plan)
        plan += [(st["wout_f"], bi) for bi in range(8)]
        p_mlp = len(plan)
        for q in range(4):
            plan += [(st["w1_f"], 8 * q + bi) for bi in range(8)]
            plan += [(st[f"w2_f{q}"], bi) for bi in range(8)]
        self.wq_plan(plan)
        self.wq_issue_to(len(self.wslots))
        for h in range(NH):
            S.dma("sp", self.r1[:, h, 0:Tw], self.YD[h, :, cols], writes=[self.r1_b], join=(h > 0))
        for h in range(NH):
            S.dma("sp", self.r1[:, 8 + h, 0:Tw], self.YM[h, :, cols], writes=[self.r1_b], join=True)
        for c in range(16):
            S.dma("sp", self.xt[:, c, 0:Tw], src[c * 128:(c + 1) * 128, cols], writes=[self.xt_b[c]])

        groups = [[0, 1, 2, 3], [4, 5, 6, 7]]
        gi = 0
        for bi in range(8):
            wd, wd_b = self.wq_get(2 * bi, oldest=2 * bi)
            wm, wm_b = self.wq_get(2 * bi + 1, oldest=2 * bi)
            for jj in range(2):
                c = bi * 2 + jj
                banks = groups[gi % 2]
                gi += 1
                gd, gd_b = self.rot("gb")
                gm, gm_b = self.rot("gb")
                S.dma("sp", gd[:, 0:Tw], self.G[c, :, cols], writes=[gd_b])
                S.dma("sp", gm[:, 0:Tw], self.G[16 + c, :, cols], writes=[gm_b])
                for s in range(nsub):
                    for (wt, wb_, off, bk) in ((wd, wd_b, 0, banks[s]), (wm, wm_b, 8, banks[2 + s])):
                        for kc in range(8):
                            S.op("pe", [wb_, self.r1_b], [psb[bk]],
                                 lambda e: e.matmul(ps[bk][:, :], wt[:, kc * WB + jj * 128:kc * WB + (jj + 1) * 128],
                                                    self.r1[:, off + kc, s * 512:(s + 1) * 512], start=(kc == 0), stop=(kc == 7)))
                for s in range(nsub):
                    t1, t1_b = self.rot("tmpf")
                    t2, t2_b = self.rot("tmpf")
                    sl = slice(s * 512, (s + 1) * 512)
                    S.op("act", [psb[banks[s]]], [t1_b],
                         lambda e: e.activation(out=t1[:, 0:512], in_=ps[banks[s]][:, :], func=AF.Copy))
                    S.op("act", [psb[banks[2 + s]]], [t2_b],
                         lambda e: e.activation(out=t2[:, 0:512], in_=ps[banks[2 + s]][:, :], func=AF.Copy))
                    S.op("dve", [t1_b, gd_b], [t1_b],
                         lambda e: e.tensor_tensor(out=t1[:, 0:512], in0=t1[:, 0:512], in1=gd[:, sl], op=ALU.mult))
                    S.op("dve", [t2_b, gm_b], [t2_b],
                         lambda e: e.tensor_tensor(out=t2[:, 0:512], in0=t2[:, 0:512], in1=gm[:, sl], op=ALU.mult))
                    S.op("dve", [t1_b, t2_b], [self.mh_b],
                         lambda e: e.tensor_tensor(out=self.mh[:, c, sl], in0=t1[:, 0:512], in1=t2[:, 0:512], op=ALU.add))

        groups3 = [[0, 1], [2, 3], [4, 5]]

        def rhs_m(kc, s):
            return self.mh[:, kc, s * 512:(s + 1) * 512]

        def epi_wout(ci, banks):
            for s in range(nsub):
                bk = banks[s]
                sl = slice(s * 512, (s + 1) * 512)
                tf, tf_b = self.rot("tmpf")
                S.op("act", [psb[bk], self.mod_b], [tf_b],
                     lambda e: e.activation(out=tf[:, 0:512], in_=ps[bk][:, :], func=AF.Copy,
                                            scale=self.mod[:, j, 32 + ci:33 + ci]))
                S.op("dve", [tf_b, self.xt_b[ci]], [self.xt_b[ci]],
                     lambda e: e.tensor_tensor(out=self.xt[:, ci, sl], in0=self.xt[:, ci, sl], in1=tf[:, 0:512], op=ALU.add))
            return None

        self.gemm_F(p_wout, st["wout_f"], list(range(8)), 16, rhs_m, [self.mh_b], nsub, groups3, epi_wout)

        def load_chunk(c):
            return self.xt[:, c, :], self.xt_b[c]

        self.norm_modulate(load_chunk, nsub, (self.A2, self.A2_b), 48, j, self.mh, self.mh_b, [6, 7])

        def rhs_u(kc, s):
            return self.r1[:, kc, s * 512:(s + 1) * 512]

        for q in range(4):
            def epi_w1(ci, banks):
                for s in range(nsub):
                    bk = banks[s]
                    sl = slice(s * 512, (s + 1) * 512)
                    tf, tf_b = self.rot("tmpf")
                    S.op("act", [psb[bk]], [tf_b], lambda e: e.activation(out=tf[:, 0:512], in_=ps[bk][:, :], func=AF.Relu))
                    S.op("dve", [tf_b], [self.r1_b],
                         lambda e: e.tensor_tensor(out=self.r1[:, ci, sl], in0=tf[:, 0:512], in1=tf[:, 0:512], op=ALU.mult))
                return None

            def epi_w2(ci, banks):
                for s in range(nsub):
                    bk = banks[s]
                    sl = slice(s * 512, (s + 1) * 512)
                    tf, tf_b = self.rot("tmpf")
                    S.op("act", [psb[bk], self.mod_b], [tf_b],
                         lambda e: e.activation(out=tf[:, 0:512], in_=ps[bk][:, :], func=AF.Copy,
                                                scale=self.mod[:, j, 80 + ci:81 + ci]))
                    S.op("dve", [tf_b, self.xt_b[ci]], [self.xt_b[ci]],
                         lambda e: e.tensor_tensor(out=self.xt[:, ci, sl], in0=self.xt[:, ci, sl], in1=tf[:, 0:512], op=ALU.add))
                return None

            base = p_mlp + q * 16
            self.gemm_F(base, st["w1_f"], list(range(8)), 16, rhs_m, [self.mh_b], nsub, groups3, epi_w1)
            self.gemm_F(base + 8, st[f"w2_f{q}"], list(range(8)), 16, rhs_u, [self.r1_b], nsub, groups3, epi_w2)
        self.wq_done()

        if not last:
            for c in range(16):
                S.dma("pool", self.XS[c * 128:(c + 1) * 128, cols], self.xt[:, c, 0:Tw], reads=[self.xt_b[c]])
        else:
            for c in range(16):
                sq, sq_b = self.rot("sq")
                S.op("act", [self.xt_b[c]], [sq_b], lambda e: e.activation(out=sq[:, 0:Tw], in_=self.xt[:, c, 0:Tw], func=AF.Square))
                for s in range(nsub):
                    bk = 6 + s
                    S.op("pe", [sq_b, self.ones_b], [psb[bk]],
                         lambda e: e.matmul(ps[bk][:, :], self.ones[:, :], sq[:, s * 512:(s + 1) * 512], start=(c == 0), stop=(c == 15)))
            self.rstd_from_stats([6, 7], nsub, D, self.rstd, self.rstd_b)
            o0 = tok0 - LAT0
            for c in range(16):
                tf, tf_b = self.rot("tmpf")
                S.op("dve", [self.xt_b[c], self.rstd_b, self.vec_b], [tf_b],
                     lambda e: e.scalar_tensor_tensor(out=tf[:, 0:Tw], in0=self.xt[:, c, 0:Tw],
                                                      scalar=self.vec[:, V_FINAL + c:V_FINAL + c + 1],
                                                      in1=self.rstd[:, 0:Tw], op0=ALU.mult, op1=ALU.mult))
                S.dma("pool", self.yout[c * 128:(c + 1) * 128, o0:o0 + Tw], tf[:, 0:Tw], reads=[tf_b])


def fm(v, nch):
    return np.ascontiguousarray(np.asarray(v, np.float32).reshape(nch, 128).T)


def rope_tables():
    rows = SEQ // 64
    row, col = np.meshgrid(np.arange(rows), np.arange(64), indexing="ij")
    row = row.reshape(-1).astype(np.float32)
    col = col.reshape(-1).astype(np.float32)
    freqs = (np.float32(10000.0) ** (-np.arange(0, 32, 2, dtype=np.float32) / np.float32(32))).astype(np.float32)
    ang_r = row[:, None] * freqs
    ang_c = col[:, None] * freqs
    ang = np.concatenate([ang_r, ang_r, ang_c, ang_c], axis=-1).astype(np.float32)
    cos = np.cos(ang).astype(np.float32).T
    sin = np.sin(ang).astype(np.float32).T
    cs = np.zeros((128, 2 * SEQ), np.float32)
    cs[0:64, 0:SEQ] = cos
    cs[64:128, 0:SEQ] = cos
    cs[0:64, SEQ:] = sin
    cs[64:128, SEQ:] = sin
    return cs


def rot_matrix_T():
    R = np.zeros((64, 64), np.float32)
    for i in range(64):
        blk, o = divmod(i, 32)
        if o < 16:
            R[i, blk * 32 + o + 16] = -1.0
        else:
            R[i, blk * 32 + o - 16] = 1.0
    R128 = np.zeros((128, 128), np.float32)
    R128[0:64, 0:64] = R
    R128[64:128, 64:128] = R
    return np.ascontiguousarray(R128.T)


def make_in_maps(inp, n_layers=DEPTH):
    x = np.asarray(inp["x"], np.float32)
    ctx = np.asarray(inp["ctx"], np.float32)
    c = np.asarray(inp["c"], np.float32)
    c_ctx = np.asarray(inp["c_ctx"], np.float32)
    vecs = np.zeros((128, NVEC), np.float32)
    for l in range(DEPTH):
        vb = l * V_PER_LAYER
        vecs[:, vb + V_N1:vb + V_N1 + 16] = fm(inp["norm1_g"][l], 16)
        vecs[:, vb + V_N2:vb + V_N2 + 16] = fm(inp["norm2_g"][l], 16)
        vecs[:, vb + V_BG:vb + V_BG + 32] = fm(inp["b_gate"][l], 32)
        vecs[:, vb + V_QA:vb + V_QA + 4] = fm(inp["q_a_norm"][l], 4)
        vecs[:, vb + V_KVA:vb + V_KVA + 2] = fm(inp["kv_a_norm"][l], 2)
        vecs[:, vb + V_SUB:vb + V_SUB + 1] = fm(inp["diff_subln"][l], 1)
        vecs[:, vb + V_BADA:vb + V_BADA + 96] = fm(inp["b_ada"][l], 96)
    vecs[:, V_FINAL:V_FINAL + 16] = fm(inp["final_norm_g"], 16)
    dlam = np.ascontiguousarray(np.broadcast_to(np.asarray(inp["diff_lambda"], np.float32).reshape(1, DEPTH * 256),
                                                (128, DEPTH * 256)))
    cs = rope_tables()
    rmat = rot_matrix_T()
    shared = {"vecs": vecs, "dlam": dlam, "cossin": cs, "rmat": rmat,
              "w_ada": np.ascontiguousarray(np.asarray(inp["w_ada"], np.float32))}
    for n in WSHAPES:
        shared[n] = np.ascontiguousarray(np.asarray(inp[n], np.float32))
    maps = []
    for core in range(NCORES):
        b0 = core * NB
        xin = np.empty((D, NTOK), np.float32)
        for i in range(NB):
            xin[:, i * CTX:(i + 1) * CTX] = ctx[b0 + i].T
            xin[:, LAT0 + i * SEQ:LAT0 + (i + 1) * SEQ] = x[b0 + i].T
        cv = np.stack([c[b0], c[b0 + 1], c_ctx], axis=-1)
        cvec = np.ascontiguousarray(cv.reshape(16, 128, 3).transpose(1, 0, 2).reshape(128, 48))
        m = dict(shared)
        m["xin"] = xin
        m["cvec"] = cvec
        maps.append(m)
    return maps


_CACHE = {}


def get_nc(n_layers=DEPTH, debug=None):
    key = (n_layers, repr(debug))
    if key not in _CACHE:
        b = Builder(n_layers, debug)
        b.build()
        _CACHE[key] = b
    return _CACHE[key]


def kernel(**inputs):
    b = get_nc()
    maps = make_in_maps(inputs)
    res = run_bass_kernel_spmd(b.nc, maps, core_ids=list(range(NCORES)))
    out = np.empty((NCORES * NB, SEQ, D), np.float32)
    for core in range(NCORES):
        y = res.results[core]["yout"]
        for i in range(NB):
            out[core * NB + i] = y[:, i * SEQ:(i + 1) * SEQ].T
    return out
```
